# Optimizing a Trainium2 kernel written in Bass

```python
import math
import jax, jax.numpy as jnp
from jax import lax
import numpy as np

D_MODEL = 2048
BATCH = 8
SEQ = 2048
DEPTH = 2

GRID_W = 64
CTX_LEN = 256
NORM_EPS = 1e-6

RW_HEADS = 8
RW_HD = 64
RW_W = RW_HEADS * RW_HD
RW_DECAY_RANK = 64
RW_ICLR_RANK = 64
RW_GATE_RANK = 128
RW_GN_EPS = 64e-5
RW_IN = 3 * RW_W + 2 * RW_DECAY_RANK + 2 * RW_ICLR_RANK + RW_GATE_RANK

SSD_HEADS = 8
SSD_HD = 64
SSD_W = SSD_HEADS * SSD_HD
SSD_GROUPS = 2
SSD_STATE = 128
SSD_CONV = 3
SSD_CHUNK = 128
SSD_XBC = SSD_W + 2 * SSD_GROUPS * SSD_STATE
SSD_IN = SSD_W + SSD_XBC + 2 * SSD_HEADS

DA_HEADS = 4
DA_HD = 64
DA_W = DA_HEADS * 2 * DA_HD
DA_IN = 3 * DA_W
DA_BLOCK = 128
DA_SUBLN_EPS = 1e-5

NA_HEADS = 8
NA_HD = 64
NA_W = NA_HEADS * NA_HD
NA_IN = 3 * NA_W
NA_WIN_R = 8
NA_WIN_C = 16

N_BRANCH = 4
BR_W = 512
IN_SIZES = (RW_IN, SSD_IN, DA_IN, NA_IN)
IN_TOTAL = RW_IN + SSD_IN + DA_IN + NA_IN

D_FF = 5632
FFN_CONV = 3
ROPE_BASE = 10000.0

kernel_name = 'hybrid_prefix_dit_block'


def _split(x, sizes):
    return jnp.split(x, np.cumsum(sizes)[:-1].tolist(), axis=-1)


def rmsnorm(x, g, eps=NORM_EPS):
    xf = x.astype(jnp.float32)
    y = xf * lax.rsqrt(jnp.mean(xf * xf, axis=-1, keepdims=True) + eps)
    return (y * g.astype(jnp.float32)).astype(x.dtype)


def dwconv_centred(x, w, b):
    k_w = w.shape[0]
    length = x.shape[1]
    pad = k_w // 2
    xp = jnp.pad(x, ((0, 0), (pad, pad), (0, 0)))
    y = b + xp[:, 0:length] * w[0]
    for j in range(1, k_w):
        y = y + xp[:, j:j + length] * w[j]
    return y


def token_shift_bi(p, mu_prev, mu_next):
    zero = jnp.zeros_like(p[:, :1])
    prev = jnp.concatenate([zero, p[:, :-1]], axis=1)
    nxt = jnp.concatenate([p[:, 1:], zero], axis=1)
    return p + mu_prev * (prev - p) + mu_next * (nxt - p)


def rope2d(x):
    length, dh = x.shape[1], x.shape[-1]
    n_freq = dh // 4
    t = jnp.arange(length)
    pos = jnp.stack([t // GRID_W, t % GRID_W], axis=-1).astype(jnp.float32)
    inv = ROPE_BASE ** (-jnp.arange(n_freq, dtype=jnp.float32) / n_freq)
    ang = pos[:, :, None] * inv
    cos = jnp.cos(ang)[:, None, None]
    sin = jnp.sin(ang)[:, None, None]
    xr = x.astype(jnp.float32).reshape(*x.shape[:-1], 2, 2, n_freq)
    x1, x2 = xr[..., 0, :], xr[..., 1, :]
    out = jnp.stack([x1 * cos - x2 * sin, x2 * cos + x1 * sin], axis=-2)
    return out.reshape(x.shape).astype(x.dtype)


def rwkv_prep(p, mu, w0, w_up, a0, a_up, g_up, k_k, k_a):
    bsz, length, _ = p.shape
    p = token_shift_bi(p, mu[0], mu[1]).astype(jnp.float32)
    r, k, v, wd, ad, gd = _split(p, (RW_W, RW_W, RW_W, 2 * RW_DECAY_RANK, 2 * RW_ICLR_RANK, RW_GATE_RANK))
    wd = wd.reshape(bsz, length, 2, RW_DECAY_RANK)
    ad = ad.reshape(bsz, length, 2, RW_ICLR_RANK)
    w_raw = w0 + jnp.einsum('bldr,drc->bldc', jnp.tanh(wd), w_up)
    decay = jnp.exp(-jnp.exp(-jax.nn.softplus(-w_raw) - 0.5))
    a = jax.nn.sigmoid(a0 + jnp.einsum('bldr,drc->bldc', ad, a_up))
    g = jax.nn.sigmoid(gd) @ g_up
    heads = lambda t: t.reshape(*t.shape[:-1], RW_HEADS, RW_HD)
    kk = heads(k * k_k)
    kk = kk / jnp.maximum(jnp.linalg.norm(kk, axis=-1, keepdims=True), 1e-12)
    k_dir = heads(k[:, :, None] * (1.0 + (a - 1.0) * k_a))
    return (heads(r), heads(decay), k_dir, heads(v), kk, heads(a), g)


def wkv_scan(r, w, k, v, kk, b, s0, reverse):
    def step(s, inp):
        r_t, w_t, k_t, v_t, kk_t, b_t = inp
        sa = jnp.einsum('bhvk,bhk->bhv', s, kk_t)
        s = s * w_t[:, :, None, :] - sa[..., None] * b_t[:, :, None, :] + v_t[..., None] * k_t[:, :, None, :]
        return s, jnp.einsum('bhvk,bhk->bhv', s, r_t)
    xs = tuple(jnp.swapaxes(t, 0, 1) for t in (r, w, k, v, kk, b))
    s, o = lax.scan(step, s0, xs, reverse=reverse)
    return jnp.swapaxes(o, 0, 1), s


def rwkv_bidir(prep, inits):
    r, decay, k_dir, v, kk, a, _ = prep
    b = kk[:, :, None] * a
    o_f, s_f = wkv_scan(r, decay[:, :, 0], k_dir[:, :, 0], v, kk, b[:, :, 0], inits[0], False)
    o_b, s_b = wkv_scan(r, decay[:, :, 1], k_dir[:, :, 1], v, kk, b[:, :, 1], inits[1], True)
    return o_f + o_b, (s_f, s_b)


def rwkv_readout(o, prep, r_k, ln_g, ln_b):
    r, _, k_dir, v, _, _, g = prep
    bsz, length = o.shape[:2]
    mu = jnp.mean(o, axis=-1, keepdims=True)
    var = jnp.mean(jnp.square(o - mu), axis=-1, keepdims=True)
    on = ((o - mu) * lax.rsqrt(var + RW_GN_EPS)).reshape(bsz, length, RW_W) * ln_g + ln_b
    bonus = jnp.sum(jnp.sum(r[:, :, None] * k_dir * r_k, axis=-1, keepdims=True) * v[:, :, None], axis=2)
    return (on + bonus.reshape(bsz, length, RW_W)) * g


def rwkv7_mixer(pl, pc, lp, ctx_out):
    prep_args = (lp['rw_mu'], lp['rw_w0'], lp['rw_w_up'], lp['rw_a0'], lp['rw_a_up'],
                 lp['rw_g_up'], lp['rw_k_k'], lp['rw_k_a'])
    prep_c = rwkv_prep(pc, *prep_args)
    prep_l = rwkv_prep(pl, *prep_args)
    s0 = jnp.zeros((pl.shape[0], RW_HEADS, RW_HD, RW_HD), jnp.float32)
    o_c, fin_c = rwkv_bidir(prep_c, (s0, s0))
    o_l, _ = rwkv_bidir(prep_l, fin_c)
    ro = (lp['rw_r_k'], lp['rw_ln_g'], lp['rw_ln_b'])
    out_l = rwkv_readout(o_l, prep_l, *ro)
    out_c = rwkv_readout(o_c, prep_c, *ro) if ctx_out else None
    return out_l, out_c


def ssd_prep(p, conv_w, conv_b, dt_bias):
    bsz, length, _ = p.shape
    z, xbc, dt_raw = _split(p, (SSD_W, SSD_XBC, 2 * SSD_HEADS))
    xbc = jax.nn.silu(dwconv_centred(xbc, conv_w, conv_b)).astype(jnp.float32)
    xs, bm, cm = _split(xbc, (SSD_W, SSD_GROUPS * SSD_STATE, SSD_GROUPS * SSD_STATE))
    xs = xs.reshape(bsz, length, SSD_HEADS, SSD_HD)
    bm = bm.reshape(bsz, length, SSD_GROUPS, SSD_STATE)
    cm = cm.reshape(bsz, length, SSD_GROUPS, SSD_STATE)
    dt = jax.nn.softplus(dt_raw.astype(jnp.float32).reshape(bsz, length, 2, SSD_HEADS) + dt_bias)
    return (z.astype(jnp.float32), xs, bm, cm, dt)


def ssd_chunked(x, dt, a, bm, cm, h0):
    bsz, length, n_h, n_p = x.shape
    q = SSD_CHUNK
    nc = length // q
    rep = n_h // SSD_GROUPS
    xq = x.reshape(bsz, nc, q, n_h, n_p)
    dtq = dt.reshape(bsz, nc, q, n_h)
    bh = jnp.repeat(bm, rep, axis=2).reshape(bsz, nc, q, n_h, SSD_STATE)
    ch = jnp.repeat(cm, rep, axis=2).reshape(bsz, nc, q, n_h, SSD_STATE)
    acs = jnp.cumsum(jnp.swapaxes(dtq * a, 2, 3), axis=-1)
    seg = acs[..., :, None] - acs[..., None, :]
    tri = jnp.tril(jnp.ones((q, q), dtype=bool))
    decay_ij = jnp.exp(jnp.where(tri, seg, -jnp.inf))
    scores = jnp.einsum('bcihn,bcjhn->bchij', ch, bh) * decay_ij
    y_diag = jnp.einsum('bchij,bcjh,bcjhp->bcihp', scores, dtq, xq)
    to_end = jnp.exp(acs[..., -1:] - acs)
    states = jnp.einsum('bchj,bcjh,bcjhn,bcjhp->bchpn', to_end, dtq, bh, xq)
    chunk_decay = jnp.exp(acs[..., -1])

    def step(h, inp):
        st, dec = inp
        return h * dec[..., None, None] + st, h
    h_last, h_in = lax.scan(step, h0, (jnp.swapaxes(states, 0, 1), jnp.swapaxes(chunk_decay, 0, 1)))
    h_in = jnp.swapaxes(h_in, 0, 1)
    y_off = jnp.einsum('bcihn,bchpn,bchi->bcihp', ch, h_in, jnp.exp(acs))
    return (y_diag + y_off).reshape(bsz, length, n_h, n_p), h_last


def ssd_bidir(prep, a_log, inits):
    _, xs, bm, cm, dt = prep
    a = -jnp.exp(a_log.astype(jnp.float32))
    flip = lambda t: jnp.flip(t, axis=1)
    y_f, h_f = ssd_chunked(xs, dt[:, :, 0], a[0], bm, cm, inits[0])
    y_b, h_b = ssd_chunked(flip(xs), flip(dt[:, :, 1]), a[1], flip(bm), flip(cm), inits[1])
    return y_f + flip(y_b), (h_f, h_b)


def ssd_readout(y, prep, d_skip, norm_g):
    z, xs = prep[0], prep[1]
    bsz, length = y.shape[:2]
    y = (y + d_skip[:, None] * xs).reshape(bsz, length, SSD_W)
    return rmsnorm(y * jax.nn.silu(z), norm_g)


def ssd_mixer(pl, pc, lp, ctx_out):
    prep_c = ssd_prep(pc, lp['ssd_conv_w'], lp['ssd_conv_b'], lp['ssd_dt_bias'])
    prep_l = ssd_prep(pl, lp['ssd_conv_w'], lp['ssd_conv_b'], lp['ssd_dt_bias'])
    h0 = jnp.zeros((pl.shape[0], SSD_HEADS, SSD_HD, SSD_STATE), jnp.float32)
    y_c, fin_c = ssd_bidir(prep_c, lp['ssd_a_log'], (h0, h0))
    y_l, _ = ssd_bidir(prep_l, lp['ssd_a_log'], fin_c)
    out_l = ssd_readout(y_l, prep_l, lp['ssd_d'], lp['ssd_norm_g'])
    out_c = ssd_readout(y_c, prep_c, lp['ssd_d'], lp['ssd_norm_g']) if ctx_out else None
    return out_l, out_c


def diff_attention(pl, pc, lp, layer_idx, ctx_out):
    bsz, seq, _ = pl.shape

    def qkv(p):
        q, k, v = _split(p, (DA_W, DA_W, DA_W))
        sh = (p.shape[0], p.shape[1], DA_HEADS, 2, DA_HD)
        return q.reshape(sh), k.reshape(sh), v.reshape(p.shape[0], p.shape[1], DA_HEADS, 2 * DA_HD)
    ql, kl, vl = qkv(pl)
    qc, kc, vc = qkv(pc)
    ql, kl = rope2d(ql), rope2d(kl)
    lam_init = 0.8 - 0.6 * math.exp(-0.3 * layer_idx)
    lam_p = lp['da_lambda'].astype(jnp.float32)
    lam = jnp.exp(jnp.sum(lam_p[0] * lam_p[1])) - jnp.exp(jnp.sum(lam_p[2] * lam_p[3])) + lam_init
    scale = DA_HD ** -0.5
    k_all = jnp.concatenate([kl, kc], axis=1)
    v_all = jnp.concatenate([vl, vc], axis=1)

    def attend(q, k, v):
        s = jnp.einsum('bqhmd,bkhmd->bhmqk', q, k).astype(jnp.float32) * scale
        p = jax.nn.softmax(s, axis=-1)
        a = p[:, :, 0] - lam * p[:, :, 1]
        o = jnp.einsum('bhqk,bkhe->bqhe', a, v.astype(jnp.float32))
        return rmsnorm(o, lp['da_subln_g'], eps=DA_SUBLN_EPS) * (1.0 - lam_init)
    nb = seq // DA_BLOCK
    qb = jnp.moveaxis(ql.reshape(bsz, nb, DA_BLOCK, DA_HEADS, 2, DA_HD), 1, 0)
    ol = lax.map(lambda qblk: attend(qblk, k_all, v_all), qb)
    out_l = jnp.moveaxis(ol, 0, 1).reshape(bsz, seq, DA_W)
    out_c = attend(qc, kc, vc).reshape(bsz, pc.shape[1], DA_W) if ctx_out else None
    return out_l, out_c


def neighbourhood_attention(pl, pc, lp, ctx_out):
    bsz, seq, _ = pl.shape
    rows = seq // GRID_W
    wr = min(NA_WIN_R, rows)

    def qkv(p):
        sh = (p.shape[0], p.shape[1], NA_HEADS, NA_HD)
        return tuple(t.reshape(sh) for t in _split(p, (NA_W, NA_W, NA_W)))
    ql, kl, vl = qkv(pl)
    qc, kc, vc = qkv(pc)
    scale = NA_HD ** -0.5
    rpb = lp['na_rpb'].astype(jnp.float32)
    qg = ql.reshape(bsz, rows, GRID_W, NA_HEADS, NA_HD)
    r_ids = jnp.arange(rows)
    row_start = jnp.clip(r_ids - wr // 2, 0, rows - wr)
    row_idx = row_start[:, None] + jnp.arange(wr)
    kr = kl.reshape(bsz, rows, GRID_W, NA_HEADS, NA_HD)[:, row_idx]
    vr = vl.reshape(bsz, rows, GRID_W, NA_HEADS, NA_HD)[:, row_idx]
    s_win = jnp.einsum('brqhd,brwkhd->bhrqwk', qg, kr).astype(jnp.float32) * scale
    c_ids = jnp.arange(GRID_W)
    col_start = jnp.clip(c_ids - NA_WIN_C // 2, 0, GRID_W - NA_WIN_C)
    in_win = (c_ids[None, :] >= col_start[:, None]) & (c_ids[None, :] < col_start[:, None] + NA_WIN_C)
    ri = row_idx - r_ids[:, None] + NA_WIN_R - 1
    ci = jnp.clip(c_ids[None, :] - c_ids[:, None], -(NA_WIN_C - 1), NA_WIN_C - 1) + NA_WIN_C - 1
    bias = rpb[:, ri[:, None, :, None], ci[None, :, None, :]]
    s_win = jnp.where(in_win[:, None, :], s_win + bias, -jnp.inf)
    s_ctx = jnp.einsum('brqhd,bkhd->bhrqk', qg, kc).astype(jnp.float32) * scale
    n_win = wr * GRID_W
    s = jnp.concatenate([s_win.reshape(bsz, NA_HEADS, rows, GRID_W, n_win), s_ctx], axis=-1)
    p = jax.nn.softmax(s, axis=-1)
    p_win = p[..., :n_win].reshape(bsz, NA_HEADS, rows, GRID_W, wr, GRID_W)
    p_ctx = p[..., n_win:]
    o = (jnp.einsum('bhrqwk,brwkhd->brqhd', p_win, vr.astype(jnp.float32))
         + jnp.einsum('bhrqk,bkhd->brqhd', p_ctx, vc.astype(jnp.float32)))
    out_l = o.reshape(bsz, seq, NA_W)
    out_c = None
    if ctx_out:
        sc = jnp.einsum('bqhd,bkhd->bhqk', qc, kc).astype(jnp.float32) * scale
        oc = jnp.einsum('bhqk,bkhd->bqhd', jax.nn.softmax(sc, axis=-1), vc.astype(jnp.float32))
        out_c = oc.reshape(bsz, pc.shape[1], NA_W)
    return out_l, out_c


def gated_merge(h, branches, lp):
    merged = None
    for n, o in enumerate(branches):
        gate = jax.nn.sigmoid(h @ lp['w_gate'][n] + lp['gate_b'][n])
        term = gate * (o.astype(h.dtype) @ lp['w_br'][n])
        merged = term if merged is None else merged + term
    return merged @ lp['w_out']


def token_mixing(hl, hc, lp, layer_idx, ctx_out):
    zl = _split(hl @ lp['w_in'], IN_SIZES)
    zc = _split(hc @ lp['w_in'], IN_SIZES)
    a_l, a_c = rwkv7_mixer(zl[0], zc[0], lp, ctx_out)
    b_l, b_c = ssd_mixer(zl[1], zc[1], lp, ctx_out)
    c_l, c_c = diff_attention(zl[2], zc[2], lp, layer_idx, ctx_out)
    d_l, d_c = neighbourhood_attention(zl[3], zc[3], lp, ctx_out)
    out_l = gated_merge(hl, (a_l, b_l, c_l, d_l), lp)
    out_c = gated_merge(hc, (a_c, b_c, c_c, d_c), lp) if ctx_out else None
    return out_l, out_c


def conv_ffn(h, up, conv_w, conv_b, down):
    u = dwconv_centred(h @ up, conv_w, conv_b)
    gate, val = jnp.split(u, 2, axis=-1)
    return (jax.nn.silu(gate) * val) @ down


def setup_inputs(seed: int = 0) -> dict:
    key = jax.random.key(seed)
    ks = iter(jax.random.split(key, 48))
    nrm = lambda shape, s: jax.random.normal(next(ks), shape, jnp.float32) * s
    uni = lambda shape, lo, hi: jax.random.uniform(next(ks), shape, jnp.float32, lo, hi)
    dp = DEPTH
    dt0 = jnp.exp(uni((dp, 2, SSD_HEADS), math.log(1e-3), math.log(1e-1)))
    return {
        'x': nrm((BATCH, SEQ, D_MODEL), 1.0),
        'c': nrm((BATCH, D_MODEL), 1.0),
        'ctx': nrm((BATCH, CTX_LEN, D_MODEL), 1.0),
        'c_ctx': nrm((D_MODEL,), 1.0),
        'ada_w': nrm((dp, D_MODEL, 6 * D_MODEL), 0.5 * D_MODEL ** -0.5),
        'ada_b': nrm((dp, 6 * D_MODEL), 0.02),
        'norm1_g': 1.0 + nrm((dp, D_MODEL), 0.05),
        'norm2_g': 1.0 + nrm((dp, D_MODEL), 0.05),
        'w_in': nrm((dp, D_MODEL, IN_TOTAL), D_MODEL ** -0.5),
        'rw_mu': uni((dp, 2, RW_IN), 0.0, 0.5),
        'rw_w0': uni((dp, 2, RW_W), -5.0, -0.5),
        'rw_w_up': nrm((dp, 2, RW_DECAY_RANK, RW_W), 0.5 * RW_DECAY_RANK ** -0.5),
        'rw_a0': nrm((dp, 2, RW_W), 0.5),
        'rw_a_up': nrm((dp, 2, RW_ICLR_RANK, RW_W), RW_ICLR_RANK ** -0.5),
        'rw_g_up': nrm((dp, RW_GATE_RANK, RW_W), RW_GATE_RANK ** -0.5),
        'rw_k_k': 0.85 + nrm((dp, RW_W), 0.05),
        'rw_k_a': 1.0 + nrm((dp, RW_W), 0.05),
        'rw_r_k': nrm((dp, RW_HEADS, RW_HD), 0.1),
        'rw_ln_g': 1.0 + nrm((dp, RW_W), 0.05),
        'rw_ln_b': nrm((dp, RW_W), 0.02),
        'ssd_conv_w': nrm((dp, SSD_CONV, SSD_XBC), SSD_CONV ** -0.5),
        'ssd_conv_b': nrm((dp, SSD_XBC), 0.02),
        'ssd_dt_bias': dt0 + jnp.log(-jnp.expm1(-dt0)),
        'ssd_a_log': jnp.log(uni((dp, 2, SSD_HEADS), 1.0, 16.0)),
        'ssd_d': 1.0 + nrm((dp, SSD_HEADS), 0.1),
        'ssd_norm_g': 1.0 + nrm((dp, SSD_W), 0.05),
        'da_lambda': nrm((dp, 4, DA_HD), 0.1),
        'da_subln_g': 1.0 + nrm((dp, 2 * DA_HD), 0.05),
        'na_rpb': nrm((dp, NA_HEADS, 2 * NA_WIN_R - 1, 2 * NA_WIN_C - 1), 0.02),
        'w_gate': nrm((dp, N_BRANCH, D_MODEL, D_MODEL), D_MODEL ** -0.5),
        'gate_b': nrm((dp, N_BRANCH, D_MODEL), 0.02),
        'w_br': nrm((dp, N_BRANCH, BR_W, D_MODEL), BR_W ** -0.5),
        'w_out': nrm((dp, D_MODEL, D_MODEL), D_MODEL ** -0.5),
        'ffn_up': nrm((dp, D_MODEL, 2 * D_FF), D_MODEL ** -0.5),
        'ffn_conv_w': nrm((dp, FFN_CONV, 2 * D_FF), FFN_CONV ** -0.5),
        'ffn_conv_b': nrm((dp, 2 * D_FF), 0.02),
        'ffn_down': nrm((dp, D_FF, D_MODEL), D_FF ** -0.5),
        'final_norm_g': 1.0 + nrm((D_MODEL,), 0.05),
    }


def reference(x, c, ctx, c_ctx, ada_w, ada_b, norm1_g, norm2_g, w_in, rw_mu, rw_w0, rw_w_up,
              rw_a0, rw_a_up, rw_g_up, rw_k_k, rw_k_a, rw_r_k, rw_ln_g, rw_ln_b, ssd_conv_w,
              ssd_conv_b, ssd_dt_bias, ssd_a_log, ssd_d, ssd_norm_g, da_lambda, da_subln_g, na_rpb,
              w_gate, gate_b, w_br, w_out, ffn_up, ffn_conv_w, ffn_conv_b, ffn_down, final_norm_g):
    xl, xc = x, ctx
    for i in range(DEPTH):
        last = i == DEPTH - 1
        lp = {
            'w_in': w_in[i], 'rw_mu': rw_mu[i], 'rw_w0': rw_w0[i], 'rw_w_up': rw_w_up[i],
            'rw_a0': rw_a0[i], 'rw_a_up': rw_a_up[i], 'rw_g_up': rw_g_up[i], 'rw_k_k': rw_k_k[i],
            'rw_k_a': rw_k_a[i], 'rw_r_k': rw_r_k[i], 'rw_ln_g': rw_ln_g[i], 'rw_ln_b': rw_ln_b[i],
            'ssd_conv_w': ssd_conv_w[i], 'ssd_conv_b': ssd_conv_b[i], 'ssd_dt_bias': ssd_dt_bias[i],
            'ssd_a_log': ssd_a_log[i], 'ssd_d': ssd_d[i], 'ssd_norm_g': ssd_norm_g[i],
            'da_lambda': da_lambda[i], 'da_subln_g': da_subln_g[i], 'na_rpb': na_rpb[i],
            'w_gate': w_gate[i], 'gate_b': gate_b[i], 'w_br': w_br[i], 'w_out': w_out[i],
        }
        mod_l = jax.nn.silu(c) @ ada_w[i] + ada_b[i]
        mod_c = jax.nn.silu(c_ctx) @ ada_w[i] + ada_b[i]
        ml = jnp.split(mod_l[:, None, :], 6, axis=-1)
        mc = jnp.split(mod_c[None, None, :], 6, axis=-1)
        hl = rmsnorm(xl, norm1_g[i]) * (1.0 + ml[1]) + ml[0]
        hc = rmsnorm(xc, norm1_g[i]) * (1.0 + mc[1]) + mc[0]
        ol, oc = token_mixing(hl, hc, lp, i, not last)
        xl = xl + ml[2] * ol
        hl2 = rmsnorm(xl, norm2_g[i]) * (1.0 + ml[4]) + ml[3]
        xl = xl + ml[5] * conv_ffn(hl2, ffn_up[i], ffn_conv_w[i], ffn_conv_b[i], ffn_down[i])
        if not last:
            xc = xc + mc[2] * oc
            hc2 = rmsnorm(xc, norm2_g[i]) * (1.0 + mc[4]) + mc[3]
            xc = xc + mc[5] * conv_ffn(hc2, ffn_up[i], ffn_conv_w[i], ffn_conv_b[i], ffn_down[i])
    return rmsnorm(xl, final_norm_g)
```

```python
import numpy as np
from contextlib import ExitStack
import concourse.bass as bass
import concourse.mybir as mybir
from concourse.bass_utils import run_bass_kernel_spmd

F32 = mybir.dt.float32
BF16 = mybir.dt.bfloat16
ALU = mybir.AluOpType
AF = mybir.ActivationFunctionType
AX = mybir.AxisListType


class Buf:
    __slots__ = ("name", "last_w", "last_w_deps", "readers", "dsem", "dcnt", "is_psum")

    def __init__(self, name):
        self.name = name
        self.last_w = None
        self.last_w_deps = []
        self.readers = []
        self.dsem = None
        self.dcnt = 0
        self.is_psum = False


class V:
    __slots__ = ("ap", "buf")

    def __init__(self, ap, buf):
        self.ap = ap
        self.buf = buf

    def __getitem__(self, idx):
        return V(self.ap[idx], self.buf)

    def rearrange(self, s, **kw):
        return V(self.ap.rearrange(s, **kw), self.buf)

    def bc(self, shape):
        return V(self.ap.broadcast_to(list(shape)), self.buf)

    @property
    def shape(self):
        return self.ap.shape


class Tile:
    def __init__(self, handle, buf):
        self.h = handle
        self.buf = buf

    def __getitem__(self, idx):
        return V(self.h[idx], self.buf)

    @property
    def v(self):
        return V(self.h[:], self.buf)


class Op:
    __slots__ = ("eng", "fn", "deps", "signal", "dma_buf", "dma_cnt", "dma_sem", "idx", "val", "waits", "tag")

    def __init__(self, eng, fn):
        self.eng = eng
        self.fn = fn
        self.deps = []
        self.signal = False
        self.dma_buf = None
        self.dma_cnt = 0
        self.dma_sem = -1
        self.idx = -1
        self.val = 0
        self.waits = []


ENGS = ("pe", "act", "dve", "pool", "sp")


class Sched:
    def __init__(self, nc):
        self.nc = nc
        self.ops = {e: [] for e in ENGS}
        self.stack = ExitStack()
        self.bufs = []
        self.dma_pending = {}
        self.sems = {}
        self.nsem = 0
        self.sem_tot = []
        self.sem_free = []
        self.annotate = False
        self.stage = None

    def sbuf(self, name, shape, dtype=F32, stack=None):
        st = stack if stack is not None else self.stack
        h = st.enter_context(self.nc.sbuf_tensor(self._uniq("s_" + name), list(shape), dtype))
        b = Buf(name)
        self.bufs.append(b)
        return Tile(h, b)

    def psum(self, name, shape, dtype=F32, stack=None):
        st = stack if stack is not None else self.stack
        h = st.enter_context(self.nc.psum_tensor(self._uniq("q_" + name), list(shape), dtype))
        b = Buf(name)
        b.is_psum = True
        self.bufs.append(b)
        return Tile(h, b)

    def _uniq(self, name):
        self.uid = getattr(self, "uid", 0) + 1
        return f"{name}_{self.uid}"

    def _sem(self, name):
        s = self.stack.enter_context(self.nc.semaphore(self._uniq(name)))
        self.nsem += 1
        return s

    def _add(self, eng, fn, reads, writes, dma_buf=None):
        op = Op(eng, fn)
        op.tag = getattr(self, "stage", None)
        op.idx = len(self.ops[eng])
        is_dma = dma_buf is not None
        deps = []
        raw = set()
        for b in reads:
            if b is None:
                continue
            if b.last_w is not None:
                deps.append(b.last_w)
                raw.add(id(b.last_w))
            if b.is_psum:
                for rd_ in b.readers:
                    if rd_.eng != eng:
                        deps.append(rd_)
        wdeps = {}
        for b in writes:
            if b is None:
                continue
            mine = []
            if b.last_w is not None:
                lw = b.last_w
                if is_dma and lw.dma_buf is not None:
                    mine.extend(b.last_w_deps)
                else:
                    mine.append(lw)
            mine.extend(b.readers)
            wdeps[id(b)] = mine
            deps.extend(mine)
        out = []
        seen = set()
        for d in deps:
            if id(d) in seen or d is op:
                continue
            seen.add(id(d))
            if d.dma_buf is None and d.eng == eng and not is_dma:
                if eng == "pe" or id(d) not in raw:
                    continue
            out.append(d)
        op.deps = out
        if is_dma:
            op.dma_buf = dma_buf
            if dma_buf.dsem is None:
                if self.sem_free:
                    dma_buf.dsem = self.sem_free.pop()
                else:
                    dma_buf.dsem = len(self.sem_tot)
                    self.sem_tot.append(0)
            self.sem_tot[dma_buf.dsem] += 16
            op.dma_sem = dma_buf.dsem
            op.dma_cnt = self.sem_tot[dma_buf.dsem]
            self.dma_pending[id(dma_buf)] = op
        for b in reads:
            if b is not None:
                b.readers.append(op)
        for b in writes:
            if b is not None:
                b.last_w = op
                b.last_w_deps = wdeps[id(b)]
                b.readers = []
        self.ops[eng].append(op)
        return op

    def barrier(self):
        lasts = []
        for e in ENGS:
            for o in reversed(self.ops[e]):
                if o.dma_buf is None and o.fn is not None:
                    lasts.append(o)
                    break
        pend = list(self.dma_pending.values())
        for e in ENGS:
            op = Op(e, None)
            op.idx = len(self.ops[e])
            op.deps = list(lasts) + pend
            self.ops[e].append(op)
        self.dma_pending = {}
        for b in self.bufs:
            b.last_w = None
            b.last_w_deps = []
            b.readers = []
            if b.dsem is not None:
                self.sem_free.append(b.dsem)
                b.dsem = None

    @staticmethod
    def _bufs(*vs):
        return [v.buf for v in vs if isinstance(v, V)]

    @staticmethod
    def _a(v):
        return v.ap if isinstance(v, V) else v

    def matmul(self, out, lhsT, rhs, start=True, stop=True):
        o, l, r = out.ap, lhsT.ap, rhs.ap
        rd = self._bufs(lhsT, rhs)
        if not start:
            rd = rd + [out.buf]
        return self._add("pe", lambda e: e.matmul(o, lhsT=l, rhs=r, start=start, stop=stop), rd, [out.buf])

    def transpose(self, out, in_, ident):
        o, i, d = out.ap, in_.ap, ident.ap
        return self._add("pe", lambda e: e.transpose(o, i, d), self._bufs(in_, ident), [out.buf])

    def act(self, out, in_, func, bias=0.0, scale=1.0, accum=None):
        o, i, b, s = out.ap, in_.ap, self._a(bias), self._a(scale)
        ac = accum.ap if accum is not None else None
        w = [out.buf] + ([accum.buf] if accum is not None else [])
        if ac is None:
            fn = lambda e: e.activation(o, i, func, bias=b, scale=s)
        else:
            fn = lambda e: e.activation(o, i, func, bias=b, scale=s, accum_out=ac)
        return self._add("act", fn, self._bufs(in_, bias, scale), w)

    def tt(self, eng, out, a, b, op):
        o, x, y = out.ap, a.ap, b.ap
        return self._add(eng, lambda e: e.tensor_tensor(o, x, y, op), self._bufs(a, b), [out.buf])

    def ts(self, eng, out, a, s1, op0, s2=None, op1=None, accum=None):
        o, x, p, q = out.ap, a.ap, self._a(s1), self._a(s2)
        ac = accum.ap if accum is not None else None
        w = [out.buf] + ([accum.buf] if accum is not None else [])
        kw = {}
        if op1 is not None:
            kw["op1"] = op1
        if ac is not None:
            kw["accum_out"] = ac
        return self._add(eng, lambda e: e.tensor_scalar(o, x, p, q, op0, **kw), self._bufs(a, s1, s2), w)

    def stt(self, eng, out, in0, scalar, in1, op0, op1):
        o, x, s, y = out.ap, in0.ap, self._a(scalar), in1.ap
        return self._add(eng, lambda e: e.scalar_tensor_tensor(o, x, s, y, op0, op1), self._bufs(in0, scalar, in1), [out.buf])

    def copy(self, eng, out, in_):
        o, i = out.ap, in_.ap
        if eng == "act":
            return self._add("act", lambda e: e.copy(o, i), [in_.buf], [out.buf])
        return self._add(eng, lambda e: e.tensor_copy(o, i), [in_.buf], [out.buf])

    def memset(self, eng, out, val):
        o = out.ap
        return self._add(eng, lambda e: e.memset(o, val), [], [out.buf])

    def reduce(self, eng, out, in_, op, axis=AX.X):
        o, i = out.ap, in_.ap
        return self._add(eng, lambda e: e.tensor_reduce(o, i, axis, op), [in_.buf], [out.buf])

    def recip(self, out, in_):
        o, i = out.ap, in_.ap
        return self._add("dve", lambda e: e.reciprocal(o, i), [in_.buf], [out.buf])

    def dma(self, q, out, in_):
        o = out.ap if isinstance(out, V) else out
        i = in_.ap if isinstance(in_, V) else in_
        ob = out.buf if isinstance(out, V) else None
        ib = in_.buf if isinstance(in_, V) else None
        sb = ob if ob is not None else ib
        assert sb is not None
        reads = [ib] if ib is not None else []
        writes = [ob] if ob is not None else []
        return self._add(q, lambda e: e.dma_start(out=o, in_=i), reads, writes, dma_buf=sb)

    def emit(self):
        nc = self.nc
        for e in ENGS:
            for op in self.ops[e]:
                for d in op.deps:
                    d.signal = True
        esem = {}
        for e in ENGS:
            cnt = 0
            for op in self.ops[e]:
                if op.dma_buf is None and op.signal and op.fn is not None:
                    cnt += 1
                    op.val = cnt
                elif op.dma_buf is None and op.fn is None and op.signal:
                    raise RuntimeError("barrier op signalled")
            if cnt:
                esem[e] = self._sem("e_" + e)
        dsems = [self._sem(f"d{i}") for i in range(len(self.sem_tot))]
        nwait = 0
        for e in ENGS:
            seen = {}
            for op in self.ops[e]:
                ws = {}
                for d in op.deps:
                    if d.dma_buf is not None:
                        key = ("d", d.dma_sem)
                        sem, val = dsems[d.dma_sem], d.dma_cnt
                    else:
                        key = ("e", d.eng)
                        sem, val = esem[d.eng], d.val
                    if seen.get(key, 0) >= val:
                        continue
                    if key not in ws or ws[key][1] < val:
                        ws[key] = (sem, val)
                for key, (sem, val) in ws.items():
                    seen[key] = val
                op.waits = list(ws.values())
                nwait += len(op.waits)
        self.stats = {e: len(self.ops[e]) for e in ENGS}
        self.stats["waits"] = nwait
        self.stats["sems"] = self.nsem
        engmap = {"pe": "tensor", "act": "scalar", "dve": "vector", "pool": "gpsimd", "sp": "sync"}
        with nc.Block() as block:
            for e in ENGS:
                ops = self.ops[e]
                if not ops:
                    continue
                sem_e = esem.get(e)

                def body(eng, ops=ops, sem_e=sem_e, dsems=dsems, self=self):
                    for op in ops:
                        for (sem, val) in op.waits:
                            eng.wait_ge(sem, val)
                        if op.fn is None:
                            continue
                        ins = op.fn(eng)
                        if self.annotate and op.tag:
                            ins.annotate(op.tag)
                        if op.dma_buf is not None:
                            ins.then_inc(dsems[op.dma_sem], 16)
                        elif op.signal:
                            ins.then_inc(sem_e, 1)
                getattr(block, engmap[e])(body)


D_MODEL = 2048
NLAT = 2048
NCTX = 256
T = NLAT + NCTX
D_FF = 5632
NORM_EPS = 1e-6
KD = 16


class Ctx:
    pass


def chunks(total, size):
    return [(i, min(size, total - i)) for i in range(0, total, size)]


def fm(ap):
    return ap.rearrange("(k p) n -> p k n", p=128)


def init_common(C):
    S = C.S
    C.ps = [S.psum(f"ps{i}", [128, 512]) for i in range(8)]
    C.psi = 0
    C.ident = S.sbuf("ident", [128, 128])
    C.ones = S.sbuf("ones", [128, 128])
    S.dma("sp", C.ident.v, C.d["ident"])
    S.memset("dve", C.ones.v, 1.0)
    C.mod = S.sbuf("mod", [128, 96, 2])
    C.g1s = S.sbuf("g1s", [128, 16, 2])
    C.g2s = S.sbuf("g2s", [128, 16, 2])
    C.dq = 0


def nextps(C):
    p = C.ps[C.psi % 8]
    C.psi += 1
    return p


def ldq(C):
    C.dq += 1
    return ("sp", "pool")[C.dq % 2]


def stage_mod(C, l):
    S = C.S
    with ExitStack() as st:
        wbuf = [S.sbuf(f"adaw{i}", [128, 16, 512], BF16, stack=st) for i in range(3)]
        cc = S.sbuf("cc", [128, 16, 2], stack=st)
        sc = S.sbuf("sc", [128, 16, 2], BF16, stack=st)
        adab = S.sbuf("adab", [128, 96], stack=st)
        ng = S.sbuf("ng", [128, 2, 16], stack=st)
        S.dma("sp", cc.v, C.d["cc"])
        S.dma("sp", adab.v, C.d["ada_b"][l])
        S.dma("sp", ng[:, 0, :], C.d["norm1_g"][l])
        S.dma("sp", ng[:, 1, :], C.d["norm2_g"][l])
        S.act(sc.v, cc.v, AF.Silu)
        W = fm(C.d["ada_w"][l])
        for fb in range(24):
            w = wbuf[fb % 3]
            S.dma("pool", w.v, W[:, :, fb * 512:(fb + 1) * 512])
            p = nextps(C)
            for j in range(4):
                for k in range(16):
                    S.matmul(p[:, j * 2:(j + 1) * 2], w[:, k, j * 128:(j + 1) * 128], sc[:, k, :],
                             start=(k == 0), stop=(k == 15))
            for col in range(2):
                S.tt("dve", C.mod[:, fb * 4:(fb + 1) * 4, col],
                     p[:, 0:8].rearrange("p (j c) -> p j c", c=2)[:, :, col],
                     adab[:, fb * 4:(fb + 1) * 4], ALU.add)
        for col in range(2):
            S.stt("dve", C.g1s[:, :, col], C.mod[:, 16:32, col], 1.0, ng[:, 0, :], ALU.add, ALU.mult)
            S.stt("dve", C.g2s[:, :, col], C.mod[:, 64:80, col], 1.0, ng[:, 1, :], ALU.add, ALU.mult)
    S.barrier()


def stage_norm(C, which, src, dst, col_chunks, final_g=None):
    S = C.S
    srcv, dstv = fm(src), fm(dst)
    with ExitStack() as st:
        xb = [S.sbuf(f"nx{i}", [128, 16, 256], stack=st) for i in range(2)]
        sq = S.sbuf("nsq", [128, 16, 256], stack=st)
        hb = [S.sbuf(f"nh{i}", [128, 16, 256], F32 if which == 0 else BF16, stack=st) for i in range(2)]
        hf = S.sbuf("nhf", [128, 16, 256], stack=st) if which != 0 else None
        rs = [S.sbuf(f"nr{i}", [128, 256], stack=st) for i in range(2)]
        for ci, (c0, w, col) in enumerate(col_chunks):
            x, h, r = xb[ci % 2], hb[ci % 2], rs[ci % 2]
            S.dma("sp", x[:, :, 0:w], srcv[:, :, c0:c0 + w])
            S.act(sq[:, :, 0:w], x[:, :, 0:w], AF.Square)
            p = nextps(C)
            for k in range(16):
                S.matmul(p[:, 0:w], C.ones.v, sq[:, k, 0:w], start=(k == 0), stop=(k == 15))
            S.ts("dve", r[:, 0:w], p[:, 0:w], 1.0 / D_MODEL, ALU.mult, NORM_EPS, ALU.add)
            S.act(r[:, 0:w], r[:, 0:w], AF.Ln)
            S.act(r[:, 0:w], r[:, 0:w], AF.Exp, scale=-0.5)
            for k in range(16):
                if which == 0:
                    S.stt("dve", h[:, k, 0:w], x[:, k, 0:w], final_g[:, k:k + 1], r[:, 0:w], ALU.mult, ALU.mult)
                else:
                    gs = C.g1s if which == 1 else C.g2s
                    sh0 = 0 if which == 1 else 48
                    S.stt("dve", hf[:, k, 0:w], x[:, k, 0:w], gs[:, k, col:col + 1], r[:, 0:w], ALU.mult, ALU.mult)
                    S.act(h[:, k, 0:w], hf[:, k, 0:w], AF.Identity, bias=C.mod[:, sh0 + k, col:col + 1])
            S.dma("pool", dstv[:, :, c0:c0 + w], h[:, :, 0:w])
    S.barrier()


def norm_chunks(with_ctx=True):
    cs = [(c0, w, 0) for (c0, w) in chunks(NLAT, 256)]
    if with_ctx:
        cs += [(NLAT, 256, 1)]
    return cs


def seg_chunks(g0, gw, size=512):
    out = []
    a = g0
    end = g0 + gw
    while a < end:
        b = min(a + size, end)
        if a < NLAT < b:
            b = NLAT
        out.append((a - g0, b - a))
        a = b
    return out


def gemm(C, inT, K, W, tok_groups, blocks, epi, group_hook=None, wblock=None, resident=None):
    S = C.S
    nk = K // 128
    if resident is None:
        resident = {}
    res = [resident.get(g, g) for g in tok_groups]
    gwmax = max(rw for _, rw in res)
    nwmax = max(nw for _, nw, _ in blocks)
    Wv = fm(W) if W is not None else None
    inv = fm(inT)
    with ExitStack() as st:
        act = S.sbuf("g_act", [128, nk, gwmax], BF16, stack=st)
        NWB = 3
        wb = [S.sbuf(f"g_w{i}", [128, nk, nwmax], BF16, stack=st) for i in range(NWB)]
        bi_glob = 0
        for gi, (gg0, ggw) in enumerate(tok_groups):
            g0, gw = res[gi]
            kq = max(1, nk // 4)
            for k0 in range(0, nk, kq):
                k1 = min(nk, k0 + kq)
                S.dma(("sp", "act")[(k0 // kq) % 2], act[:, k0:k1, 0:gw], inv[:, k0:k1, g0:g0 + gw])
            if group_hook is not None:
                group_hook("begin", g0, gw)
            for (n0, nw, mode) in blocks:
                wt = wb[bi_glob % NWB]
                bi_glob += 1
                wsrc = wblock(n0, nw) if wblock is not None else Wv[:, :, n0:n0 + nw]
                S.dma("pool", wt[:, :, 0:nw], wsrc)
                if mode == "T":
                    for t0 in range(0, gw, 128):
                        ps = nextps(C)
                        for k in range(nk):
                            S.matmul(ps[:, 0:nw], act[:, k, t0:t0 + 128], wt[:, k, 0:nw],
                                     start=(k == 0), stop=(k == nk - 1))
                        epi("T", ps, n0, nw, g0 + t0, 128, g0, gw)
                else:
                    for f0 in range(0, nw, 128):
                        for (t0, tw) in seg_chunks(g0, gw):
                            ps = nextps(C)
                            for k in range(nk):
                                S.matmul(ps[:, 0:tw], wt[:, k, f0:f0 + 128], act[:, k, t0:t0 + tw],
                                         start=(k == 0), stop=(k == nk - 1))
                            epi("F", ps, n0 + f0, 128, g0 + t0, tw, g0, gw)
            if group_hook is not None:
                group_hook("end", g0, gw)
    S.barrier()


class Stager:
    def __init__(self, C, st, name, n=4, width=512):
        self.C = C
        self.bufs = [C.S.sbuf(f"{name}{i}", [128, width], stack=st) for i in range(n)]
        self.i = 0

    def next(self):
        b = self.bufs[self.i % len(self.bufs)]
        self.i += 1
        return b

    def evac(self, dst_view, src_view):
        S = self.C.S
        if self.i % 2 == 0:
            S.copy("act", dst_view, src_view)
        else:
            S.copy("dve", dst_view, src_view)


def stage_win(C, l):
    S = C.S
    d = C.d
    blocks = []
    outmap = []

    def addT(n0, n1, dram):
        for (o, w) in chunks(n1 - n0, 512):
            blocks.append((n0 + o, w, "T"))
        outmap.append((n0, n1, "T", dram))

    def addF(n0, n1, dram):
        for (o, w) in chunks(n1 - n0, 512):
            blocks.append((n0 + o, w, "F"))
        outmap.append((n0, n1, "F", dram))

    addT(0, 1920, d["p_rw"])
    addT(1920, 3472, d["p_ssd"])
    addF(3472, 4496, d["da_qkT"])
    addT(4496, 5008, d["da_v"])
    addF(5008, 6032, d["na_qkT"])
    addT(6032, 6544, d["na_v"])

    def find(n0):
        for (a, b, m, dr) in outmap:
            if a <= n0 < b:
                return a, dr
        raise KeyError

    with ExitStack() as st:
        sg = Stager(C, st, "wst", 4)

        def epi(mode, ps, n0, nw, c0, cw, g0, gw):
            a, dr = find(n0)
            sb = sg.next()
            if mode == "T":
                sg.evac(sb[:, 0:nw], ps[:, 0:nw])
                S.dma("act" if sg.i % 2 else "sp", dr[c0:c0 + 128, n0 - a:n0 - a + nw], sb[:, 0:nw])
            else:
                sg.evac(sb[:, 0:cw], ps[:, 0:cw])
                S.dma("act" if sg.i % 2 else "sp", dr[n0 - a:n0 - a + 128, c0:c0 + cw], sb[:, 0:cw])

        gemm(C, d["hT"], D_MODEL, d["w_in"][l], [(0, T)], blocks, epi)


def stage_merge(C, l, with_ctx):
    S = C.S
    d = C.d
    Tt = T if with_ctx else NLAT
    NG = 2
    gw = Tt // NG
    hv = fm(d["hT"])
    bv = d["brT"].rearrange("n (k p) t -> p (n k) t", p=128)
    wg = d["w_gate"][l].rearrange("n (k p) m -> n p k m", p=128)
    wbr = d["w_br"][l].rearrange("n (k p) m -> n p k m", p=128)
    mgv = fm(d["mgT"])
    CW = 384
    with ExitStack() as st:
        hact = S.sbuf("m_h", [128, 16, gw], BF16, stack=st)
        bact = S.sbuf("m_b", [128, 16, gw], BF16, stack=st)
        wgb = [S.sbuf(f"m_wg{i}", [128, 16, 256], BF16, stack=st) for i in range(3)]
        wbb = [S.sbuf(f"m_wb{i}", [128, 4, 256], BF16, stack=st) for i in range(3)]
        gb = S.sbuf("m_gb", [128, 4, 16], stack=st)
        S.dma("sp", gb.v, d["gate_b"][l])
        sig = [S.sbuf(f"m_sig{i}", [128, CW], stack=st) for i in range(2)]
        tmp = [S.sbuf(f"m_tmp{i}", [128, CW], stack=st) for i in range(2)]
        acc = [S.sbuf(f"m_acc{i}", [128, 2, gw], stack=st) for i in range(2)]
        aout = [S.sbuf(f"m_ao{i}", [128, 2, gw], BF16, stack=st) for i in range(2)]
        wi = 0
        ai = 0
        si = 0
        for g in range(NG):
            g0 = g * gw
            for k0 in range(0, 16, 4):
                S.dma("sp", hact[:, k0:k0 + 4, :], hv[:, k0:k0 + 4, g0:g0 + gw])
                S.dma("act", bact[:, k0:k0 + 4, :], bv[:, k0:k0 + 4, g0:g0 + gw])
            tch = seg_chunks(g0, gw, CW)
            for db in range(8):
                a = acc[ai % 2]
                ao = aout[ai % 2]
                ai += 1
                for n in range(4):
                    wgt, wbt = wgb[wi % 3], wbb[wi % 3]
                    wi += 1
                    S.dma("pool", wgt.v, wg[n][:, :, db * 256:(db + 1) * 256])
                    S.dma("pool", wbt.v, wbr[n][:, :, db * 256:(db + 1) * 256])
                    for dc in range(2):
                        for (t0, tw) in tch:
                            pg = nextps(C)
                            for k in range(16):
                                S.matmul(pg[:, 0:tw], wgt[:, k, dc * 128:(dc + 1) * 128], hact[:, k, t0:t0 + tw],
                                         start=(k == 0), stop=(k == 15))
                            pb = nextps(C)
                            for k in range(4):
                                S.matmul(pb[:, 0:tw], wbt[:, k, dc * 128:(dc + 1) * 128], bact[:, n * 4 + k, t0:t0 + tw],
                                         start=(k == 0), stop=(k == 3))
                            sg = sig[si % 2]
                            tp = tmp[si % 2]
                            si += 1
                            S.act(sg[:, 0:tw], pg[:, 0:tw], AF.Sigmoid, bias=gb[:, n, db * 2 + dc:db * 2 + dc + 1])
                            if n == 0:
                                S.tt("dve", a[:, dc, t0:t0 + tw], sg[:, 0:tw], pb[:, 0:tw], ALU.mult)
                            else:
                                S.tt("dve", tp[:, 0:tw], sg[:, 0:tw], pb[:, 0:tw], ALU.mult)
                                dst = ao if n == 3 else a
                                S.tt("dve", dst[:, dc, t0:t0 + tw], a[:, dc, t0:t0 + tw], tp[:, 0:tw], ALU.add)
                S.dma("sp", mgv[:, db * 2:db * 2 + 2, g0:g0 + gw], ao.v)
    S.barrier()


def resid_epi(C, st, gate0, xsrc, xdram):
    S = C.S
    xt = [S.sbuf(f"r_x{i}", [128, 512], stack=st) for i in range(4)]
    cnt = [0]

    def epi(mode, ps, n0, nw, c0, cw, g0, gw):
        assert mode == "F"
        x = xt[cnt[0] % 4]
        cnt[0] += 1
        col = 0 if c0 < NLAT else 1
        k = n0 // 128
        S.dma("sp", x[:, 0:cw], xsrc[n0:n0 + 128, c0:c0 + cw])
        S.stt("dve", x[:, 0:cw], ps[:, 0:cw], C.mod[:, gate0 + k, col:col + 1], x[:, 0:cw], ALU.mult, ALU.add)
        S.dma("act", xdram[n0:n0 + 128, c0:c0 + cw], x[:, 0:cw])
    return epi


def stage_wout(C, l, with_ctx):
    d = C.d
    xsrc = d["xT"] if l == 0 else d["xres"]
    Tt = T if with_ctx else NLAT
    groups = [(0, Tt)]
    blocks = [(n0, 512, "F") for n0 in range(0, D_MODEL, 512)]
    with ExitStack() as st:
        epi = resid_epi(C, st, 32, xsrc, d["xres"])
        gemm(C, d["mgT"], D_MODEL, d["w_out"][l], groups, blocks, epi)


def stage_ffn_up(C, l, with_ctx):
    S = C.S
    d = C.d
    Tt = T if with_ctx else NLAT
    half = Tt
    groups = [(0, Tt)]
    resident = {}
    blocks = [(j * 256, 256, "F") for j in range(44)]
    guv = d["guT"]
    with ExitStack() as st:
        cw_t = S.sbuf("f_cw", [128, 3, 88], stack=st)
        cb_t = S.sbuf("f_cb", [128, 88], stack=st)
        S.dma("sp", cw_t.v, d["ffn_conv_w"][l])
        S.dma("sp", cb_t.v, d["ffn_conv_b"][l])
        RW = Tt
        ub = [[S.sbuf(f"f_u{i}{j}", [128, RW], stack=st) for j in range(2)] for i in range(2)]
        yb = [[S.sbuf(f"f_y{i}{j}", [128, RW], stack=st) for j in range(2)] for i in range(2)]
        gob = [S.sbuf(f"f_go{i}", [128, RW], BF16, stack=st) for i in range(2)]
        state = {"i": 0}

        def epi(mode, ps, n0, nw, c0, cw, g0, gw):
            j = n0 // 256
            isval = (n0 % 256) // 128
            u = ub[j % 2][isval]
            S.copy("act" if isval else "dve", u[:, c0 - g0:c0 - g0 + cw], ps[:, 0:cw])
            last_chunk = (c0 + cw == g0 + gw)
            if not (isval and last_chunk):
                return
            o0 = g0
            o1 = g0 + gw
            pieces = []
            a = o0
            while a < o1:
                b = o1
                if a < NLAT < b:
                    b = NLAT
                pieces.append((a, b))
                a = b
            for (a, b) in pieces:
                seg0, seg1 = (0, NLAT) if a < NLAT else (NLAT, T)
                if not with_ctx:
                    seg0, seg1 = 0, NLAT
                for v_ in range(2):
                    u = ub[j % 2][v_]
                    y = yb[j % 2][v_]
                    ch = j + 44 * v_
                    la, lb = a - g0, b - g0
                    S.act(y[:, la:lb], u[:, la:lb], AF.Identity, bias=cb_t[:, ch:ch + 1], scale=cw_t[:, 1, ch:ch + 1])
                    lo = la + 1 if a == seg0 else la
                    S.stt("dve", y[:, lo:lb], u[:, lo - 1:lb - 1], cw_t[:, 0, ch:ch + 1], y[:, lo:lb], ALU.mult, ALU.add)
                    hi = lb - 1 if b == seg1 else lb
                    S.stt("dve", y[:, la:hi], u[:, la + 1:hi + 1], cw_t[:, 2, ch:ch + 1], y[:, la:hi], ALU.mult, ALU.add)
                yg, yv = yb[j % 2][0], yb[j % 2][1]
                la, lb = a - g0, b - g0
                S.act(ub[j % 2][0][:, la:lb], yg[:, la:lb], AF.Silu)
                S.tt("dve", gob[j % 2][:, la:lb], yv[:, la:lb], ub[j % 2][0][:, la:lb], ALU.mult)
                S.dma("sp", guv[j * 128:(j + 1) * 128, a:b], gob[j % 2][:, la:lb])

        gemm(C, d["hT"], D_MODEL, d["ffn_up"][l], groups, blocks, epi, resident=resident)


def stage_ffn_down(C, l, with_ctx):
    d = C.d
    Tt = T if with_ctx else NLAT
    gw = Tt // 2
    groups = [(g * gw, gw) for g in range(2)]
    blocks = [(n0, 128, "F") for n0 in range(0, D_MODEL, 128)]
    wv = d["ffn_down"][l]

    def wblock(n0, nw):
        return wv[n0 // 128]
    with ExitStack() as st:
        epi = resid_epi(C, st, 80, d["xres"], d["xres"])
        gemm(C, d["guT"], D_FF, None, groups, blocks, epi, wblock=wblock)

import math

DA_SUBLN_EPS = 1e-5


def load_bcast(S, q, tile_v, dram_ap, n=128):
    return S.dma(q, tile_v, dram_ap.partition_broadcast(n))


def stage_da(C, l, ctx_out):
    S = C.S
    d = C.d
    lam_init = 0.8 - 0.6 * math.exp(-0.3 * l)
    qk = d["da_qkT"]
    vd = d["da_v"]
    out = d["brT"][2]
    acc = C.ps[0:4]
    rot = C.ps[4:7]
    misc = C.ps[7]
    with ExitStack() as st:
        cos = S.sbuf("da_cos", [128, NLAT], stack=st)
        sin = S.sbuf("da_sin", [128, NLAT], stack=st)
        perm = S.sbuf("da_perm", [128, 128], stack=st)
        S.dma("sp", cos.v, d["rope_cos"])
        S.dma("act", sin.v, d["rope_sin"])
        S.dma("sp", perm.v, d["rope_perm"])
        lp = S.sbuf("da_lp", [128, 4, 64], stack=st)
        load_bcast(S, "sp", lp.v, d["da_lambda"][l])
        pr = S.sbuf("da_pr", [128, 2, 64], stack=st)
        sm = S.sbuf("da_sm", [128, 2], stack=st)
        S.tt("dve", pr[:, 0, :], lp[:, 0, :], lp[:, 1, :], ALU.mult)
        S.tt("dve", pr[:, 1, :], lp[:, 2, :], lp[:, 3, :], ALU.mult)
        S.reduce("dve", sm.v, pr.v, ALU.add)
        S.act(sm.v, sm.v, AF.Exp)
        nlam = S.sbuf("da_nlam", [128, 1], stack=st)
        S.tt("dve", nlam.v, sm[:, 1:2], sm[:, 0:1], ALU.subtract)
        S.ts("dve", nlam.v, nlam.v, -lam_init, ALU.add)
        gsub = S.sbuf("da_g", [128, 1], stack=st)
        S.dma("sp", gsub.v, d["da_subln_g"][l])
        S.ts("dve", gsub.v, gsub.v, 1.0 - lam_init, ALU.mult)

        qT = S.sbuf("da_q", [128, T], stack=st)
        kT = S.sbuf("da_k", [128, T], stack=st)
        qB = S.sbuf("da_qb", [128, T], BF16, stack=st)
        kB = S.sbuf("da_kb", [128, T], BF16, stack=st)
        vt = S.sbuf("da_vt", [128, 18, 128], BF16, stack=st)
        onesb = S.sbuf("da_1b", [128, 128], BF16, stack=st)
        S.memset("pool", onesb.v, 1.0)
        tmp = [S.sbuf(f"da_tmp{i}", [128, 512], stack=st) for i in range(2)]
        et = [S.sbuf(f"da_e{i}", [128, 512], BF16, stack=st) for i in range(3)]
        rr = [S.sbuf(f"da_r{i}", [128, 512], stack=st) for i in range(2)]
        oo = [S.sbuf(f"da_o{i}", [128, 512], stack=st) for i in range(2)]
        sq = S.sbuf("da_sq", [128, 512], stack=st)
        es = S.sbuf("da_es", [128, 512], stack=st)
        esb = S.sbuf("da_esb", [128, 512], BF16, stack=st)
        rs = S.sbuf("da_rs", [128, 512], stack=st)
        ob = [S.sbuf(f"da_ob{i}", [128, 512], BF16, stack=st) for i in range(2)]
        ei = 0
        oi = 0
        for h in range(4):
            S.dma("sp", qT.v, qk[h * 128:(h + 1) * 128, :])
            S.dma("act", kT.v, qk[512 + h * 128:512 + (h + 1) * 128, :])
            S.dma("pool", vt.v, vd[:, h * 128:(h + 1) * 128].rearrange("(b p) e -> p b e", p=128))
            ti = 0
            for X, XB in ((qT, qB), (kT, kB)):
                for c0 in range(0, NLAT, 512):
                    S.matmul(misc.v, perm.v, X[:, c0:c0 + 512])
                    t = tmp[ti % 2]
                    ti += 1
                    S.tt("dve", t.v, misc.v, sin[:, c0:c0 + 512], ALU.mult)
                    S.tt("dve", X[:, c0:c0 + 512], X[:, c0:c0 + 512], cos[:, c0:c0 + 512], ALU.mult)
                    S.tt("dve", XB[:, c0:c0 + 512], X[:, c0:c0 + 512], t.v, ALU.add)
                S.copy("act", XB[:, NLAT:T], X[:, NLAT:T])
            qchunks = [(c0, 512, list(range(18))) for c0 in range(0, NLAT, 512)]
            if ctx_out:
                qchunks.append((NLAT, 256, [16, 17]))
            items = []
            for (c0, cw, kbs) in qchunks:
                for m in range(2):
                    for i, kb in enumerate(kbs):
                        items.append((c0, cw, m, i, kb, len(kbs)))
            slots = {}

            def emitA(n):
                c0, cw, m, i, kb, nk_ = items[n]
                ps = rot[n % 3]
                e = et[n % 3]
                S.matmul(ps[:, 0:cw], kB[m * 64:(m + 1) * 64, kb * 128:(kb + 1) * 128],
                         qB[m * 64:(m + 1) * 64, c0:c0 + cw])
                S.act(e[:, 0:cw], ps[:, 0:cw], AF.Exp, scale=0.125)

            def emitB(n):
                nonlocal oi
                c0, cw, m, i, kb, nk_ = items[n]
                e = et[n % 3]
                OT, SM = acc[m], acc[2 + m]
                S.matmul(OT[:, 0:cw], vt[:, kb, :], e[:, 0:cw], start=(i == 0), stop=(i == nk_ - 1))
                if i == 0:
                    S.copy("dve", es[:, 0:cw], e[:, 0:cw])
                else:
                    S.tt("dve", es[:, 0:cw], es[:, 0:cw], e[:, 0:cw], ALU.add)
                if i != nk_ - 1:
                    return
                S.copy("dve", esb[:, 0:cw], es[:, 0:cw])
                S.matmul(SM[:, 0:cw], onesb.v, esb[:, 0:cw])
                S.recip(rr[m][:, 0:cw], SM[:, 0:cw])
                S.tt("dve", oo[m][:, 0:cw], OT[:, 0:cw], rr[m][:, 0:cw], ALU.mult)
                if m != 1:
                    return
                o = oo[0]
                S.stt("dve", o[:, 0:cw], oo[1][:, 0:cw], nlam.v, o[:, 0:cw], ALU.mult, ALU.add)
                S.act(sq[:, 0:cw], o[:, 0:cw], AF.Square)
                S.matmul(misc[:, 0:cw], C.ones.v, sq[:, 0:cw])
                S.ts("dve", rs[:, 0:cw], misc[:, 0:cw], 1.0 / 128, ALU.mult, DA_SUBLN_EPS, ALU.add)
                S.act(rs[:, 0:cw], rs[:, 0:cw], AF.Ln)
                S.act(rs[:, 0:cw], rs[:, 0:cw], AF.Exp, scale=-0.5)
                b = ob[oi % 2]
                oi += 1
                S.stt("dve", b[:, 0:cw], o[:, 0:cw], gsub.v, rs[:, 0:cw], ALU.mult, ALU.mult)
                S.dma("sp", out[h * 128:(h + 1) * 128, c0:c0 + cw], b[:, 0:cw])

            LA = 2
            for n in range(len(items) + LA):
                if n < len(items):
                    emitA(n)
                if n - LA >= 0:
                    emitB(n - LA)
    S.barrier()


def rope_tables():
    GRID_W = 64
    t = np.arange(NLAT)
    pos = np.stack([t // GRID_W, t % GRID_W], axis=-1).astype(np.float32)
    n_freq = 16
    inv = (np.float32(10000.0) ** (-np.arange(n_freq, dtype=np.float32) / np.float32(n_freq))).astype(np.float32)
    ang = (pos[:, :, None] * inv).astype(np.float32)
    cosv, sinv = np.cos(ang).astype(np.float32), np.sin(ang).astype(np.float32)
    cos = np.zeros((128, NLAT), np.float32)
    sin = np.zeros((128, NLAT), np.float32)
    perm = np.zeros((128, 128), np.float32)
    for p in range(128):
        dd = p % 64
        half, j, f = dd // 32, (dd % 32) // 16, dd % 16
        cos[p] = cosv[:, half, f]
        if j == 0:
            sin[p] = -sinv[:, half, f]
            partner = p + 16
        else:
            sin[p] = sinv[:, half, f]
            partner = p - 16
        perm[partner, p] = 1.0
    return cos, sin, perm


NT = T // 128
SEG_FIRST = {0: True, 16: True}
SEG_LAST = {15: True, 17: True}


def bc_last(v, n):
    shp = list(v.ap.shape)
    return V(v.ap.unsqueeze(len(shp)).broadcast_to(shp + [n]), v.buf)


def load_shifted(S, st_tiles, src, t0, width, c0, first, last):
    xm, x0, xp = st_tiles
    S.dma("sp", x0[:, 0:width], src[t0:t0 + 128, c0:c0 + width])
    if first:
        S.memset("pool", xm[0:32, 0:width], 0.0)
        S.dma("pool", xm[1:128, 0:width], src[t0:t0 + 127, c0:c0 + width])
    else:
        S.dma("pool", xm[:, 0:width], src[t0 - 1:t0 + 127, c0:c0 + width])
    if last:
        S.memset("pool", xp[96:128, 0:width], 0.0)
        S.dma("act", xp[0:127, 0:width], src[t0 + 1:t0 + 128, c0:c0 + width])
    else:
        S.dma("act", xp[:, 0:width], src[t0 + 1:t0 + 129, c0:c0 + width])


def stage_ssd(C, l, ctx_out):
    S = C.S
    d = C.d
    src = d["p_ssd"]
    out = d["brT"][1].rearrange("(k p) t -> p k t", p=128)
    P = C.ps
    with ExitStack() as st:
        triF = S.sbuf("tr_f", [128, 128], stack=st)
        triB = S.sbuf("tr_b", [128, 128], stack=st)
        S.dma("sp", triF.v, d["triF"])
        S.dma("sp", triB.v, d["triB"])
        dtb = S.sbuf("s_dtb", [128, 16], stack=st)
        load_bcast(S, "sp", dtb.v, d["ssd_dt_bias"][l])
        negA = S.sbuf("s_negA", [128, 16], stack=st)
        load_bcast(S, "sp", negA.v, d["ssd_a_log"][l])
        S.act(negA.v, negA.v, AF.Exp)
        S.ts("dve", negA.v, negA.v, -1.0, ALU.mult)
        dsk = S.sbuf("s_dsk", [128, 8], stack=st)
        load_bcast(S, "sp", dsk.v, d["ssd_d"][l])
        ngb = S.sbuf("s_ng", [128, 512], stack=st)
        load_bcast(S, "pool", ngb.v, d["ssd_norm_g"][l])

        xall = S.sbuf("s_xall", [128, NT, 768], stack=st)
        bct = S.sbuf("s_bct", [128, NT, 4, 128], stack=st)
        dta = S.sbuf("s_dta", [128, NT, 2, 16], stack=st)
        H = S.sbuf("s_H", [128, 2, 2, 256], stack=st)
        S.memset("dve", H.v, 0.0)
        st2 = ExitStack()
        cw = S.sbuf("s_cw", [128, 4, 1024], stack=st2)
        load_bcast(S, "sp", cw[:, 0:3, :], d["ssd_conv_w"][l])
        load_bcast(S, "pool", cw[:, 3, :], d["ssd_conv_b"][l])
        sh = [[S.sbuf(f"s_sh{i}{j}", [128, 1024], stack=st2) for j in range(3)] for i in range(2)]
        ta = [S.sbuf(f"s_ta{i}", [128, 1024], stack=st2) for i in range(2)]
        tb = [S.sbuf(f"s_tb{i}", [128, 1024], stack=st2) for i in range(2)]
        dtr = [S.sbuf(f"s_dtr{i}", [128, 16], stack=st2) for i in range(2)]
        for ti in range(NT):
            t0 = ti * 128
            tl = sh[ti % 2]
            load_shifted(S, tl, src, t0, 1024, 512, ti in SEG_FIRST, ti in SEG_LAST)
            a, b = ta[ti % 2], tb[ti % 2]
            S.tt("dve", a.v, tl[0].v, cw[:, 0, :], ALU.mult)
            S.tt("dve", b.v, tl[1].v, cw[:, 1, :], ALU.mult)
            S.tt("dve", a.v, a.v, b.v, ALU.add)
            S.tt("dve", b.v, tl[2].v, cw[:, 2, :], ALU.mult)
            S.tt("dve", a.v, a.v, b.v, ALU.add)
            S.tt("dve", a.v, a.v, cw[:, 3, :], ALU.add)
            S.act(b.v, a.v, AF.Silu)
            S.copy("act", xall[:, ti, :], b[:, 0:768])
            r = dtr[ti % 2]
            S.dma("sp", r.v, src[t0:t0 + 128, 1536:1552])
            S.tt("dve", r.v, r.v, dtb.v, ALU.add)
            S.act(r.v, r.v, AF.Exp)
            S.act(dta[:, ti, 0, :], r.v, AF.Ln, bias=1.0)
            S.tt("dve", dta[:, ti, 1, :], dta[:, ti, 0, :], negA.v, ALU.mult)
            for q in range(4):
                S.transpose(P[0][:, q * 128:(q + 1) * 128], b[:, 512 + q * 128:512 + (q + 1) * 128], C.ident.v)
            S.copy("dve", bct[:, ti, :, :], P[0].v.rearrange("p (q t) -> p q t", q=4))

        S.barrier()
        st2.close()
        yacc = S.sbuf("s_yacc", [128, NT, 512], stack=st)
        ct = [S.sbuf(f"s_ct{i}", [128, 16], stack=st) for i in range(2)]
        te = [S.sbuf(f"s_te{i}", [128, 8], stack=st) for i in range(2)]
        sc = [S.sbuf(f"s_sc{i}", [128, 8], stack=st) for i in range(2)]
        ee = [S.sbuf(f"s_ee{i}", [128, 16], stack=st) for i in range(2)]
        xs = [S.sbuf(f"s_xs{i}", [128, 512], stack=st) for i in range(2)]
        sm = [S.sbuf(f"s_sm{i}", [128, 2, 128], stack=st) for i in range(2)]
        atr = [S.sbuf(f"s_atr{i}", [128, 128], stack=st) for i in range(3)]
        sg = [S.sbuf(f"s_sg{i}", [128, 128], stack=st) for i in range(3)]
        mt = [S.sbuf(f"s_mt{i}", [128, 128], stack=st) for i in range(3)]
        yt = [S.sbuf(f"s_yt{i}", [128, 512], stack=st) for i in range(2)]
        it = 0
        ih = 0
        for dr in range(2):
            tri = triF if dr == 0 else triB
            order = [16, 17] + list(range(16)) if dr == 0 else [17, 16] + list(range(15, -1, -1))
            for ti in order:
                k = it % 2
                it += 1
                dt = dta[:, ti, 0, dr * 8:(dr + 1) * 8]
                aa = dta[:, ti, 1, dr * 8:(dr + 1) * 8]
                X = xall[:, ti, 0:512]
                Bm = xall[:, ti, 512:768]
                S.matmul(P[1][:, 0:8], tri.v, aa)
                S.matmul(P[1][:, 8:16], C.ones.v, aa)
                S.copy("dve", ct[k].v, P[1][:, 0:16])
                cum, tot = ct[k][:, 0:8], ct[k][:, 8:16]
                S.tt("dve", te[k].v, tot, cum, ALU.subtract)
                S.act(te[k].v, te[k].v, AF.Exp)
                S.tt("dve", sc[k].v, te[k].v, dt, ALU.mult)
                S.act(ee[k].v, ct[k].v, AF.Exp)
                S.tt("dve", xs[k].v.rearrange("p (h e) -> p h e", h=8), X.rearrange("p (h e) -> p h e", h=8),
                     bc_last(sc[k].v, 64), ALU.mult)
                for g in range(2):
                    S.matmul(P[3][:, g * 256:(g + 1) * 256], bct[:, ti, 2 + g, :], H[:, dr, g, :])
                    S.matmul(P[5][:, g * 128:(g + 1) * 128], bct[:, ti, g, :], bct[:, ti, 2 + g, :])
                for g in range(2):
                    S.tt("dve", sm[k][:, g, :], P[5][:, g * 128:(g + 1) * 128], tri.v, ALU.mult)
                for g in range(2):
                    S.matmul(P[2][:, g * 256:(g + 1) * 256], Bm[:, g * 128:(g + 1) * 128], xs[k][:, g * 256:(g + 1) * 256])
                for g in range(2):
                    Hg = H[:, dr, g, :].rearrange("p (h e) -> p h e", h=4)
                    S.tt("dve", Hg, Hg, bc_last(ee[k][:, 8 + g * 4:8 + (g + 1) * 4], 64), ALU.mult)
                    S.tt("dve", H[:, dr, g, :], H[:, dr, g, :], P[2][:, g * 256:(g + 1) * 256], ALU.add)
                for h in range(8):
                    g = h // 4
                    j = ih % 3
                    pb = P[6 + ih % 2]
                    ih += 1
                    S.act(atr[j].v, tri.v, AF.Copy, scale=aa[:, h:h + 1])
                    S.matmul(pb[:, 0:128], C.ones.v, atr[j].v)
                    S.ts("dve", sg[j].v, pb[:, 0:128], cum[:, h:h + 1], ALU.subtract, 0.0, ALU.min)
                    S.act(sg[j].v, sg[j].v, AF.Exp)
                    S.stt("dve", mt[j].v, sg[j].v, dt[:, h:h + 1], sm[k][:, g, :], ALU.mult, ALU.mult)
                    S.matmul(P[4][:, h * 64:(h + 1) * 64], mt[j].v, X[:, h * 64:(h + 1) * 64])
                y = yt[k]
                S.tt("dve", y.v.rearrange("p (h e) -> p h e", h=8), P[3].v.rearrange("p (h e) -> p h e", h=8),
                     bc_last(ee[k][:, 0:8], 64), ALU.mult)
                if dr == 0:
                    S.tt("dve", yacc[:, ti, :], y.v, P[4].v, ALU.add)
                else:
                    S.tt("dve", y.v, y.v, P[4].v, ALU.add)
                    S.tt("dve", yacc[:, ti, :], yacc[:, ti, :], y.v, ALU.add)
        zt = [S.sbuf(f"s_z{i}", [128, 512], stack=st) for i in range(2)]
        y2 = [S.sbuf(f"s_y2{i}", [128, 512], stack=st) for i in range(2)]
        ssq = [S.sbuf(f"s_ssq{i}", [128, 1], stack=st) for i in range(2)]
        junk = S.sbuf("s_junk", [128, 512], stack=st)
        ot = [S.sbuf(f"s_ot{i}", [128, 4, 128], BF16, stack=st) for i in range(2)]
        tiles = list(range(NT)) if ctx_out else list(range(16))
        for n, ti in enumerate(tiles):
            k = n % 2
            t0 = ti * 128
            S.dma("sp", zt[k].v, src[t0:t0 + 128, 0:512])
            S.act(zt[k].v, zt[k].v, AF.Silu)
            y = y2[k]
            S.tt("dve", y.v.rearrange("p (h e) -> p h e", h=8), xall[:, ti, 0:512].rearrange("p (h e) -> p h e", h=8),
                 bc_last(dsk.v, 64), ALU.mult)
            S.tt("dve", y.v, y.v, yacc[:, ti, :], ALU.add)
            S.tt("dve", y.v, y.v, zt[k].v, ALU.mult)
            S.memset("pool", ssq[k].v, 0.0)
            S.act(junk.v, y.v, AF.Square, accum=ssq[k].v)
            S.ts("dve", ssq[k].v, ssq[k].v, 1.0 / 512, ALU.mult, NORM_EPS, ALU.add)
            S.act(ssq[k].v, ssq[k].v, AF.Ln)
            S.act(ssq[k].v, ssq[k].v, AF.Exp, scale=-0.5)
            S.stt("dve", y.v, y.v, ssq[k].v, ngb.v, ALU.mult, ALU.mult)
            for q in range(4):
                S.transpose(P[0][:, q * 128:(q + 1) * 128], y[:, q * 128:(q + 1) * 128], C.ident.v)
            S.copy("act", ot[k].v, P[0].v.rearrange("p (q t) -> p q t", q=4))
            S.dma("pool", out[:, :, t0:t0 + 128], ot[k].v)
    S.barrier()


def tri_consts():
    tf = np.triu(np.ones((128, 128), np.float32))
    return tf, np.ascontiguousarray(tf.T)


GRID_W = 64
NROWS = 32


def na_tables(rpb):
    kc = np.arange(64)[:, None]
    c = np.arange(64)[None, :]
    ci = np.clip(kc - c, -15, 15) + 15
    col_start = np.clip(np.arange(64) - 8, 0, 48)
    in_win = (kc >= col_start[None, :]) & (kc < col_start[None, :] + 16)
    dr0 = np.arange(14)
    wl = np.arange(2)
    ri = dr0[None, :] + wl[:, None]
    b = rpb[:, :, ri[:, None, :, None], ci[None, :, None, :]]
    b = np.ascontiguousarray(b.reshape(rpb.shape[0], 8, 128, 14, 64), dtype=np.float32)
    m = np.broadcast_to(in_win[None, :, None, :], (2, 64, 14, 64)).reshape(128, 14 * 64)
    return b, np.ascontiguousarray(m, dtype=np.float32)


def stage_na(C, l, ctx_out):
    S = C.S
    d = C.d
    qk = d["na_qkT"]
    vd = d["na_v"]
    out = d["brT"][3]
    P = C.ps
    with ExitStack() as st:
        mask = S.sbuf("na_mask", [128, 14 * 64], stack=st)
        S.dma("sp", mask.v, d["na_mask"])
        qT = S.sbuf("na_q", [128, T], stack=st)
        kT = S.sbuf("na_k", [128, T], stack=st)
        qB = S.sbuf("na_qb", [128, T], BF16, stack=st)
        kB = S.sbuf("na_kb", [128, T], BF16, stack=st)
        ve = S.sbuf("na_ve", [128, 18, 128], BF16, stack=st)
        vo = S.sbuf("na_vo", [128, 15, 128], BF16, stack=st)
        ones64 = S.sbuf("na_1b", [128, 64], BF16, stack=st)
        S.memset("pool", ones64.v, 1.0)
        eb = [S.sbuf(f"na_eb{i}", [128, 14 * 64], stack=st) for i in range(2)]
        et = [S.sbuf(f"na_e{i}", [128, 384], BF16, stack=st) for i in range(3)]
        ef = [S.sbuf(f"na_ef{i}", [128, 256], stack=st) for i in range(3)]
        ec = [S.sbuf(f"na_ec{i}", [128, 256], BF16, stack=st) for i in range(2)]
        rr = [S.sbuf(f"na_r{i}", [64, 256], stack=st) for i in range(2)]
        ob = [S.sbuf(f"na_ob{i}", [64, T], BF16, stack=st) for i in range(2)]
        ei = 0
        ri_ = 0
        for hp in range(4):
            S.dma("sp", qT.v, qk[hp * 128:(hp + 1) * 128, :])
            S.dma("act", kT.v, qk[512 + hp * 128:512 + (hp + 1) * 128, :])
            S.dma("pool", ve.v, vd[:, hp * 128:(hp + 1) * 128].rearrange("(b p) e -> p b e", p=128))
            S.dma("pool", vo.v, vd[64:64 + 15 * 128, hp * 128:(hp + 1) * 128].rearrange("(b p) e -> p b e", p=128))
            S.copy("dve", qB.v, qT.v)
            S.copy("act", kB.v, kT.v)
            for hh in range(2):
                h = hp * 2 + hh
                ebt = eb[h % 2]
                o = ob[h % 2]
                S.dma("sp", ebt.v, d["na_bias"][l][h].rearrange("p a c -> p (a c)"))
                S.act(ebt.v, ebt.v, AF.Exp)
                S.tt("dve", ebt.v, ebt.v, mask.v, ALU.mult)
                ebv = ebt.v.rearrange("p (a c) -> p a c", c=64)
                pl, ph = hh * 64, (hh + 1) * 64
                def emitA(r):
                    rs = min(max(r - 4, 0), NROWS - 8)
                    q = qB[pl:ph, r * 64:(r + 1) * 64]
                    ps = P[4 + r % 4]
                    e = et[r % 3]
                    f = ef[r % 3]
                    for i, w0 in enumerate((0, 2, 4, 6)):
                        kr = rs + w0
                        S.matmul(ps[:, i * 64:(i + 1) * 64], kB[pl:ph, kr * 64:kr * 64 + 128], q)
                    for cb in range(2):
                        S.matmul(ps[:, 256 + cb * 64:256 + (cb + 1) * 64], kB[pl:ph, NLAT + cb * 128:NLAT + (cb + 1) * 128], q)
                    dr0 = rs - r + 7
                    S.act(f.v, ps[:, 0:256], AF.Exp, scale=0.125)
                    S.act(e[:, 256:384], ps[:, 256:384], AF.Exp, scale=0.125)
                    S.tt("dve", e[:, 0:256].rearrange("p (a c) -> p a c", c=64), f.v.rearrange("p (a c) -> p a c", c=64),
                         ebv[:, dr0:dr0 + 7:2, :], ALU.mult)

                def emitB(r):
                    nonlocal ri_
                    rs = min(max(r - 4, 0), NROWS - 8)
                    acc = P[r % 2]
                    accs = P[2 + r % 2]
                    e = et[r % 3]
                    vts = []
                    for w0 in (0, 2, 4, 6):
                        kr = rs + w0
                        vts.append(ve[:, kr // 2, pl:ph] if kr % 2 == 0 else vo[:, (kr - 1) // 2, pl:ph])
                    for cb in range(2):
                        vts.append(ve[:, 16 + cb, pl:ph])
                    for i in range(6):
                        S.matmul(acc[0:64, 0:64], vts[i], e[:, i * 64:(i + 1) * 64], start=(i == 0), stop=(i == 5))
                    for i in range(6):
                        S.matmul(accs[0:64, 0:64], ones64.v, e[:, i * 64:(i + 1) * 64], start=(i == 0), stop=(i == 5))
                    rt = rr[ri_ % 2]
                    ri_ += 1
                    S.recip(rt[:, 0:64], accs[0:64, 0:64])
                    S.tt("dve", o[:, r * 64:(r + 1) * 64], acc[0:64, 0:64], rt[:, 0:64], ALU.mult)

                LA = 2
                for r in range(NROWS + LA):
                    if r < NROWS:
                        emitA(r)
                    if r - LA >= 0:
                        emitB(r - LA)
                if ctx_out:
                    acc = P[0]
                    accs = P[2]
                    q = qB[pl:ph, NLAT:T]
                    for cb in range(2):
                        ps = P[4 + ei % 4]
                        ei += 1
                        e = ec[cb]
                        S.matmul(ps[:, 0:256], kB[pl:ph, NLAT + cb * 128:NLAT + (cb + 1) * 128], q)
                        S.act(e.v, ps[:, 0:256], AF.Exp, scale=0.125)
                    for cb in range(2):
                        S.matmul(acc[0:64, 0:256], ve[:, 16 + cb, pl:ph], ec[cb].v, start=(cb == 0), stop=(cb == 1))
                    for cb in range(2):
                        S.matmul(accs[0:64, 0:256], ones64.v, ec[cb].v, start=(cb == 0), stop=(cb == 1))
                    rt = rr[ri_ % 2]
                    ri_ += 1
                    S.recip(rt.v, accs[0:64, 0:256])
                    S.tt("dve", o[:, NLAT:T], acc[0:64, 0:256], rt.v, ALU.mult)
                    S.dma("sp", out[h * 64:(h + 1) * 64, :], o.v)
                else:
                    S.dma("sp", out[h * 64:(h + 1) * 64, 0:NLAT], o[:, 0:NLAT])
    S.barrier()

import os
RW_STOP = os.environ.get('RW_STOP', '')
RW_NT = int(os.environ.get('RW_NT', '99'))
RW_LP = BF16 if os.environ.get('RW_LP', 'f32') == 'bf16' else F32

RW_GN_EPS = 64e-5
EXPM05 = 0.6065306597126334


def rw_consts():
    s = np.arange(128)[:, None]
    t = np.arange(128)[None, :]
    f = np.float32
    US = (s < t).astype(f)
    UF = (s <= t).astype(f)
    LS = (s > t).astype(f)
    LF = (s >= t).astype(f)
    I_ = np.eye(128, dtype=f)
    masks = np.stack([np.tile(m_, (1, 4)) for m_ in (US, UF, LS, LF, -US, -LS, I_)], 1)
    lo = (s <= 63).astype(f)
    hi = (s >= 64).astype(f)
    dq = np.stack([UF - lo, US - lo, LF - hi, LS - hi], 1)
    mvec = np.stack([np.concatenate([lo, hi], 1), np.concatenate([hi, lo], 1)], 1)
    return np.ascontiguousarray(masks), np.ascontiguousarray(dq), np.ascontiguousarray(mvec.astype(f))


def rw_phase1(C, l):
    S = C.S
    d = C.d
    src = d["p_rw"]
    prep = d["rw_prep"]
    P = C.ps
    with ExitStack() as st:
        mu = S.sbuf("rw_mu", [128, 3, 1920], stack=st)
        load_bcast(S, "sp", mu[:, 0:2, :], d["rw_mu"][l])
        S.tt("dve", mu[:, 2, :], mu[:, 0, :], mu[:, 1, :], ALU.add)
        S.ts("dve", mu[:, 2, :], mu[:, 2, :], -1.0, ALU.mult, 1.0, ALU.add)
        w0b = S.sbuf("rw_w0b", [128, 2, 512], stack=st)
        a0b = S.sbuf("rw_a0b", [128, 2, 512], stack=st)
        load_bcast(S, "pool", w0b.v, d["rw_w0"][l])
        load_bcast(S, "pool", a0b.v, d["rw_a0"][l])
        kkb = S.sbuf("rw_kkb", [128, 512], stack=st)
        kab = S.sbuf("rw_kab", [128, 512], stack=st)
        rkb = S.sbuf("rw_rkb", [128, 512], stack=st)
        load_bcast(S, "sp", kkb.v, d["rw_k_k"][l])
        load_bcast(S, "sp", kab.v, d["rw_k_a"][l])
        load_bcast(S, "sp", rkb.v, d["rw_r_k"][l])
        wup = S.sbuf("rw_wup", [128, 512], stack=st)
        aup = S.sbuf("rw_aup", [128, 512], stack=st)
        gup = S.sbuf("rw_gup", [128, 512], stack=st)
        S.dma("sp", wup.v, d["rw_w_up"][l])
        S.dma("sp", aup.v, d["rw_a_up"][l])
        S.dma("sp", gup.v, d["rw_g_up"][l])
        sh = [[S.sbuf(f"rw_sh{i}{j}", [128, 1920], stack=st) for j in range(3)] for i in range(2)]
        sb = [S.sbuf(f"rw_s{i}", [128, 1920], stack=st) for i in range(2)]
        t1 = S.sbuf("rw_t1", [128, 1920], stack=st)
        th = S.sbuf("rw_th", [128, 2, 128], stack=st)
        thT = S.sbuf("rw_thT", [128, 3, 128], stack=st)
        ot = [S.sbuf(f"rw_ot{i}", [128, 11, 512], stack=st) for i in range(2)]
        wk = [S.sbuf(f"rw_wk{i}", [128, 512], stack=st) for i in range(6)]
        sm = [S.sbuf(f"rw_sm{i}", [128, 8], stack=st) for i in range(4)]

        def h8(v):
            return v.rearrange("p (h e) -> p h e", h=8)

        for ti in range(NT):
            t0 = ti * 128
            tl = sh[ti % 2]
            s = sb[ti % 2]
            o = ot[ti % 2]
            load_shifted(S, tl, src, t0, 1920, 0, ti in SEG_FIRST, ti in SEG_LAST)
            S.tt("dve", s.v, tl[1].v, mu[:, 2, :], ALU.mult)
            S.tt("dve", t1.v, tl[0].v, mu[:, 0, :], ALU.mult)
            S.tt("dve", s.v, s.v, t1.v, ALU.add)
            S.tt("dve", t1.v, tl[2].v, mu[:, 1, :], ALU.mult)
            S.tt("dve", s.v, s.v, t1.v, ALU.add)
            r, k, v = s[:, 0:512], s[:, 512:1024], s[:, 1024:1536]
            S.copy("act", o[:, 0, :], r)
            S.copy("act", o[:, 1, :], v)
            S.act(th[:, 0, :], s[:, 1536:1664], AF.Tanh)
            S.act(th[:, 1, :], s[:, 1792:1920], AF.Sigmoid)
            S.transpose(P[0][:, 0:128], th[:, 0, :], C.ident.v)
            S.transpose(P[0][:, 128:256], s[:, 1664:1792], C.ident.v)
            S.transpose(P[0][:, 256:384], th[:, 1, :], C.ident.v)
            S.copy("dve", thT.v, P[0][:, 0:384].rearrange("p (q t) -> p q t", q=3))
            a_t = [wk[0], wk[1]]
            for dr in range(2):
                pl, ph = dr * 64, (dr + 1) * 64
                S.matmul(P[1 + dr].v, thT[pl:ph, 0, :], wup[pl:ph, :])
                lw = o[:, 5 + 3 * dr, :]
                S.tt("dve", lw, P[1 + dr].v, w0b[:, dr, :], ALU.add)
                S.act(lw, lw, AF.Sigmoid)
                S.ts("dve", lw, lw, -EXPM05, ALU.mult)
                S.matmul(P[3 + dr].v, thT[pl:ph, 1, :], aup[pl:ph, :])
                S.tt("dve", a_t[dr].v, P[3 + dr].v, a0b[:, dr, :], ALU.add)
                S.act(a_t[dr].v, a_t[dr].v, AF.Sigmoid)
            S.matmul(P[5].v, thT[:, 2, :], gup.v)
            S.copy("act", o[:, 3, :], P[5].v)
            kk = o[:, 2, :]
            S.tt("dve", kk, k, kkb.v, ALU.mult)
            S.tt("dve", wk[2].v, kk, kk, ALU.mult)
            S.reduce("dve", sm[0].v, h8(wk[2].v), ALU.add)
            S.act(sm[0].v, sm[0].v, AF.Sqrt)
            S.ts("dve", sm[0].v, sm[0].v, 1e-12, ALU.max)
            S.recip(sm[0].v, sm[0].v)
            S.tt("dve", h8(kk), h8(kk), bc_last(sm[0].v, 64), ALU.mult)
            S.tt("dve", wk[3].v, r, rkb.v, ALU.mult)
            for dr in range(2):
                kd = o[:, 6 + 3 * dr, :]
                bb = o[:, 7 + 3 * dr, :]
                S.stt("dve", wk[4].v, a_t[dr].v, -1.0, kab.v, ALU.add, ALU.mult)
                S.stt("dve", kd, wk[4].v, 1.0, k, ALU.add, ALU.mult)
                S.tt("dve", bb, kk, a_t[dr].v, ALU.mult)
                S.tt("dve", wk[5].v, wk[3].v, kd, ALU.mult)
                S.reduce("dve", sm[1 + dr].v, h8(wk[5].v), ALU.add)
            S.tt("dve", sm[3].v, sm[1].v, sm[2].v, ALU.add)
            S.tt("dve", h8(o[:, 4, :]), h8(v), bc_last(sm[3].v, 64), ALU.mult)
            S.dma("act", prep[t0:t0 + 128, :, :], o.v)
    S.barrier()


def run_interleaved(gens):
    gens = list(gens)
    while gens:
        for g in list(gens):
            try:
                next(g)
            except StopIteration:
                gens.remove(g)


def rw_phase2(C, l):
    S = C.S
    d = C.d
    prep = d["rw_prep"]
    P = C.ps
    with ExitStack() as st:
        msk = S.sbuf("rw_msk", [128, 7, 512], stack=st)
        dqm = S.sbuf("rw_dq", [128, 4, 128], stack=st)
        mv = S.sbuf("rw_mv", [128, 2, 2], stack=st)
        S.dma("sp", msk.v, d["rw_masks"])
        S.dma("sp", dqm.v, d["rw_dqc"])
        S.dma("sp", mv.v, d["rw_mvec"])
        St = S.sbuf("rw_St", [128, 2, 4, 64], stack=st)
        S.memset("dve", St.v, 0.0)
        R_ = []
        for dr in range(2):
            def mk(nm, w=512, n=2, dt_=F32):
                return [S.sbuf(f"rw_{nm}{dr}{h}", [128, w], dt_, stack=st) for h in range(n)]
            LP = RW_LP
            res = dict(
                S0s=S.sbuf(f"rw_S0s{dr}", [128, 4, 64], stack=st),
                inp=S.sbuf(f"rw_in{dr}", [128, 6, 512], stack=st),
                ex=mk("ex", 512, 3), tm=mk("tm", 512, 4),
                tT=[S.sbuf(f"rw_tT{dr}{j}", [128, 4, 128], LP, stack=st) for j in range(4)],
                eh=S.sbuf(f"rw_eh{dr}", [128, 4, 2], stack=st),
                Q=[mk("Qa", dt_=LP), mk("Qb", dt_=LP)], R=[mk("Ra", dt_=LP), mk("Rb", dt_=LP)],
                Y=[mk("Ya", dt_=LP), mk("Yb", dt_=LP)],
                BmT=mk("BmT", dt_=LP), AbT=mk("AbT", dt_=LP), AkT=mk("AkT", dt_=LP), Wsb=mk("Wsb", 256, dt_=LP),
                Usb=S.sbuf(f"rw_U{dr}", [128, 4, 128], LP, stack=st),
                ob=S.sbuf(f"rw_ob{dr}", [128, 512], stack=st),
                lpc=S.sbuf(f"rw_lpc{dr}", [128, 3, 512], LP, stack=st),
                S0b=S.sbuf(f"rw_S0b{dr}", [128, 4, 64], LP, stack=st),
            )
            R_.append(res)
        bc_ = [0]

        def bank():
            b_ = P[4 + bc_[0] % 4]
            bc_[0] += 1
            return b_

        ev = [0]

        def evac_copy(dst, src):
            ev[0] += 1
            S.copy("act" if ev[0] % 2 else "dve", dst, src)

        def q4(v):
            return v.rearrange("p (q t) -> p q t", q=4)

        def hinfo(h):
            hp, hh = h // 2, h % 2
            return hp, hh, hh * 64, (hh + 1) * 64

        def dir_gen(dr):
            rs_ = R_[dr]
            PA, PB = P[2 * dr], P[2 * dr + 1]
            S0s, X, ex, eh, Usb = rs_["S0s"], rs_["inp"], rs_["ex"], rs_["eh"], rs_["Usb"]
            Q, R, Y, BmT, AbT, AkT, Wsb = (rs_[k_] for k_ in ("Q", "R", "Y", "BmT", "AbT", "AkT", "Wsb"))
            order = [16, 17] + list(range(16)) if dr == 0 else [17, 16] + list(range(15, -1, -1))
            if dr == 0:
                mS, mF, mSn, mAn = msk[:, 0, :], msk[:, 1, :], msk[:, 4, :], msk[:, 5, :]
            else:
                mS, mF, mSn, mAn = msk[:, 2, :], msk[:, 3, :], msk[:, 5, :], msk[:, 4, :]
            for ti in order[:RW_NT]:
                t0 = ti * 128
                S.dma("sp", X[:, 0:3, :], prep[t0:t0 + 128, 0:3, :])
                S.dma("act", X[:, 3:6, :], prep[t0:t0 + 128, 5 + 3 * dr:8 + 3 * dr, :])
                r_, v_, kk_, lw_, kd_, b_ = (X[:, j, :] for j in range(6))
                S.matmul(PA.v, dqm[:, 2 * dr, :], lw_)
                S.matmul(PB.v, dqm[:, 2 * dr + 1, :], lw_)
                S.act(ex[0].v, PA.v, AF.Exp)
                S.act(ex[1].v, PA.v, AF.Exp, scale=-1.0)
                S.act(ex[2].v, PB.v, AF.Exp)
                rq, kq, bn, kn = rs_["tm"]
                S.tt("dve", rq.v, r_, ex[0].v, ALU.mult)
                S.tt("dve", kq.v, kk_, ex[2].v, ALU.mult)
                S.tt("dve", bn.v, b_, ex[1].v, ALU.mult)
                S.tt("dve", kn.v, kd_, ex[1].v, ALU.mult)
                lpc = rs_["lpc"]
                S0b = rs_["S0b"]
                if RW_LP is F32:
                    vB, bnB, knB = v_, bn.v, kn.v
                else:
                    S.copy("act", lpc[:, 0, :], v_)
                    S.copy("act", lpc[:, 1, :], bn.v)
                    S.copy("act", lpc[:, 2, :], kn.v)
                    vB, bnB, knB = lpc[:, 0, :], lpc[:, 1, :], lpc[:, 2, :]
                yield
                rqT, kqT, bnT, knT = rs_["tT"]
                for j, (src_, dst_) in enumerate(((rq, rqT), (kq, kqT), (bn, bnT), (kn, knT))):
                    pb = (PA, PB)[j % 2]
                    for hp in range(4):
                        S.transpose(pb[:, hp * 128:(hp + 1) * 128], src_[:, hp * 128:(hp + 1) * 128], C.ident.v)
                    evac_copy(dst_.v, q4(pb.v))
                eg = bank()
                for hp in range(4):
                    S.matmul(eg[:, hp * 2:(hp + 1) * 2], lw_[:, hp * 128:(hp + 1) * 128], mv[:, dr, :])
                S.act(eh.v, eg[:, 0:8].rearrange("p (q c) -> p q c", c=2), AF.Exp)
                for hp in range(4):
                    S.ts("dve", S0s[:, hp, :], St[:, dr, hp, :], eh[:, hp, 0:1], ALU.mult)
                if RW_LP is F32:
                    S0m = S0s
                else:
                    S.copy("act", S0b.v, S0s.v)
                    S0m = S0b
                yield
                for half in range(2):
                    hs = [2 * i_ + half for i_ in range(4)]
                    specs = ((bnT, kqT, Q[0][half], mSn), (kqT, bnT, R[0][half], mAn), (knT, kqT, BmT[half], mS),
                             (bnT, rqT, AbT[half], mF), (knT, rqT, AkT[half], mF))
                    for (LT, RT, dst, mk_) in specs:
                        g = bank()
                        for i, h in enumerate(hs):
                            hp, hh, pl, ph = hinfo(h)
                            S.matmul(g[:, i * 128:(i + 1) * 128], LT[pl:ph, hp, :], RT[pl:ph, hp, :])
                        S.tt("dve", dst.v, g.v, mk_, ALU.mult)
                    S.tt("dve", Y[0][half].v, Q[0][half].v, msk[:, 6, :], ALU.add)
                    yield
                for half in range(2):
                    g = bank()
                    for i, h in enumerate([2 * i_ + half for i_ in range(4)]):
                        hp, hh, pl, ph = hinfo(h)
                        S.matmul(g[:, i * 64:(i + 1) * 64], kqT[pl:ph, hp, :], S0m[pl:ph, hp, :], start=True, stop=False)
                        S.matmul(g[:, i * 64:(i + 1) * 64], BmT[half][:, i * 128:(i + 1) * 128], vB[:, h * 64:(h + 1) * 64],
                                 start=False, stop=True)
                    evac_copy(Wsb[half].v, g[:, 0:256])
                yield
                cur = 0
                for lev in range(1, 7):
                    nxt = 1 - cur
                    for half in range(2):
                        if lev < 6:
                            g = bank()
                            for i in range(4):
                                sl = slice(i * 128, (i + 1) * 128)
                                S.matmul(g[:, sl], R[cur][half][:, sl], Q[cur][half][:, sl])
                            evac_copy(Q[nxt][half].v, g.v)
                        g = bank()
                        for i in range(4):
                            sl = slice(i * 128, (i + 1) * 128)
                            S.matmul(g[:, sl], Q[cur][half][:, sl], R[cur][half][:, sl])
                        evac_copy(R[nxt][half].v, g.v)
                    yield
                    for half in range(2):
                        g = bank()
                        for i in range(4):
                            sl = slice(i * 128, (i + 1) * 128)
                            S.matmul(g[:, sl], R[nxt][half][:, sl], Y[cur][half][:, sl])
                        S.tt("dve", Y[nxt][half].v, Y[cur][half].v, g.v, ALU.add)
                    cur = nxt
                    yield
                for half in range(2):
                    g = bank()
                    for i in range(4):
                        S.matmul(g[:, i * 64:(i + 1) * 64], Y[cur][half][:, i * 128:(i + 1) * 128], Wsb[half][:, i * 64:(i + 1) * 64])
                    S.ts("dve", Usb[:, :, half * 64:(half + 1) * 64], g[:, 0:256].rearrange("p (a b) -> p a b", a=4), -1.0, ALU.mult)
                yield
                for h in range(8):
                    hp, hh, pl, ph = hinfo(h)
                    half, i = h % 2, h // 2
                    oc = PB[:, h * 64:(h + 1) * 64]
                    S.matmul(oc, rqT[pl:ph, hp, :], S0m[pl:ph, hp, :], start=True, stop=False)
                    S.matmul(oc, AbT[half][:, i * 128:(i + 1) * 128], Usb[:, hp, hh * 64:(hh + 1) * 64], start=False, stop=False)
                    S.matmul(oc, AkT[half][:, i * 128:(i + 1) * 128], vB[:, h * 64:(h + 1) * 64], start=False, stop=True)
                S.copy("act", rs_["ob"].v, PB.v)
                S.dma("sp", d["rw_o"][dr, t0:t0 + 128, :], rs_["ob"].v)
                yield
                g = bank()
                for hp in range(4):
                    sl = slice(hp * 128, (hp + 1) * 128)
                    S.matmul(g[:, sl], bnB[:, sl], Usb[:, hp, :], start=True, stop=False)
                    S.matmul(g[:, sl], knB[:, sl], vB[:, sl], start=False, stop=True)
                for hp in range(4):
                    for hh in range(2):
                        pl, ph = hh * 64, (hh + 1) * 64
                        S.tt("dve", St[pl:ph, dr, hp, :], S0s[pl:ph, hp, :], g[pl:ph, hp * 128 + hh * 64:hp * 128 + (hh + 1) * 64], ALU.add)
                for hp in range(4):
                    S.ts("dve", St[:, dr, hp, :], St[:, dr, hp, :], eh[:, hp, 1:2], ALU.mult)
                yield

        run_interleaved([dir_gen(0), dir_gen(1)])
    S.barrier()
    with ExitStack() as st:
        rw_phase3(C, l, st, None)
    S.barrier()


def rw_phase3(C, l, st, oacc):
    S = C.S
    d = C.d
    prep = d["rw_prep"]
    out = d["brT"][0].rearrange("(k p) t -> p k t", p=128)
    P = C.ps
    lg = S.sbuf("rw_lg", [128, 2, 512], stack=st)
    load_bcast(S, "sp", lg[:, 0, :], d["rw_ln_g"][l])
    load_bcast(S, "sp", lg[:, 1, :], d["rw_ln_b"][l])
    gb = [S.sbuf(f"rw_gb{i}", [128, 2, 512], stack=st) for i in range(2)]
    cen = [S.sbuf(f"rw_cen{i}", [128, 512], stack=st) for i in range(2)]
    sq = S.sbuf("rw_sq3", [128, 512], stack=st)
    mn = [S.sbuf(f"rw_mn{i}", [128, 8], stack=st) for i in range(2)]
    vr = [S.sbuf(f"rw_vr{i}", [128, 8], stack=st) for i in range(2)]
    ot = [S.sbuf(f"rw_o3{i}", [128, 4, 128], BF16, stack=st) for i in range(2)]

    def h8(v):
        return v.rearrange("p (h e) -> p h e", h=8)

    tiles = C.rw_tiles
    of = [S.sbuf(f"rw_of{i}", [128, 2, 512], stack=st) for i in range(2)]
    for n, ti in enumerate(tiles):
        k = n % 2
        t0 = ti * 128
        S.dma("sp", gb[k].v, prep[t0:t0 + 128, 3:5, :])
        S.dma("act", of[k][:, 0, :], d["rw_o"][0, t0:t0 + 128, :])
        S.dma("act", of[k][:, 1, :], d["rw_o"][1, t0:t0 + 128, :])
        S.tt("dve", of[k][:, 0, :], of[k][:, 0, :], of[k][:, 1, :], ALU.add)
        o = of[k][:, 0, :]
        S.reduce("dve", mn[k].v, h8(o), ALU.add)
        S.ts("dve", mn[k].v, mn[k].v, 1.0 / 64, ALU.mult)
        c = cen[k]
        S.tt("dve", h8(c.v), h8(o), bc_last(mn[k].v, 64), ALU.subtract)
        S.act(sq.v, c.v, AF.Square)
        S.reduce("dve", vr[k].v, h8(sq.v), ALU.add)
        S.ts("dve", vr[k].v, vr[k].v, 1.0 / 64, ALU.mult, RW_GN_EPS, ALU.add)
        S.act(vr[k].v, vr[k].v, AF.Ln)
        S.act(vr[k].v, vr[k].v, AF.Exp, scale=-0.5)
        S.tt("dve", h8(c.v), h8(c.v), bc_last(vr[k].v, 64), ALU.mult)
        S.tt("dve", c.v, c.v, lg[:, 0, :], ALU.mult)
        S.tt("dve", c.v, c.v, lg[:, 1, :], ALU.add)
        S.tt("dve", c.v, c.v, gb[k][:, 1, :], ALU.add)
        S.tt("dve", c.v, c.v, gb[k][:, 0, :], ALU.mult)
        for q in range(4):
            S.transpose(P[0][:, q * 128:(q + 1) * 128], c[:, q * 128:(q + 1) * 128], C.ident.v)
        S.copy("act", ot[k].v, P[0].v.rearrange("p (q t) -> p q t", q=4))
        S.dma("sp", out[:, :, t0:t0 + 128], ot[k].v)


def stage_rwkv(C, l, ctx_out):
    C.rw_tiles = list(range(NT)) if ctx_out else list(range(16))
    rw_phase1(C, l)
    rw_phase2(C, l)


DEPTH = 2
IN_TOTAL = 6544

INPUT_SHAPES = {
    "xT": [D_MODEL, T],
    "cc": [128, 16, 2],
    "ident": [128, 128],
    "ada_w": [DEPTH, D_MODEL, 6 * D_MODEL],
    "ada_b": [DEPTH, 128, 96],
    "norm1_g": [DEPTH, 128, 16],
    "norm2_g": [DEPTH, 128, 16],
    "w_in": [DEPTH, D_MODEL, IN_TOTAL],
    "w_gate": [DEPTH, 4, D_MODEL, D_MODEL],
    "gate_b": [DEPTH, 128, 4, 16],
    "w_br": [DEPTH, 4, 512, D_MODEL],
    "w_out": [DEPTH, D_MODEL, D_MODEL],
    "ffn_up": [DEPTH, D_MODEL, 2 * D_FF],
    "ffn_conv_w": [DEPTH, 128, 3, 88],
    "ffn_conv_b": [DEPTH, 128, 88],
    "ffn_down": [DEPTH, 16, 128, 44, 128],
    "final_norm_g": [128, 16],
    "rope_cos": [128, NLAT],
    "rope_sin": [128, NLAT],
    "rope_perm": [128, 128],
    "triF": [128, 128],
    "triB": [128, 128],
    "ssd_conv_w": [DEPTH, 3, 1024],
    "ssd_conv_b": [DEPTH, 1024],
    "ssd_dt_bias": [DEPTH, 16],
    "ssd_a_log": [DEPTH, 16],
    "ssd_d": [DEPTH, 8],
    "ssd_norm_g": [DEPTH, 512],
    "rw_mu": [DEPTH, 2, 1920],
    "rw_w0": [DEPTH, 2, 512],
    "rw_a0": [DEPTH, 2, 512],
    "rw_k_k": [DEPTH, 512],
    "rw_k_a": [DEPTH, 512],
    "rw_r_k": [DEPTH, 512],
    "rw_w_up": [DEPTH, 128, 512],
    "rw_a_up": [DEPTH, 128, 512],
    "rw_g_up": [DEPTH, 128, 512],
    "rw_ln_g": [DEPTH, 512],
    "rw_ln_b": [DEPTH, 512],
    "rw_masks": [128, 7, 512],
    "rw_dqc": [128, 4, 128],
    "rw_mvec": [128, 2, 2],
    "na_bias": [DEPTH, 8, 128, 14, 64],
    "na_mask": [128, 14 * 64],
    "da_lambda": [DEPTH, 4, 64],
    "da_subln_g": [DEPTH, 128, 1],
}

SCRATCH_SHAPES = {
    "hT": [D_MODEL, T],
    "p_rw": [T, 1920],
    "p_ssd": [T, 1552],
    "da_qkT": [1024, T],
    "da_v": [T, 512],
    "na_qkT": [1024, T],
    "na_v": [T, 512],
    "modout": [128, 192],
    "rw_prep": [T, 11, 512],
    "rw_o": [2, T, 512],
    "brT": [4, 512, T],
    "mgT": [D_MODEL, T],
    "guT": [D_FF, T],
    "xres": [D_MODEL, T],
    "outT": [D_MODEL, NLAT],
}


ANNOTATE = False
BF16_SCRATCH = {"hT", "brT", "mgT", "guT"}


def host_inputs(inp, b):
    f = np.float32
    o = {}
    o["xT"] = np.ascontiguousarray(np.concatenate([inp["x"][b].T, inp["ctx"][b].T], axis=1), dtype=f)
    cc = np.stack([inp["c"][b], inp["c_ctx"]], axis=-1)
    o["cc"] = np.ascontiguousarray(cc.reshape(16, 128, 2).transpose(1, 0, 2), dtype=f)
    o["ident"] = np.eye(128, dtype=f)
    o["ada_w"] = inp["ada_w"]
    o["ada_b"] = np.ascontiguousarray(inp["ada_b"].reshape(DEPTH, 96, 128).transpose(0, 2, 1), dtype=f)
    o["norm1_g"] = np.ascontiguousarray(inp["norm1_g"].reshape(DEPTH, 16, 128).transpose(0, 2, 1), dtype=f)
    o["norm2_g"] = np.ascontiguousarray(inp["norm2_g"].reshape(DEPTH, 16, 128).transpose(0, 2, 1), dtype=f)
    o["w_in"] = inp["w_in"]
    o["w_gate"] = inp["w_gate"]
    o["gate_b"] = np.ascontiguousarray(inp["gate_b"].reshape(DEPTH, 4, 16, 128).transpose(0, 3, 1, 2), dtype=f)
    o["w_br"] = inp["w_br"]
    o["w_out"] = inp["w_out"]
    fu = inp["ffn_up"].reshape(DEPTH, D_MODEL, 2, 44, 128).transpose(0, 1, 3, 2, 4)
    o["ffn_up"] = np.ascontiguousarray(fu.reshape(DEPTH, D_MODEL, 2 * D_FF), dtype=f)
    o["ffn_conv_w"] = np.ascontiguousarray(inp["ffn_conv_w"].reshape(DEPTH, 3, 88, 128).transpose(0, 3, 1, 2), dtype=f)
    o["ffn_conv_b"] = np.ascontiguousarray(inp["ffn_conv_b"].reshape(DEPTH, 88, 128).transpose(0, 2, 1), dtype=f)
    o["ffn_down"] = np.ascontiguousarray(inp["ffn_down"].reshape(DEPTH, 44, 128, 16, 128).transpose(0, 3, 2, 1, 4), dtype=f)
    o["rope_cos"], o["rope_sin"], o["rope_perm"] = rope_tables()
    o["triF"], o["triB"] = tri_consts()
    o["ssd_conv_w"] = inp["ssd_conv_w"]
    o["ssd_conv_b"] = inp["ssd_conv_b"]
    o["ssd_dt_bias"] = np.ascontiguousarray(inp["ssd_dt_bias"].reshape(DEPTH, 16), dtype=f)
    o["ssd_a_log"] = np.ascontiguousarray(inp["ssd_a_log"].reshape(DEPTH, 16), dtype=f)
    o["ssd_d"] = inp["ssd_d"]
    o["ssd_norm_g"] = inp["ssd_norm_g"]
    for k_ in ["rw_mu", "rw_w0", "rw_a0", "rw_k_k", "rw_k_a", "rw_ln_g", "rw_ln_b"]:
        o[k_] = inp[k_]
    o["rw_r_k"] = np.ascontiguousarray(inp["rw_r_k"].reshape(DEPTH, 512), dtype=f)
    o["rw_w_up"] = np.ascontiguousarray(inp["rw_w_up"].reshape(DEPTH, 128, 512), dtype=f)
    o["rw_a_up"] = np.ascontiguousarray(inp["rw_a_up"].reshape(DEPTH, 128, 512), dtype=f)
    o["rw_g_up"] = inp["rw_g_up"]
    o["rw_masks"], o["rw_dqc"], o["rw_mvec"] = rw_consts()
    o["na_bias"], o["na_mask"] = na_tables(inp["na_rpb"])
    o["da_lambda"] = inp["da_lambda"]
    o["da_subln_g"] = np.ascontiguousarray(inp["da_subln_g"].reshape(DEPTH, 128, 1), dtype=f)
    o["final_norm_g"] = np.ascontiguousarray(inp["final_norm_g"].reshape(16, 128).T, dtype=f)
    return o


def make_program(stage_fn, ext_in, ext_out):
    nc = bass.Bass("TRN2", target_bir_lowering=False)
    C = Ctx()
    C.nc = nc
    C.d = {}
    allshapes = dict(INPUT_SHAPES)
    allshapes.update(SCRATCH_SHAPES)
    for name, shp in allshapes.items():
        if name in ext_in:
            kind = "ExternalInput"
        elif name in ext_out:
            kind = "ExternalOutput"
        elif name in SCRATCH_SHAPES:
            kind = "Internal"
        else:
            continue
        dtp = BF16 if name in BF16_SCRATCH else F32
        C.d[name] = nc.dram_tensor(name, list(shp), dtp, kind=kind).ap()
    S = Sched(nc)
    S.annotate = ANNOTATE
    C.S = S
    with S.stack:
        init_common(C)
        stage_fn(C)
        S.barrier()
        S.emit()
    return nc, S


def _tagged(C, name, fn, *a, **k):
    C.S.stage = name
    fn(C, *a, **k)
    C.S.stage = None


def full_stages(C):
    d = C.d
    S = C.S
    for l in range(DEPTH):
        last = (l == DEPTH - 1)
        ctx_out = not last
        _tagged(C, f"L{l}_mod", stage_mod, l)
        xsrc = d["xT"] if l == 0 else d["xres"]
        _tagged(C, f"L{l}_norm1", stage_norm, 1, xsrc, d["hT"], norm_chunks(True))
        _tagged(C, f"L{l}_win", stage_win, l)
        _tagged(C, f"L{l}_rwkv", stage_rwkv, l, ctx_out)
        _tagged(C, f"L{l}_ssd", stage_ssd, l, ctx_out)
        _tagged(C, f"L{l}_da", stage_da, l, ctx_out)
        _tagged(C, f"L{l}_na", stage_na, l, ctx_out)
        _tagged(C, f"L{l}_merge", stage_merge, l, ctx_out)
        _tagged(C, f"L{l}_wout", stage_wout, l, ctx_out)
        _tagged(C, f"L{l}_norm2", stage_norm, 2, d["xres"], d["hT"], norm_chunks(ctx_out))
        _tagged(C, f"L{l}_ffnup", stage_ffn_up, l, ctx_out)
        _tagged(C, f"L{l}_ffndn", stage_ffn_down, l, ctx_out)
    with ExitStack() as st:
        fg = S.sbuf("fin_g", [128, 16], stack=st)
        S.dma("sp", fg.v, d["final_norm_g"])
        stage_norm(C, 0, d["xres"], d["outT"], [(c0, w, 0) for (c0, w) in chunks(NLAT, 256)], final_g=fg)


_PROG = {}


def kernel(**inputs):
    inp = {k: np.asarray(v) for k, v in inputs.items()}
    n = 8
    shared = None
    in_maps = []
    for b in range(n):
        hi = host_inputs(inp, b) if shared is None else None
        if shared is None:
            shared = hi
            in_maps.append(hi)
        else:
            m = dict(shared)
            m["xT"] = np.ascontiguousarray(np.concatenate([inp["x"][b].T, inp["ctx"][b].T], axis=1), dtype=np.float32)
            cc = np.stack([inp["c"][b], inp["c_ctx"]], axis=-1)
            m["cc"] = np.ascontiguousarray(cc.reshape(16, 128, 2).transpose(1, 0, 2), dtype=np.float32)
            in_maps.append(m)
    if "nc" not in _PROG:
        nc, S = make_program(full_stages, set(INPUT_SHAPES.keys()), {"outT"})
        _PROG["nc"] = nc
    nc = _PROG["nc"]
    res = run_bass_kernel_spmd(nc, in_maps, core_ids=list(range(n)))
    out = np.stack([np.ascontiguousarray(res.results[b]["outT"].T) for b in range(n)], axis=0)
    return out.astype(np.float32)
```

```python
import numpy as np
from contextlib import ExitStack
import concourse.bass as bass
import concourse.mybir as mybir
from concourse.bass_utils import run_bass_kernel_spmd

F32 = mybir.dt.float32
BF16 = mybir.dt.bfloat16
ALU = mybir.AluOpType
AF = mybir.ActivationFunctionType
AX = mybir.AxisListType


class Buf:
    __slots__ = ("name", "last_w", "last_w_deps", "readers", "dsem", "dcnt", "is_psum")

    def __init__(self, name):
        self.name = name
        self.last_w = None
        self.last_w_deps = []
        self.readers = []
        self.dsem = None
        self.dcnt = 0
        self.is_psum = False


class V:
    __slots__ = ("ap", "buf")

    def __init__(self, ap, buf):
        self.ap = ap
        self.buf = buf

    def __getitem__(self, idx):
        return V(self.ap[idx], self.buf)

    def rearrange(self, s, **kw):
        return V(self.ap.rearrange(s, **kw), self.buf)

    def bc(self, shape):
        return V(self.ap.broadcast_to(list(shape)), self.buf)

    @property
    def shape(self):
        return self.ap.shape


class Tile:
    def __init__(self, handle, buf):
        self.h = handle
        self.buf = buf

    def __getitem__(self, idx):
        return V(self.h[idx], self.buf)

    @property
    def v(self):
        return V(self.h[:], self.buf)


class Op:
    __slots__ = ("eng", "fn", "deps", "signal", "dma_buf", "dma_cnt", "dma_sem", "idx", "val", "waits", "tag")

    def __init__(self, eng, fn):
        self.eng = eng
        self.fn = fn
        self.deps = []
        self.signal = False
        self.dma_buf = None
        self.dma_cnt = 0
        self.dma_sem = -1
        self.idx = -1
        self.val = 0
        self.waits = []


ENGS = ("pe", "act", "dve", "pool", "sp")


class Sched:
    def __init__(self, nc):
        self.nc = nc
        self.ops = {e: [] for e in ENGS}
        self.stack = ExitStack()
        self.bufs = []
        self.dma_pending = {}
        self.sems = {}
        self.nsem = 0
        self.sem_tot = []
        self.sem_free = []
        self.annotate = False
        self.stage = None

    def sbuf(self, name, shape, dtype=F32, stack=None):
        st = stack if stack is not None else self.stack
        h = st.enter_context(self.nc.sbuf_tensor(self._uniq("s_" + name), list(shape), dtype))
        b = Buf(name)
        self.bufs.append(b)
        return Tile(h, b)

    def psum(self, name, shape, dtype=F32, stack=None):
        st = stack if stack is not None else self.stack
        h = st.enter_context(self.nc.psum_tensor(self._uniq("q_" + name), list(shape), dtype))
        b = Buf(name)
        b.is_psum = True
        self.bufs.append(b)
        return Tile(h, b)

    def subviews(self, tile, n, psum=False):
        out = []
        for j in range(n):
            b = Buf(f"{tile.buf.name}_{j}")
            b.is_psum = psum
            self.bufs.append(b)
            out.append(V(tile.h[:, j], b))
        return out

    def _uniq(self, name):
        self.uid = getattr(self, "uid", 0) + 1
        return f"{name}_{self.uid}"

    def _sem(self, name):
        s = self.stack.enter_context(self.nc.semaphore(self._uniq(name)))
        self.nsem += 1
        return s

    def _add(self, eng, fn, reads, writes, dma_buf=None):
        op = Op(eng, fn)
        op.tag = getattr(self, "stage", None)
        op.idx = len(self.ops[eng])
        is_dma = dma_buf is not None
        deps = []
        raw = set()
        for b in reads:
            if b is None:
                continue
            if b.last_w is not None:
                deps.append(b.last_w)
                raw.add(id(b.last_w))
            if b.is_psum:
                for rd_ in b.readers:
                    if rd_.eng != eng:
                        deps.append(rd_)
        wdeps = {}
        for b in writes:
            if b is None:
                continue
            mine = []
            if b.last_w is not None:
                lw = b.last_w
                if is_dma and lw.dma_buf is not None:
                    mine.extend(b.last_w_deps)
                else:
                    mine.append(lw)
            mine.extend(b.readers)
            wdeps[id(b)] = mine
            deps.extend(mine)
        out = []
        seen = set()
        for d in deps:
            if id(d) in seen or d is op:
                continue
            seen.add(id(d))
            if d.dma_buf is None and d.eng == eng and not is_dma:
                if eng == "pe" or id(d) not in raw:
                    continue
            out.append(d)
        op.deps = out
        if is_dma:
            op.dma_buf = dma_buf
            if dma_buf.dsem is None:
                if self.sem_free:
                    dma_buf.dsem = self.sem_free.pop()
                else:
                    dma_buf.dsem = len(self.sem_tot)
                    self.sem_tot.append(0)
            self.sem_tot[dma_buf.dsem] += 16
            op.dma_sem = dma_buf.dsem
            op.dma_cnt = self.sem_tot[dma_buf.dsem]
            self.dma_pending[id(dma_buf)] = op
        for b in reads:
            if b is not None:
                b.readers.append(op)
        for b in writes:
            if b is not None:
                b.last_w = op
                b.last_w_deps = wdeps[id(b)]
                b.readers = []
        self.ops[eng].append(op)
        return op

    def barrier(self):
        lasts = []
        for e in ENGS:
            for o in reversed(self.ops[e]):
                if o.dma_buf is None and o.fn is not None:
                    lasts.append(o)
                    break
        pend = list(self.dma_pending.values())
        for e in ENGS:
            op = Op(e, None)
            op.idx = len(self.ops[e])
            op.deps = list(lasts) + pend
            self.ops[e].append(op)
        self.dma_pending = {}
        for b in self.bufs:
            b.last_w = None
            b.last_w_deps = []
            b.readers = []
            if b.dsem is not None:
                self.sem_free.append(b.dsem)
                b.dsem = None

    @staticmethod
    def _bufs(*vs):
        return [v.buf for v in vs if isinstance(v, V)]

    @staticmethod
    def _a(v):
        return v.ap if isinstance(v, V) else v

    def matmul(self, out, lhsT, rhs, start=True, stop=True):
        o, l, r = out.ap, lhsT.ap, rhs.ap
        rd = self._bufs(lhsT, rhs)
        if not start:
            rd = rd + [out.buf]
        return self._add("pe", lambda e: e.matmul(o, lhsT=l, rhs=r, start=start, stop=stop), rd, [out.buf])

    def transpose(self, out, in_, ident):
        o, i, d = out.ap, in_.ap, ident.ap
        return self._add("pe", lambda e: e.transpose(o, i, d), self._bufs(in_, ident), [out.buf])

    def act(self, out, in_, func, bias=0.0, scale=1.0, accum=None):
        o, i, b, s = out.ap, in_.ap, self._a(bias), self._a(scale)
        ac = accum.ap if accum is not None else None
        w = [out.buf] + ([accum.buf] if accum is not None else [])
        if ac is None:
            fn = lambda e: e.activation(o, i, func, bias=b, scale=s)
        else:
            fn = lambda e: e.activation(o, i, func, bias=b, scale=s, accum_out=ac)
        return self._add("act", fn, self._bufs(in_, bias, scale), w)

    def tt(self, eng, out, a, b, op):
        o, x, y = out.ap, a.ap, b.ap
        return self._add(eng, lambda e: e.tensor_tensor(o, x, y, op), self._bufs(a, b), [out.buf])

    def ts(self, eng, out, a, s1, op0, s2=None, op1=None, accum=None):
        o, x, p, q = out.ap, a.ap, self._a(s1), self._a(s2)
        ac = accum.ap if accum is not None else None
        w = [out.buf] + ([accum.buf] if accum is not None else [])
        kw = {}
        if op1 is not None:
            kw["op1"] = op1
        if ac is not None:
            kw["accum_out"] = ac
        return self._add(eng, lambda e: e.tensor_scalar(o, x, p, q, op0, **kw), self._bufs(a, s1, s2), w)

    def stt(self, eng, out, in0, scalar, in1, op0, op1):
        o, x, s, y = out.ap, in0.ap, self._a(scalar), in1.ap
        return self._add(eng, lambda e: e.scalar_tensor_tensor(o, x, s, y, op0, op1), self._bufs(in0, scalar, in1), [out.buf])

    def copy(self, eng, out, in_):
        o, i = out.ap, in_.ap
        if eng == "act":
            return self._add("act", lambda e: e.copy(o, i), [in_.buf], [out.buf])
        return self._add(eng, lambda e: e.tensor_copy(o, i), [in_.buf], [out.buf])

    def memset(self, eng, out, val):
        o = out.ap
        return self._add(eng, lambda e: e.memset(o, val), [], [out.buf])

    def reduce(self, eng, out, in_, op, axis=AX.X):
        o, i = out.ap, in_.ap
        return self._add(eng, lambda e: e.tensor_reduce(o, i, axis, op), [in_.buf], [out.buf])

    def recip(self, out, in_):
        o, i = out.ap, in_.ap
        return self._add("dve", lambda e: e.reciprocal(o, i), [in_.buf], [out.buf])

    def dma(self, q, out, in_, extra_reads=()):
        o = out.ap if isinstance(out, V) else out
        i = in_.ap if isinstance(in_, V) else in_
        ob = out.buf if isinstance(out, V) else None
        ib = in_.buf if isinstance(in_, V) else None
        sb = ob if ob is not None else ib
        assert sb is not None
        reads = ([ib] if ib is not None else []) + list(extra_reads)
        writes = [ob] if ob is not None else []
        return self._add(q, lambda e: e.dma_start(out=o, in_=i), reads, writes, dma_buf=sb)

    def emit(self):
        nc = self.nc
        for e in ENGS:
            for op in self.ops[e]:
                for d in op.deps:
                    d.signal = True
        esem = {}
        for e in ENGS:
            cnt = 0
            for op in self.ops[e]:
                if op.dma_buf is None and op.signal and op.fn is not None:
                    cnt += 1
                    op.val = cnt
                elif op.dma_buf is None and op.fn is None and op.signal:
                    raise RuntimeError("barrier op signalled")
            if cnt:
                esem[e] = self._sem("e_" + e)
        dsems = [self._sem(f"d{i}") for i in range(len(self.sem_tot))]
        nwait = 0
        for e in ENGS:
            seen = {}
            for op in self.ops[e]:
                ws = {}
                for d in op.deps:
                    if d.dma_buf is not None:
                        key = ("d", d.dma_sem)
                        sem, val = dsems[d.dma_sem], d.dma_cnt
                    else:
                        key = ("e", d.eng)
                        sem, val = esem[d.eng], d.val
                    if seen.get(key, 0) >= val:
                        continue
                    if key not in ws or ws[key][1] < val:
                        ws[key] = (sem, val)
                for key, (sem, val) in ws.items():
                    seen[key] = val
                op.waits = list(ws.values())
                nwait += len(op.waits)
        self.stats = {e: len(self.ops[e]) for e in ENGS}
        self.stats["waits"] = nwait
        self.stats["sems"] = self.nsem
        engmap = {"pe": "tensor", "act": "scalar", "dve": "vector", "pool": "gpsimd", "sp": "sync"}
        with nc.Block() as block:
            for e in ENGS:
                ops = self.ops[e]
                if not ops:
                    continue
                sem_e = esem.get(e)

                def body(eng, ops=ops, sem_e=sem_e, dsems=dsems, self=self):
                    for op in ops:
                        for (sem, val) in op.waits:
                            eng.wait_ge(sem, val)
                        if op.fn is None:
                            continue
                        ins = op.fn(eng)
                        if self.annotate and op.tag:
                            ins.annotate(op.tag)
                        if op.dma_buf is not None:
                            ins.then_inc(dsems[op.dma_sem], 16)
                        elif op.signal:
                            ins.then_inc(sem_e, 1)
                getattr(block, engmap[e])(body)


D_MODEL = 2048
NLAT = 2048
NCTX = 256
T = NLAT + NCTX
D_FF = 5632
NORM_EPS = 1e-6
KD = 16


class Ctx:
    pass


def chunks(total, size):
    return [(i, min(size, total - i)) for i in range(0, total, size)]


def fm(ap):
    return ap.rearrange("(k p) n -> p k n", p=128)


def init_common(C):
    S = C.S
    C.ps = [S.psum(f"ps{i}", [128, 512]) for i in range(8)]
    C.psi = 0
    C.ident = S.sbuf("ident", [128, 128])
    C.ones = S.sbuf("ones", [128, 128])
    S.dma("sp", C.ident.v, C.d["ident"])
    S.memset("dve", C.ones.v, 1.0)
    C.mod = S.sbuf("mod", [128, 96, 2])
    C.g1s = S.sbuf("g1s", [128, 16, 2])
    C.g2s = S.sbuf("g2s", [128, 16, 2])
    C.dq = 0


def nextps(C):
    p = C.ps[C.psi % 8]
    C.psi += 1
    return p


def ldq(C):
    C.dq += 1
    return ("sp", "pool")[C.dq % 2]


def stage_mod(C, l):
    S = C.S
    with ExitStack() as st:
        wbuf = [S.sbuf(f"adaw{i}", [128, 16, 512], BF16, stack=st) for i in range(3)]
        cc = S.sbuf("cc", [128, 16, 2], stack=st)
        sc = S.sbuf("sc", [128, 16, 2], BF16, stack=st)
        adab = S.sbuf("adab", [128, 96], stack=st)
        ng = S.sbuf("ng", [128, 2, 16], stack=st)
        S.dma("sp", cc.v, C.d["cc"])
        S.dma("sp", adab.v, C.d["ada_b"][l])
        S.dma("sp", ng[:, 0, :], C.d["norm1_g"][l])
        S.dma("sp", ng[:, 1, :], C.d["norm2_g"][l])
        S.act(sc.v, cc.v, AF.Silu)
        W = fm(C.d["ada_w"][l])
        for fb in range(24):
            w = wbuf[fb % 3]
            S.dma("pool", w.v, W[:, :, fb * 512:(fb + 1) * 512])
            p = nextps(C)
            for j in range(4):
                for k in range(16):
                    S.matmul(p[:, j * 2:(j + 1) * 2], w[:, k, j * 128:(j + 1) * 128], sc[:, k, :],
                             start=(k == 0), stop=(k == 15))
            for col in range(2):
                S.tt("dve", C.mod[:, fb * 4:(fb + 1) * 4, col],
                     p[:, 0:8].rearrange("p (j c) -> p j c", c=2)[:, :, col],
                     adab[:, fb * 4:(fb + 1) * 4], ALU.add)
        for col in range(2):
            S.stt("dve", C.g1s[:, :, col], C.mod[:, 16:32, col], 1.0, ng[:, 0, :], ALU.add, ALU.mult)
            S.stt("dve", C.g2s[:, :, col], C.mod[:, 64:80, col], 1.0, ng[:, 1, :], ALU.add, ALU.mult)
    S.barrier()


def stage_norm(C, which, src, dst, col_chunks, final_g=None):
    S = C.S
    srcv, dstv = fm(src), fm(dst)
    with ExitStack() as st:
        xb = [S.sbuf(f"nx{i}", [128, 16, 256], stack=st) for i in range(2)]
        sqb = [S.sbuf(f"nsq{i}", [128, 16, 256], stack=st) for i in range(2)]
        hb = [S.sbuf(f"nh{i}", [128, 16, 256], F32 if which == 0 else BF16, stack=st) for i in range(2)]
        hf = S.sbuf("nhf", [128, 16, 256], stack=st) if which != 0 else None
        rs = [S.sbuf(f"nr{i}", [128, 256], stack=st) for i in range(2)]
        def stats(ci):
            c0, w, col = col_chunks[ci]
            x, r = xb[ci % 2], rs[ci % 2]
            sq = sqb[ci % 2]
            S.dma(("sp", "act")[ci % 2], x[:, :, 0:w], srcv[:, :, c0:c0 + w])
            S.act(sq[:, :, 0:w], x[:, :, 0:w], AF.Square)
            p = nextps(C)
            for k in range(16):
                S.matmul(p[:, 0:w], C.ones.v, sq[:, k, 0:w], start=(k == 0), stop=(k == 15))
            S.ts("dve", r[:, 0:w], p[:, 0:w], 1.0 / D_MODEL, ALU.mult, NORM_EPS, ALU.add)
            S.act(r[:, 0:w], r[:, 0:w], AF.Ln)
            S.act(r[:, 0:w], r[:, 0:w], AF.Exp, scale=-0.5)

        def apply(ci):
            c0, w, col = col_chunks[ci]
            x, h, r = xb[ci % 2], hb[ci % 2], rs[ci % 2]
            for k in range(16):
                if which == 0:
                    S.stt("dve", h[:, k, 0:w], x[:, k, 0:w], final_g[:, k:k + 1], r[:, 0:w], ALU.mult, ALU.mult)
                else:
                    gs = C.g1s if which == 1 else C.g2s
                    sh0 = 0 if which == 1 else 48
                    S.stt("dve", hf[:, k, 0:w], x[:, k, 0:w], gs[:, k, col:col + 1], r[:, 0:w], ALU.mult, ALU.mult)
                    S.act(h[:, k, 0:w], hf[:, k, 0:w], AF.Identity, bias=C.mod[:, sh0 + k, col:col + 1])
            S.dma("pool", dstv[:, :, c0:c0 + w], h[:, :, 0:w])

        n_ = len(col_chunks)
        stats(0)
        for ci in range(n_):
            if ci + 1 < n_:
                stats(ci + 1)
            apply(ci)
    S.barrier()


def norm_chunks(with_ctx=True):
    cs = [(c0, w, 0) for (c0, w) in chunks(NLAT, 256)]
    if with_ctx:
        cs += [(NLAT, 256, 1)]
    return cs


def seg_chunks(g0, gw, size=512):
    out = []
    a = g0
    end = g0 + gw
    while a < end:
        b = min(a + size, end)
        if a < NLAT < b:
            b = NLAT
        out.append((a - g0, b - a))
        a = b
    return out


def gemm(C, inT, K, W, tok_groups, blocks, epi, group_hook=None, wblock=None, resident=None):
    S = C.S
    nk = K // 128
    if resident is None:
        resident = {}
    res = [resident.get(g, g) for g in tok_groups]
    gwmax = max(rw for _, rw in res)
    nwmax = max(nw for _, nw, _ in blocks)
    Wv = fm(W) if W is not None else None
    inv = fm(inT)
    with ExitStack() as st:
        act = S.sbuf("g_act", [128, nk, gwmax], BF16, stack=st)
        NWB = 3
        wb = [S.sbuf(f"g_w{i}", [128, nk, nwmax], BF16, stack=st) for i in range(NWB)]
        bi_glob = 0
        for gi, (gg0, ggw) in enumerate(tok_groups):
            g0, gw = res[gi]
            kq = max(1, nk // 4)
            for k0 in range(0, nk, kq):
                k1 = min(nk, k0 + kq)
                S.dma(("sp", "act")[(k0 // kq) % 2], act[:, k0:k1, 0:gw], inv[:, k0:k1, g0:g0 + gw])
            if group_hook is not None:
                group_hook("begin", g0, gw)
            for (n0, nw, mode) in blocks:
                wt = wb[bi_glob % NWB]
                bi_glob += 1
                wsrc = wblock(n0, nw) if wblock is not None else Wv[:, :, n0:n0 + nw]
                S.dma("pool", wt[:, :, 0:nw], wsrc)
                if mode == "T":
                    for t0 in range(0, gw, 128):
                        ps = nextps(C)
                        for k in range(nk):
                            S.matmul(ps[:, 0:nw], act[:, k, t0:t0 + 128], wt[:, k, 0:nw],
                                     start=(k == 0), stop=(k == nk - 1))
                        epi("T", ps, n0, nw, g0 + t0, 128, g0, gw)
                else:
                    for f0 in range(0, nw, 128):
                        for (t0, tw) in seg_chunks(g0, gw):
                            ps = nextps(C)
                            for k in range(nk):
                                S.matmul(ps[:, 0:tw], wt[:, k, f0:f0 + 128], act[:, k, t0:t0 + tw],
                                         start=(k == 0), stop=(k == nk - 1))
                            epi("F", ps, n0 + f0, 128, g0 + t0, tw, g0, gw)
            if group_hook is not None:
                group_hook("end", g0, gw)
    S.barrier()


class Stager:
    def __init__(self, C, st, name, n=4, width=512):
        self.C = C
        self.bufs = [C.S.sbuf(f"{name}{i}", [128, width], stack=st) for i in range(n)]
        self.i = 0

    def next(self):
        b = self.bufs[self.i % len(self.bufs)]
        self.i += 1
        return b

    def evac(self, dst_view, src_view):
        S = self.C.S
        if self.i % 2 == 0:
            S.copy("act", dst_view, src_view)
        else:
            S.copy("dve", dst_view, src_view)


def stage_win(C, l):
    S = C.S
    d = C.d
    blocks = []
    outmap = []

    def addT(n0, n1, dram):
        for (o, w) in chunks(n1 - n0, 512):
            blocks.append((n0 + o, w, "T"))
        outmap.append((n0, n1, "T", dram))

    def addF(n0, n1, dram):
        for (o, w) in chunks(n1 - n0, 512):
            blocks.append((n0 + o, w, "F"))
        outmap.append((n0, n1, "F", dram))

    addT(0, 1920, d["p_rw"])
    addT(1920, 3472, d["p_ssd"])
    addF(3472, 4496, d["da_qkT"])
    addT(4496, 5008, d["da_v"])
    addF(5008, 6032, d["na_qkT"])
    addT(6032, 6544, d["na_v"])

    def find(n0):
        for (a, b, m, dr) in outmap:
            if a <= n0 < b:
                return a, dr
        raise KeyError

    with ExitStack() as st:
        sg = Stager(C, st, "wst", 4)

        def epi(mode, ps, n0, nw, c0, cw, g0, gw):
            a, dr = find(n0)
            sb = sg.next()
            if mode == "T":
                sg.evac(sb[:, 0:nw], ps[:, 0:nw])
                S.dma("act" if sg.i % 2 else "sp", dr[c0:c0 + 128, n0 - a:n0 - a + nw], sb[:, 0:nw])
            else:
                sg.evac(sb[:, 0:cw], ps[:, 0:cw])
                S.dma("act" if sg.i % 2 else "sp", dr[n0 - a:n0 - a + 128, c0:c0 + cw], sb[:, 0:cw])

        gemm(C, d["hT"], D_MODEL, d["w_in"][l], [(0, T)], blocks, epi)


def stage_merge(C, l, with_ctx):
    S = C.S
    d = C.d
    Tt = T if with_ctx else NLAT
    NG = 2
    gw = Tt // NG
    hv = fm(d["hT"])
    bv = d["brT"].rearrange("n (k p) t -> p (n k) t", p=128)
    wg = d["w_gate"][l].rearrange("n (k p) m -> n p k m", p=128)
    wbr = d["w_br"][l].rearrange("n (k p) m -> n p k m", p=128)
    mgv = fm(d["mgT"])
    CW = 384
    with ExitStack() as st:
        hact = S.sbuf("m_h", [128, 16, gw], BF16, stack=st)
        bact = S.sbuf("m_b", [128, 16, gw], BF16, stack=st)
        wgb = [S.sbuf(f"m_wg{i}", [128, 16, 256], BF16, stack=st) for i in range(3)]
        wbb = [S.sbuf(f"m_wb{i}", [128, 4, 256], BF16, stack=st) for i in range(3)]
        gb = S.sbuf("m_gb", [128, 4, 16], stack=st)
        S.dma("sp", gb.v, d["gate_b"][l])
        sig = [S.sbuf(f"m_sig{i}", [128, CW], stack=st) for i in range(2)]
        tmp = [S.sbuf(f"m_tmp{i}", [128, CW], stack=st) for i in range(2)]
        acc = [S.sbuf(f"m_acc{i}", [128, 2, gw], stack=st) for i in range(2)]
        aout = [S.sbuf(f"m_ao{i}", [128, 2, gw], BF16, stack=st) for i in range(2)]
        wi = 0
        ai = 0
        si = 0
        for g in range(NG):
            g0 = g * gw
            for k0 in range(0, 16, 4):
                S.dma("sp", hact[:, k0:k0 + 4, :], hv[:, k0:k0 + 4, g0:g0 + gw])
                S.dma("act", bact[:, k0:k0 + 4, :], bv[:, k0:k0 + 4, g0:g0 + gw])
            tch = seg_chunks(g0, gw, CW)
            for db in range(8):
                a = acc[ai % 2]
                ao = aout[ai % 2]
                ai += 1
                for n in range(4):
                    wgt, wbt = wgb[wi % 3], wbb[wi % 3]
                    wi += 1
                    S.dma("pool", wgt.v, wg[n][:, :, db * 256:(db + 1) * 256])
                    S.dma("pool", wbt.v, wbr[n][:, :, db * 256:(db + 1) * 256])
                    for dc in range(2):
                        for (t0, tw) in tch:
                            pg = nextps(C)
                            for k in range(16):
                                S.matmul(pg[:, 0:tw], wgt[:, k, dc * 128:(dc + 1) * 128], hact[:, k, t0:t0 + tw],
                                         start=(k == 0), stop=(k == 15))
                            pb = nextps(C)
                            for k in range(4):
                                S.matmul(pb[:, 0:tw], wbt[:, k, dc * 128:(dc + 1) * 128], bact[:, n * 4 + k, t0:t0 + tw],
                                         start=(k == 0), stop=(k == 3))
                            sg = sig[si % 2]
                            tp = tmp[si % 2]
                            si += 1
                            S.act(sg[:, 0:tw], pg[:, 0:tw], AF.Sigmoid, bias=gb[:, n, db * 2 + dc:db * 2 + dc + 1])
                            if n == 0:
                                S.tt("dve", a[:, dc, t0:t0 + tw], sg[:, 0:tw], pb[:, 0:tw], ALU.mult)
                            else:
                                S.tt("dve", tp[:, 0:tw], sg[:, 0:tw], pb[:, 0:tw], ALU.mult)
                                dst = ao if n == 3 else a
                                S.tt("dve", dst[:, dc, t0:t0 + tw], a[:, dc, t0:t0 + tw], tp[:, 0:tw], ALU.add)
                S.dma("sp", mgv[:, db * 2:db * 2 + 2, g0:g0 + gw], ao.v)
    S.barrier()


def resid_epi(C, st, gate0, xsrc, xdram):
    S = C.S
    xt = [S.sbuf(f"r_x{i}", [128, 512], stack=st) for i in range(4)]
    cnt = [0]

    def epi(mode, ps, n0, nw, c0, cw, g0, gw):
        assert mode == "F"
        x = xt[cnt[0] % 4]
        cnt[0] += 1
        col = 0 if c0 < NLAT else 1
        k = n0 // 128
        S.dma("sp", x[:, 0:cw], xsrc[n0:n0 + 128, c0:c0 + cw])
        S.stt("dve", x[:, 0:cw], ps[:, 0:cw], C.mod[:, gate0 + k, col:col + 1], x[:, 0:cw], ALU.mult, ALU.add)
        S.dma("act", xdram[n0:n0 + 128, c0:c0 + cw], x[:, 0:cw])
    return epi


def stage_wout(C, l, with_ctx):
    d = C.d
    xsrc = d["xT"] if l == 0 else d["xres"]
    Tt = T if with_ctx else NLAT
    groups = [(0, Tt)]
    blocks = [(n0, 512, "F") for n0 in range(0, D_MODEL, 512)]
    with ExitStack() as st:
        epi = resid_epi(C, st, 32, xsrc, d["xres"])
        gemm(C, d["mgT"], D_MODEL, d["w_out"][l], groups, blocks, epi)


def stage_ffn_up(C, l, with_ctx):
    S = C.S
    d = C.d
    Tt = T if with_ctx else NLAT
    half = Tt
    groups = [(0, Tt)]
    resident = {}
    blocks = [(j * 256, 256, "F") for j in range(44)]
    guv = d["guT"]
    with ExitStack() as st:
        cw_t = S.sbuf("f_cw", [128, 3, 88], stack=st)
        cb_t = S.sbuf("f_cb", [128, 88], stack=st)
        S.dma("sp", cw_t.v, d["ffn_conv_w"][l])
        S.dma("sp", cb_t.v, d["ffn_conv_b"][l])
        RW = Tt
        ub = [[S.sbuf(f"f_u{i}{j}", [128, RW], stack=st) for j in range(2)] for i in range(2)]
        yb = [[S.sbuf(f"f_y{i}{j}", [128, RW], stack=st) for j in range(2)] for i in range(2)]
        gob = [S.sbuf(f"f_go{i}", [128, RW], BF16, stack=st) for i in range(2)]
        state = {"i": 0}

        def epi(mode, ps, n0, nw, c0, cw, g0, gw):
            j = n0 // 256
            isval = (n0 % 256) // 128
            u = ub[j % 2][isval]
            S.copy("act" if isval else "dve", u[:, c0 - g0:c0 - g0 + cw], ps[:, 0:cw])
            last_chunk = (c0 + cw == g0 + gw)
            if not (isval and last_chunk):
                return
            o0 = g0
            o1 = g0 + gw
            pieces = []
            a = o0
            while a < o1:
                b = o1
                if a < NLAT < b:
                    b = NLAT
                pieces.append((a, b))
                a = b
            for (a, b) in pieces:
                seg0, seg1 = (0, NLAT) if a < NLAT else (NLAT, T)
                if not with_ctx:
                    seg0, seg1 = 0, NLAT
                for v_ in range(2):
                    u = ub[j % 2][v_]
                    y = yb[j % 2][v_]
                    ch = j + 44 * v_
                    la, lb = a - g0, b - g0
                    S.act(y[:, la:lb], u[:, la:lb], AF.Identity, bias=cb_t[:, ch:ch + 1], scale=cw_t[:, 1, ch:ch + 1])
                    lo = la + 1 if a == seg0 else la
                    S.stt("dve", y[:, lo:lb], u[:, lo - 1:lb - 1], cw_t[:, 0, ch:ch + 1], y[:, lo:lb], ALU.mult, ALU.add)
                    hi = lb - 1 if b == seg1 else lb
                    S.stt("dve", y[:, la:hi], u[:, la + 1:hi + 1], cw_t[:, 2, ch:ch + 1], y[:, la:hi], ALU.mult, ALU.add)
                yg, yv = yb[j % 2][0], yb[j % 2][1]
                la, lb = a - g0, b - g0
                S.act(ub[j % 2][0][:, la:lb], yg[:, la:lb], AF.Silu)
                S.tt("dve", gob[j % 2][:, la:lb], yv[:, la:lb], ub[j % 2][0][:, la:lb], ALU.mult)
                S.dma("sp", guv[j * 128:(j + 1) * 128, a:b], gob[j % 2][:, la:lb])

        gemm(C, d["hT"], D_MODEL, d["ffn_up"][l], groups, blocks, epi, resident=resident)


def stage_ffn_down(C, l, with_ctx):
    d = C.d
    Tt = T if with_ctx else NLAT
    gw = Tt // 2
    groups = [(g * gw, gw) for g in range(2)]
    blocks = [(n0, 128, "F") for n0 in range(0, D_MODEL, 128)]
    wv = d["ffn_down"][l]

    def wblock(n0, nw):
        return wv[n0 // 128]
    with ExitStack() as st:
        epi = resid_epi(C, st, 80, d["xres"], d["xres"])
        gemm(C, d["guT"], D_FF, None, groups, blocks, epi, wblock=wblock)

import math

DA_SUBLN_EPS = 1e-5


def load_bcast(S, q, tile_v, dram_ap, n=128):
    return S.dma(q, tile_v, dram_ap.partition_broadcast(n))


def stage_da(C, l, ctx_out):
    S = C.S
    d = C.d
    lam_init = 0.8 - 0.6 * math.exp(-0.3 * l)
    qk = d["da_qkT"]
    vd = d["da_v"]
    out = d["brT"][2]
    acc = C.ps[0:4]
    rot = C.ps[4:7]
    misc = C.ps[7]
    NR = 3
    with ExitStack() as st:
        cos = S.sbuf("da_cos", [128, NLAT], stack=st)
        sin = S.sbuf("da_sin", [128, NLAT], stack=st)
        perm = S.sbuf("da_perm", [128, 128], stack=st)
        S.dma("sp", cos.v, d["rope_cos"])
        S.dma("act", sin.v, d["rope_sin"])
        S.dma("sp", perm.v, d["rope_perm"])
        lp = S.sbuf("da_lp", [128, 4, 64], stack=st)
        load_bcast(S, "sp", lp.v, d["da_lambda"][l])
        pr = S.sbuf("da_pr", [128, 2, 64], stack=st)
        sm = S.sbuf("da_sm", [128, 2], stack=st)
        S.tt("dve", pr[:, 0, :], lp[:, 0, :], lp[:, 1, :], ALU.mult)
        S.tt("dve", pr[:, 1, :], lp[:, 2, :], lp[:, 3, :], ALU.mult)
        S.reduce("dve", sm.v, pr.v, ALU.add)
        S.act(sm.v, sm.v, AF.Exp)
        nlam = S.sbuf("da_nlam", [128, 1], stack=st)
        S.tt("dve", nlam.v, sm[:, 1:2], sm[:, 0:1], ALU.subtract)
        S.ts("dve", nlam.v, nlam.v, -lam_init, ALU.add)
        gsub = S.sbuf("da_g", [128, 1], stack=st)
        S.dma("sp", gsub.v, d["da_subln_g"][l])
        S.ts("dve", gsub.v, gsub.v, 1.0 - lam_init, ALU.mult)

        qT = S.sbuf("da_q", [128, T], stack=st)
        kT = S.sbuf("da_k", [128, T], stack=st)
        qB = S.sbuf("da_qb", [128, T], BF16, stack=st)
        kB = S.sbuf("da_kb", [128, T], BF16, stack=st)
        kZ = [S.sbuf(f"da_kz{m}", [128, T], BF16, stack=st) for m in range(2)]
        S.memset("dve", kZ[0].v, 0.0)
        S.memset("dve", kZ[1].v, 0.0)
        vt = S.sbuf("da_vt", [128, 18, 128], BF16, stack=st)
        onesb = S.sbuf("da_1b", [128, 128], BF16, stack=st)
        S.memset("pool", onesb.v, 1.0)
        tmp = [S.sbuf(f"da_tmp{i}", [128, 512], stack=st) for i in range(2)]
        et = [S.sbuf(f"da_e{i}", [128, 512], BF16, stack=st) for i in range(5)]
        rr = [S.sbuf(f"da_r{i}", [128, 512], stack=st) for i in range(2)]
        oo = [S.sbuf(f"da_o{i}", [128, 512], stack=st) for i in range(2)]
        sq = S.sbuf("da_sq", [128, 512], stack=st)
        es = S.sbuf("da_es", [128, 512], stack=st)
        esb = S.sbuf("da_esb", [128, 512], BF16, stack=st)
        rs = S.sbuf("da_rs", [128, 512], stack=st)
        ob = [S.sbuf(f"da_ob{i}", [128, 512], BF16, stack=st) for i in range(2)]
        ei = 0
        oi = 0
        for h in range(4):
            S.dma("sp", qT.v, qk[h * 128:(h + 1) * 128, :])
            S.dma("act", kT.v, qk[512 + h * 128:512 + (h + 1) * 128, :])
            S.dma("pool", vt.v, vd[:, h * 128:(h + 1) * 128].rearrange("(b p) e -> p b e", p=128))
            ti = 0
            for X, XB in ((qT, qB), (kT, kB)):
                for c0 in range(0, NLAT, 512):
                    S.matmul(misc.v, perm.v, X[:, c0:c0 + 512])
                    t = tmp[ti % 2]
                    ti += 1
                    S.tt("dve", t.v, misc.v, sin[:, c0:c0 + 512], ALU.mult)
                    S.tt("dve", X[:, c0:c0 + 512], X[:, c0:c0 + 512], cos[:, c0:c0 + 512], ALU.mult)
                    S.tt("dve", XB[:, c0:c0 + 512], X[:, c0:c0 + 512], t.v, ALU.add)
                S.copy("act", XB[:, NLAT:T], X[:, NLAT:T])
            S.copy("act", kZ[0][0:64, :], kB[0:64, :])
            S.copy("dve", kZ[1][64:128, :], kB[64:128, :])
            qchunks = [(c0, 512, list(range(18))) for c0 in range(0, NLAT, 512)]
            if ctx_out:
                qchunks.append((NLAT, 256, [16, 17]))
            items = []
            for (c0, cw, kbs) in qchunks:
                for m in range(2):
                    for i, kb in enumerate(kbs):
                        items.append((c0, cw, m, i, kb, len(kbs)))
            slots = {}

            def emitA(n):
                c0, cw, m, i, kb, nk_ = items[n]
                ps = rot[n % NR]
                e = et[n % NR]
                S.matmul(ps[:, 0:cw], kZ[m][:, kb * 128:(kb + 1) * 128], qB[:, c0:c0 + cw])
                S.act(e[:, 0:cw], ps[:, 0:cw], AF.Exp, scale=0.125)

            def emitB(n):
                nonlocal oi
                c0, cw, m, i, kb, nk_ = items[n]
                e = et[n % NR]
                OT, SM = acc[m], acc[2 + m]
                S.matmul(OT[:, 0:cw], vt[:, kb, :], e[:, 0:cw], start=(i == 0), stop=(i == nk_ - 1))
                S.matmul(SM[:, 0:cw], onesb.v, e[:, 0:cw], start=(i == 0), stop=(i == nk_ - 1))
                if i != nk_ - 1:
                    return
                S.recip(rr[m][:, 0:cw], SM[:, 0:cw])
                S.tt("dve", oo[m][:, 0:cw], OT[:, 0:cw], rr[m][:, 0:cw], ALU.mult)
                if m != 1:
                    return
                o = oo[0]
                S.stt("dve", o[:, 0:cw], oo[1][:, 0:cw], nlam.v, o[:, 0:cw], ALU.mult, ALU.add)
                S.act(sq[:, 0:cw], o[:, 0:cw], AF.Square)
                S.matmul(misc[:, 0:cw], C.ones.v, sq[:, 0:cw])
                S.ts("dve", rs[:, 0:cw], misc[:, 0:cw], 1.0 / 128, ALU.mult, DA_SUBLN_EPS, ALU.add)
                S.act(rs[:, 0:cw], rs[:, 0:cw], AF.Ln)
                S.act(rs[:, 0:cw], rs[:, 0:cw], AF.Exp, scale=-0.5)
                b = ob[oi % 2]
                oi += 1
                S.stt("dve", b[:, 0:cw], o[:, 0:cw], gsub.v, rs[:, 0:cw], ALU.mult, ALU.mult)
                S.dma("sp", out[h * 128:(h + 1) * 128, c0:c0 + cw], b[:, 0:cw])

            LA = 2
            for n in range(len(items) + LA):
                if n < len(items):
                    emitA(n)
                if n - LA >= 0:
                    emitB(n - LA)
    S.barrier()


def rope_tables():
    GRID_W = 64
    t = np.arange(NLAT)
    pos = np.stack([t // GRID_W, t % GRID_W], axis=-1).astype(np.float32)
    n_freq = 16
    inv = (np.float32(10000.0) ** (-np.arange(n_freq, dtype=np.float32) / np.float32(n_freq))).astype(np.float32)
    ang = (pos[:, :, None] * inv).astype(np.float32)
    cosv, sinv = np.cos(ang).astype(np.float32), np.sin(ang).astype(np.float32)
    cos = np.zeros((128, NLAT), np.float32)
    sin = np.zeros((128, NLAT), np.float32)
    perm = np.zeros((128, 128), np.float32)
    for p in range(128):
        dd = p % 64
        half, j, f = dd // 32, (dd % 32) // 16, dd % 16
        cos[p] = cosv[:, half, f]
        if j == 0:
            sin[p] = -sinv[:, half, f]
            partner = p + 16
        else:
            sin[p] = sinv[:, half, f]
            partner = p - 16
        perm[partner, p] = 1.0
    return cos, sin, perm


NT = T // 128
SEG_FIRST = {0: True, 16: True}
SEG_LAST = {15: True, 17: True}


def bc_last(v, n):
    shp = list(v.ap.shape)
    return V(v.ap.unsqueeze(len(shp)).broadcast_to(shp + [n]), v.buf)


def load_shifted(S, st_tiles, src, t0, width, c0, first, last):
    xm, x0, xp = st_tiles
    S.dma("sp", x0[:, 0:width], src[t0:t0 + 128, c0:c0 + width])
    if first:
        S.memset("pool", xm[0:32, 0:width], 0.0)
        S.dma("pool", xm[1:128, 0:width], src[t0:t0 + 127, c0:c0 + width])
    else:
        S.dma("pool", xm[:, 0:width], src[t0 - 1:t0 + 127, c0:c0 + width])
    if last:
        S.memset("pool", xp[96:128, 0:width], 0.0)
        S.dma("act", xp[0:127, 0:width], src[t0 + 1:t0 + 128, c0:c0 + width])
    else:
        S.dma("act", xp[:, 0:width], src[t0 + 1:t0 + 129, c0:c0 + width])


def stage_ssd(C, l, ctx_out):
    S = C.S
    d = C.d
    src = d["p_ssd"]
    out = d["brT"][1].rearrange("(k p) t -> p k t", p=128)
    P = C.ps
    with ExitStack() as st:
        triF = S.sbuf("tr_f", [128, 128], stack=st)
        triB = S.sbuf("tr_b", [128, 128], stack=st)
        S.dma("sp", triF.v, d["triF"])
        S.dma("sp", triB.v, d["triB"])
        dtb = S.sbuf("s_dtb", [128, 16], stack=st)
        load_bcast(S, "sp", dtb.v, d["ssd_dt_bias"][l])
        negA = S.sbuf("s_negA", [128, 16], stack=st)
        load_bcast(S, "sp", negA.v, d["ssd_a_log"][l])
        S.act(negA.v, negA.v, AF.Exp)
        S.ts("dve", negA.v, negA.v, -1.0, ALU.mult)
        dsk = S.sbuf("s_dsk", [128, 8], stack=st)
        load_bcast(S, "sp", dsk.v, d["ssd_d"][l])
        ngb = S.sbuf("s_ng", [128, 512], stack=st)
        load_bcast(S, "pool", ngb.v, d["ssd_norm_g"][l])

        xall = S.sbuf("s_xall", [128, NT, 768], stack=st)
        bct = S.sbuf("s_bct", [128, NT, 4, 128], stack=st)
        dta = S.sbuf("s_dta", [128, NT, 2, 16], stack=st)
        H = S.sbuf("s_H", [128, 2, 2, 256], stack=st)
        S.memset("dve", H.v, 0.0)
        st2 = ExitStack()
        cw = S.sbuf("s_cw", [128, 4, 1024], stack=st2)
        load_bcast(S, "sp", cw[:, 0:3, :], d["ssd_conv_w"][l])
        load_bcast(S, "pool", cw[:, 3, :], d["ssd_conv_b"][l])
        sh = [[S.sbuf(f"s_sh{i}{j}", [128, 1024], stack=st2) for j in range(3)] for i in range(2)]
        ta = [S.sbuf(f"s_ta{i}", [128, 1024], stack=st2) for i in range(2)]
        tb = [S.sbuf(f"s_tb{i}", [128, 1024], stack=st2) for i in range(2)]
        dtr = [S.sbuf(f"s_dtr{i}", [128, 16], stack=st2) for i in range(2)]
        for ti in range(NT):
            t0 = ti * 128
            tl = sh[ti % 2]
            load_shifted(S, tl, src, t0, 1024, 512, ti in SEG_FIRST, ti in SEG_LAST)
            a, b = ta[ti % 2], tb[ti % 2]
            S.tt("dve", a.v, tl[0].v, cw[:, 0, :], ALU.mult)
            S.tt("dve", b.v, tl[1].v, cw[:, 1, :], ALU.mult)
            S.tt("dve", a.v, a.v, b.v, ALU.add)
            S.tt("dve", b.v, tl[2].v, cw[:, 2, :], ALU.mult)
            S.tt("dve", a.v, a.v, b.v, ALU.add)
            S.tt("dve", a.v, a.v, cw[:, 3, :], ALU.add)
            S.act(b.v, a.v, AF.Silu)
            S.copy("act", xall[:, ti, :], b[:, 0:768])
            r = dtr[ti % 2]
            S.dma("sp", r.v, src[t0:t0 + 128, 1536:1552])
            S.tt("dve", r.v, r.v, dtb.v, ALU.add)
            S.act(r.v, r.v, AF.Exp)
            S.act(dta[:, ti, 0, :], r.v, AF.Ln, bias=1.0)
            S.tt("dve", dta[:, ti, 1, :], dta[:, ti, 0, :], negA.v, ALU.mult)
            for q in range(4):
                S.transpose(P[0][:, q * 128:(q + 1) * 128], b[:, 512 + q * 128:512 + (q + 1) * 128], C.ident.v)
            S.copy("dve", bct[:, ti, :, :], P[0].v.rearrange("p (q t) -> p q t", q=4))

        S.barrier()
        st2.close()
        yacc = S.sbuf("s_yacc", [128, NT, 512], stack=st)
        ct = [S.sbuf(f"s_ct{i}", [128, 16], stack=st) for i in range(2)]
        te = [S.sbuf(f"s_te{i}", [128, 8], stack=st) for i in range(2)]
        sc = [S.sbuf(f"s_sc{i}", [128, 8], stack=st) for i in range(2)]
        ee = [S.sbuf(f"s_ee{i}", [128, 16], stack=st) for i in range(2)]
        xs = [S.sbuf(f"s_xs{i}", [128, 512], stack=st) for i in range(2)]
        sm = [S.sbuf(f"s_sm{i}", [128, 2, 128], stack=st) for i in range(2)]
        atr4 = [S.sbuf(f"s_atr{i}", [128, 4, 128], stack=st) for i in range(2)]
        sg4 = [S.sbuf(f"s_sg{i}", [128, 4, 128], stack=st) for i in range(2)]
        mt4 = [S.sbuf(f"s_mt{i}", [128, 4, 128], stack=st) for i in range(2)]

        def bc_mid4(v):
            return V(v.ap.unsqueeze(1).broadcast_to([128, 4, 128]), v.buf)
        yt = [S.sbuf(f"s_yt{i}", [128, 512], stack=st) for i in range(2)]
        it = 0
        ih = 0
        for dr in range(2):
            tri = triF if dr == 0 else triB
            order = [16, 17] + list(range(16)) if dr == 0 else [17, 16] + list(range(15, -1, -1))
            for ti in order:
                k = it % 2
                it += 1
                dt = dta[:, ti, 0, dr * 8:(dr + 1) * 8]
                aa = dta[:, ti, 1, dr * 8:(dr + 1) * 8]
                X = xall[:, ti, 0:512]
                Bm = xall[:, ti, 512:768]
                S.matmul(P[1][:, 0:8], tri.v, aa)
                S.matmul(P[1][:, 8:16], C.ones.v, aa)
                S.copy("dve", ct[k].v, P[1][:, 0:16])
                cum, tot = ct[k][:, 0:8], ct[k][:, 8:16]
                S.tt("dve", te[k].v, tot, cum, ALU.subtract)
                S.act(te[k].v, te[k].v, AF.Exp)
                S.tt("dve", sc[k].v, te[k].v, dt, ALU.mult)
                S.act(ee[k].v, ct[k].v, AF.Exp)
                S.tt("dve", xs[k].v.rearrange("p (h e) -> p h e", h=8), X.rearrange("p (h e) -> p h e", h=8),
                     bc_last(sc[k].v, 64), ALU.mult)
                for g in range(2):
                    S.matmul(P[3][:, g * 256:(g + 1) * 256], bct[:, ti, 2 + g, :], H[:, dr, g, :])
                    S.matmul(P[5][:, g * 128:(g + 1) * 128], bct[:, ti, g, :], bct[:, ti, 2 + g, :])
                for g in range(2):
                    S.tt("dve", sm[k][:, g, :], P[5][:, g * 128:(g + 1) * 128], tri.v, ALU.mult)
                for g in range(2):
                    S.matmul(P[2][:, g * 256:(g + 1) * 256], Bm[:, g * 128:(g + 1) * 128], xs[k][:, g * 256:(g + 1) * 256])
                for g in range(2):
                    Hg = H[:, dr, g, :].rearrange("p (h e) -> p h e", h=4)
                    S.tt("dve", Hg, Hg, bc_last(ee[k][:, 8 + g * 4:8 + (g + 1) * 4], 64), ALU.mult)
                    S.tt("dve", H[:, dr, g, :], H[:, dr, g, :], P[2][:, g * 256:(g + 1) * 256], ALU.add)
                for g in range(2):
                    j = ih % 2
                    pb = P[6 + ih % 2]
                    ih += 1
                    hs = slice(g * 4, (g + 1) * 4)
                    S.tt("dve", atr4[j].v, bc_mid4(tri.v), bc_last(aa[:, hs], 128), ALU.mult)
                    S.matmul(pb.v, C.ones.v, atr4[j].v.rearrange("p h i -> p (h i)"))
                    S.tt("dve", sg4[j].v, pb.v.rearrange("p (h i) -> p h i", h=4), bc_last(cum[:, hs], 128), ALU.subtract)
                    S.ts("dve", sg4[j].v, sg4[j].v, 0.0, ALU.min)
                    S.act(sg4[j].v, sg4[j].v, AF.Exp)
                    S.tt("dve", sg4[j].v, sg4[j].v, bc_last(dt[:, hs], 128), ALU.mult)
                    S.tt("dve", mt4[j].v, sg4[j].v, bc_mid4(sm[k][:, g, :]), ALU.mult)
                    for hh in range(4):
                        h = g * 4 + hh
                        S.matmul(P[4][:, h * 64:(h + 1) * 64], mt4[j][:, hh, :], X[:, h * 64:(h + 1) * 64])
                y = yt[k]
                S.tt("dve", y.v.rearrange("p (h e) -> p h e", h=8), P[3].v.rearrange("p (h e) -> p h e", h=8),
                     bc_last(ee[k][:, 0:8], 64), ALU.mult)
                if dr == 0:
                    S.tt("dve", yacc[:, ti, :], y.v, P[4].v, ALU.add)
                else:
                    S.tt("dve", y.v, y.v, P[4].v, ALU.add)
                    S.tt("dve", yacc[:, ti, :], yacc[:, ti, :], y.v, ALU.add)
        zt = [S.sbuf(f"s_z{i}", [128, 512], stack=st) for i in range(2)]
        y2 = [S.sbuf(f"s_y2{i}", [128, 512], stack=st) for i in range(2)]
        ssq = [S.sbuf(f"s_ssq{i}", [128, 1], stack=st) for i in range(2)]
        junk = S.sbuf("s_junk", [128, 512], stack=st)
        ot = [S.sbuf(f"s_ot{i}", [128, 4, 128], BF16, stack=st) for i in range(2)]
        tiles = list(range(NT)) if ctx_out else list(range(16))
        for n, ti in enumerate(tiles):
            k = n % 2
            t0 = ti * 128
            S.dma("sp", zt[k].v, src[t0:t0 + 128, 0:512])
            S.act(zt[k].v, zt[k].v, AF.Silu)
            y = y2[k]
            S.tt("dve", y.v.rearrange("p (h e) -> p h e", h=8), xall[:, ti, 0:512].rearrange("p (h e) -> p h e", h=8),
                 bc_last(dsk.v, 64), ALU.mult)
            S.tt("dve", y.v, y.v, yacc[:, ti, :], ALU.add)
            S.tt("dve", y.v, y.v, zt[k].v, ALU.mult)
            S.memset("pool", ssq[k].v, 0.0)
            S.act(junk.v, y.v, AF.Square, accum=ssq[k].v)
            S.ts("dve", ssq[k].v, ssq[k].v, 1.0 / 512, ALU.mult, NORM_EPS, ALU.add)
            S.act(ssq[k].v, ssq[k].v, AF.Ln)
            S.act(ssq[k].v, ssq[k].v, AF.Exp, scale=-0.5)
            S.stt("dve", y.v, y.v, ssq[k].v, ngb.v, ALU.mult, ALU.mult)
            for q in range(4):
                S.transpose(P[0][:, q * 128:(q + 1) * 128], y[:, q * 128:(q + 1) * 128], C.ident.v)
            S.copy("act", ot[k].v, P[0].v.rearrange("p (q t) -> p q t", q=4))
            S.dma("pool", out[:, :, t0:t0 + 128], ot[k].v)
    S.barrier()


def tri_consts():
    tf = np.triu(np.ones((128, 128), np.float32))
    return tf, np.ascontiguousarray(tf.T)


GRID_W = 64
NROWS = 32


def na_tables(rpb):
    kc = np.arange(64)[:, None]
    c = np.arange(64)[None, :]
    ci = np.clip(kc - c, -15, 15) + 15
    col_start = np.clip(np.arange(64) - 8, 0, 48)
    in_win = (kc >= col_start[None, :]) & (kc < col_start[None, :] + 16)
    dr0 = np.arange(14)
    wl = np.arange(2)
    ri = dr0[None, :] + wl[:, None]
    b = rpb[:, :, ri[:, None, :, None], ci[None, :, None, :]]
    b = np.ascontiguousarray(b.reshape(rpb.shape[0], 8, 128, 14, 64), dtype=np.float32)
    m = np.broadcast_to(in_win[None, :, None, :], (2, 64, 14, 64)).reshape(128, 14 * 64)
    return b, np.ascontiguousarray(m, dtype=np.float32)


def stage_na(C, l, ctx_out):
    S = C.S
    d = C.d
    qk = d["na_qkT"]
    vd = d["na_v"]
    out = d["brT"][3]
    P = C.ps
    with ExitStack() as st:
        mask = S.sbuf("na_mask", [128, 14 * 64], stack=st)
        S.dma("sp", mask.v, d["na_mask"])
        qT = S.sbuf("na_q", [128, T], stack=st)
        kT = S.sbuf("na_k", [128, T], stack=st)
        qB = S.sbuf("na_qb", [128, T], BF16, stack=st)
        kB = S.sbuf("na_kb", [128, T], BF16, stack=st)
        ve = S.sbuf("na_ve", [128, 18, 128], BF16, stack=st)
        vo = S.sbuf("na_vo", [128, 15, 128], BF16, stack=st)
        kZ = [S.sbuf(f"na_kz{m}", [128, T], BF16, stack=st) for m in range(2)]
        S.memset("dve", kZ[0].v, 0.0)
        S.memset("dve", kZ[1].v, 0.0)
        ones64 = S.sbuf("na_1b", [128, 64], BF16, stack=st)
        S.memset("pool", ones64.v, 1.0)
        eb = [S.sbuf(f"na_eb{i}", [128, 14 * 64], stack=st) for i in range(2)]
        et = [S.sbuf(f"na_e{i}", [128, 384], BF16, stack=st) for i in range(3)]
        ef = [S.sbuf(f"na_ef{i}", [128, 256], stack=st) for i in range(3)]
        ec = [S.sbuf(f"na_ec{i}", [128, 256], BF16, stack=st) for i in range(2)]
        rr = [S.sbuf(f"na_r{i}", [64, 256], stack=st) for i in range(2)]
        ob = [S.sbuf(f"na_ob{i}", [64, T], BF16, stack=st) for i in range(2)]
        ei = 0
        ri_ = 0
        for hp in range(4):
            S.dma("sp", qT.v, qk[hp * 128:(hp + 1) * 128, :])
            S.dma("act", kT.v, qk[512 + hp * 128:512 + (hp + 1) * 128, :])
            S.dma("pool", ve.v, vd[:, hp * 128:(hp + 1) * 128].rearrange("(b p) e -> p b e", p=128))
            S.dma("pool", vo.v, vd[64:64 + 15 * 128, hp * 128:(hp + 1) * 128].rearrange("(b p) e -> p b e", p=128))
            S.copy("dve", qB.v, qT.v)
            S.copy("act", kB.v, kT.v)
            S.copy("act", kZ[0][0:64, :], kT[0:64, :])
            S.copy("dve", kZ[1][64:128, :], kT[64:128, :])
            for hh in range(2):
                h = hp * 2 + hh
                ebt = eb[h % 2]
                o = ob[h % 2]
                S.dma("sp", ebt.v, d["na_bias"][l][h].rearrange("p a c -> p (a c)"))
                S.act(ebt.v, ebt.v, AF.Exp)
                S.tt("dve", ebt.v, ebt.v, mask.v, ALU.mult)
                ebv = ebt.v.rearrange("p (a c) -> p a c", c=64)
                pl, ph = hh * 64, (hh + 1) * 64
                def emitA(r):
                    rs = min(max(r - 4, 0), NROWS - 8)
                    q = qB[:, r * 64:(r + 1) * 64]
                    ps = P[4 + r % 4]
                    e = et[r % 3]
                    f = ef[r % 3]
                    for i, w0 in enumerate((0, 2, 4, 6)):
                        kr = rs + w0
                        S.matmul(ps[:, i * 64:(i + 1) * 64], kZ[hh][:, kr * 64:kr * 64 + 128], q)
                    for cb in range(2):
                        S.matmul(ps[:, 256 + cb * 64:256 + (cb + 1) * 64], kZ[hh][:, NLAT + cb * 128:NLAT + (cb + 1) * 128], q)
                    dr0 = rs - r + 7
                    S.act(f.v, ps[:, 0:256], AF.Exp, scale=0.125)
                    S.act(e[:, 256:384], ps[:, 256:384], AF.Exp, scale=0.125)
                    S.tt("dve", e[:, 0:256].rearrange("p (a c) -> p a c", c=64), f.v.rearrange("p (a c) -> p a c", c=64),
                         ebv[:, dr0:dr0 + 7:2, :], ALU.mult)

                def emitB(r):
                    nonlocal ri_
                    rs = min(max(r - 4, 0), NROWS - 8)
                    acc = P[r % 2]
                    accs = P[2 + r % 2]
                    e = et[r % 3]
                    vts = []
                    for w0 in (0, 2, 4, 6):
                        kr = rs + w0
                        vts.append(ve[:, kr // 2, pl:ph] if kr % 2 == 0 else vo[:, (kr - 1) // 2, pl:ph])
                    for cb in range(2):
                        vts.append(ve[:, 16 + cb, pl:ph])
                    for i in range(6):
                        S.matmul(acc[0:64, 0:64], vts[i], e[:, i * 64:(i + 1) * 64], start=(i == 0), stop=(i == 5))
                    for i in range(6):
                        S.matmul(accs[0:64, 0:64], ones64.v, e[:, i * 64:(i + 1) * 64], start=(i == 0), stop=(i == 5))
                    rt = rr[ri_ % 2]
                    ri_ += 1
                    S.recip(rt[:, 0:64], accs[0:64, 0:64])
                    S.tt("dve", o[:, r * 64:(r + 1) * 64], acc[0:64, 0:64], rt[:, 0:64], ALU.mult)

                LA = 2
                for r in range(NROWS + LA):
                    if r < NROWS:
                        emitA(r)
                    if r - LA >= 0:
                        emitB(r - LA)
                if ctx_out:
                    acc = P[0]
                    accs = P[2]
                    q = qB[pl:ph, NLAT:T]
                    for cb in range(2):
                        ps = P[4 + ei % 4]
                        ei += 1
                        e = ec[cb]
                        S.matmul(ps[:, 0:256], kB[pl:ph, NLAT + cb * 128:NLAT + (cb + 1) * 128], q)
                        S.act(e.v, ps[:, 0:256], AF.Exp, scale=0.125)
                    for cb in range(2):
                        S.matmul(acc[0:64, 0:256], ve[:, 16 + cb, pl:ph], ec[cb].v, start=(cb == 0), stop=(cb == 1))
                    for cb in range(2):
                        S.matmul(accs[0:64, 0:256], ones64.v, ec[cb].v, start=(cb == 0), stop=(cb == 1))
                    rt = rr[ri_ % 2]
                    ri_ += 1
                    S.recip(rt.v, accs[0:64, 0:256])
                    S.tt("dve", o[:, NLAT:T], acc[0:64, 0:256], rt.v, ALU.mult)
                    S.dma("sp", out[h * 64:(h + 1) * 64, :], o.v)
                else:
                    S.dma("sp", out[h * 64:(h + 1) * 64, 0:NLAT], o[:, 0:NLAT])
    S.barrier()

import os
RW_STOP = os.environ.get('RW_STOP', '')
RW_NT = int(os.environ.get('RW_NT', '99'))
RW_LP = BF16 if os.environ.get('RW_LP', 'f32') == 'bf16' else F32

RW_GN_EPS = 64e-5
EXPM05 = 0.6065306597126334


def rw_consts():
    s = np.arange(128)[:, None]
    t = np.arange(128)[None, :]
    f = np.float32
    US = (s < t).astype(f)
    UF = (s <= t).astype(f)
    LS = (s > t).astype(f)
    LF = (s >= t).astype(f)
    I_ = np.eye(128, dtype=f)
    masks = np.stack([np.tile(m_, (1, 4)) for m_ in (US, UF, LS, LF, -US, -LS, I_)], 1)
    lo = (s <= 63).astype(f)
    hi = (s >= 64).astype(f)
    dq = np.stack([UF - lo, US - lo, LF - hi, LS - hi], 1)
    mvec = np.stack([np.concatenate([lo, hi], 1), np.concatenate([hi, lo], 1)], 1)
    return np.ascontiguousarray(masks), np.ascontiguousarray(dq), np.ascontiguousarray(mvec.astype(f))


class _OV:
    def __init__(self, views):
        self.views = views

    def __getitem__(self, idx):
        assert idx[0] == slice(None) and isinstance(idx[1], int)
        v = self.views[idx[1]]
        return v if idx[2] == slice(None) else v[:, idx[2]]


def rw_phase1(C, l):
    S = C.S
    d = C.d
    src = d["p_rw"]
    prep = d["rw_prep"]
    P = C.ps
    with ExitStack() as st:
        mu = S.sbuf("rw_mu", [128, 3, 1920], stack=st)
        load_bcast(S, "sp", mu[:, 0:2, :], d["rw_mu"][l])
        S.tt("dve", mu[:, 2, :], mu[:, 0, :], mu[:, 1, :], ALU.add)
        S.ts("dve", mu[:, 2, :], mu[:, 2, :], -1.0, ALU.mult, 1.0, ALU.add)
        w0b = S.sbuf("rw_w0b", [128, 2, 512], stack=st)
        a0b = S.sbuf("rw_a0b", [128, 2, 512], stack=st)
        load_bcast(S, "pool", w0b.v, d["rw_w0"][l])
        load_bcast(S, "pool", a0b.v, d["rw_a0"][l])
        kkb = S.sbuf("rw_kkb", [128, 512], stack=st)
        kab = S.sbuf("rw_kab", [128, 512], stack=st)
        rkb = S.sbuf("rw_rkb", [128, 512], stack=st)
        load_bcast(S, "sp", kkb.v, d["rw_k_k"][l])
        load_bcast(S, "sp", kab.v, d["rw_k_a"][l])
        load_bcast(S, "sp", rkb.v, d["rw_r_k"][l])
        wup = S.sbuf("rw_wup", [128, 512], stack=st)
        aup = S.sbuf("rw_aup", [128, 512], stack=st)
        gup = S.sbuf("rw_gup", [128, 512], stack=st)
        S.dma("sp", wup.v, d["rw_w_up"][l])
        S.dma("sp", aup.v, d["rw_a_up"][l])
        S.dma("sp", gup.v, d["rw_g_up"][l])
        sh = [[S.sbuf(f"rw_sh{i}{j}", [128, 1920], stack=st) for j in range(3)] for i in range(2)]
        sb = [S.sbuf(f"rw_s{i}", [128, 1920], stack=st) for i in range(2)]
        t1 = S.sbuf("rw_t1", [128, 1920], stack=st)
        th = S.sbuf("rw_th", [128, 2, 128], stack=st)
        thT = S.sbuf("rw_thT", [128, 3, 128], stack=st)
        ot = [S.sbuf(f"rw_ot{i}", [128, 11, 512], stack=st) for i in range(2)]
        otv = [S.subviews(t_, 11) for t_ in ot]
        wk = [S.sbuf(f"rw_wk{i}", [128, 512], stack=st) for i in range(8)]
        sm = [S.sbuf(f"rw_sm{i}", [128, 8], stack=st) for i in range(4)]

        def h8(v):
            return v.rearrange("p (h e) -> p h e", h=8)

        for ti in range(NT):
            t0 = ti * 128
            tl = sh[ti % 2]
            s = sb[ti % 2]
            oT = ot[ti % 2]
            o = _OV(otv[ti % 2])
            load_shifted(S, tl, src, t0, 1920, 0, ti in SEG_FIRST, ti in SEG_LAST)
            S.tt("dve", s.v, tl[1].v, mu[:, 2, :], ALU.mult)
            S.tt("dve", t1.v, tl[0].v, mu[:, 0, :], ALU.mult)
            S.tt("dve", s.v, s.v, t1.v, ALU.add)
            S.tt("dve", t1.v, tl[2].v, mu[:, 1, :], ALU.mult)
            S.tt("dve", s.v, s.v, t1.v, ALU.add)
            r, k, v = s[:, 0:512], s[:, 512:1024], s[:, 1024:1536]
            S.copy("act", o[:, 0, :], r)
            S.copy("act", o[:, 1, :], v)
            S.act(th[:, 0, :], s[:, 1536:1664], AF.Tanh)
            S.act(th[:, 1, :], s[:, 1792:1920], AF.Sigmoid)
            S.transpose(P[0][:, 0:128], th[:, 0, :], C.ident.v)
            S.transpose(P[0][:, 128:256], s[:, 1664:1792], C.ident.v)
            S.transpose(P[0][:, 256:384], th[:, 1, :], C.ident.v)
            S.copy("dve", thT.v, P[0][:, 0:384].rearrange("p (q t) -> p q t", q=3))
            a_t = [wk[0], wk[1]]
            for dr in range(2):
                pl, ph = dr * 64, (dr + 1) * 64
                S.matmul(P[1 + dr].v, thT[pl:ph, 0, :], wup[pl:ph, :])
                lw = o[:, 5 + 3 * dr, :]
                S.tt("dve", lw, P[1 + dr].v, w0b[:, dr, :], ALU.add)
                S.act(lw, lw, AF.Sigmoid)
                S.ts("dve", lw, lw, -EXPM05, ALU.mult)
                S.matmul(P[3 + dr].v, thT[pl:ph, 1, :], aup[pl:ph, :])
                S.tt("dve", a_t[dr].v, P[3 + dr].v, a0b[:, dr, :], ALU.add)
                S.act(a_t[dr].v, a_t[dr].v, AF.Sigmoid)
            S.matmul(P[5].v, thT[:, 2, :], gup.v)
            S.copy("act", o[:, 3, :], P[5].v)
            kk = o[:, 2, :]
            S.tt("dve", kk, k, kkb.v, ALU.mult)
            S.tt("dve", wk[2].v, kk, kk, ALU.mult)
            S.reduce("dve", sm[0].v, h8(wk[2].v), ALU.add)
            S.act(sm[0].v, sm[0].v, AF.Sqrt)
            S.ts("dve", sm[0].v, sm[0].v, 1e-12, ALU.max)
            S.recip(sm[0].v, sm[0].v)
            S.tt("dve", h8(kk), h8(kk), bc_last(sm[0].v, 64), ALU.mult)
            S.tt("dve", wk[3].v, r, rkb.v, ALU.mult)
            for dr in range(2):
                kd = o[:, 6 + 3 * dr, :]
                bb = o[:, 7 + 3 * dr, :]
                S.stt("dve", wk[4 + dr].v, a_t[dr].v, -1.0, kab.v, ALU.add, ALU.mult)
                S.stt("dve", kd, wk[4 + dr].v, 1.0, k, ALU.add, ALU.mult)
                S.tt("dve", bb, kk, a_t[dr].v, ALU.mult)
                S.tt("dve", wk[6 + dr].v, wk[3].v, kd, ALU.mult)
                S.reduce("dve", sm[1 + dr].v, h8(wk[6 + dr].v), ALU.add)
            S.tt("dve", sm[3].v, sm[1].v, sm[2].v, ALU.add)
            S.tt("dve", h8(o[:, 4, :]), h8(v), bc_last(sm[3].v, 64), ALU.mult)
            S.dma("act", prep[t0:t0 + 128, :, :], V(oT.h[:], oT.buf), extra_reads=[v_.buf for v_ in otv[ti % 2]])
    S.barrier()


def run_interleaved(gens):
    gens = list(gens)
    while gens:
        for g in list(gens):
            try:
                next(g)
            except StopIteration:
                gens.remove(g)


def rw_phase2(C, l):
    S = C.S
    d = C.d
    prep = d["rw_prep"]
    P = C.ps
    with ExitStack() as st:
        msk = S.sbuf("rw_msk", [128, 7, 512], stack=st)
        dqm = S.sbuf("rw_dq", [128, 4, 128], stack=st)
        mv = S.sbuf("rw_mv", [128, 2, 2], stack=st)
        S.dma("sp", msk.v, d["rw_masks"])
        S.dma("sp", dqm.v, d["rw_dqc"])
        S.dma("sp", mv.v, d["rw_mvec"])
        St = S.sbuf("rw_St", [128, 2, 4, 64], stack=st)
        S.memset("dve", St.v, 0.0)
        R_ = []
        for dr in range(2):
            def mk(nm, w=512, n=2, dt_=F32):
                return [S.sbuf(f"rw_{nm}{dr}{h}", [128, w], dt_, stack=st) for h in range(n)]
            LP = RW_LP
            res = dict(
                S0s=S.sbuf(f"rw_S0s{dr}", [128, 4, 64], stack=st),
                inp=S.sbuf(f"rw_in{dr}", [128, 6, 512], stack=st),
                ex=mk("ex", 512, 3), tm=mk("tm", 512, 4),
                tT=[S.sbuf(f"rw_tT{dr}{j}", [128, 4, 128], LP, stack=st) for j in range(4)],
                eh=S.sbuf(f"rw_eh{dr}", [128, 4, 2], stack=st),
                Q=[mk("Qa", dt_=LP), mk("Qb", dt_=LP)], R=[mk("Ra", dt_=LP), mk("Rb", dt_=LP)],
                Y=[mk("Ya", dt_=LP), mk("Yb", dt_=LP)],
                BmT=mk("BmT", dt_=LP), AbT=mk("AbT", dt_=LP), AkT=mk("AkT", dt_=LP), Wsb=mk("Wsb", 256, dt_=LP),
                Usb=S.sbuf(f"rw_U{dr}", [128, 4, 128], LP, stack=st),
                ob=S.sbuf(f"rw_ob{dr}", [128, 512], stack=st),
                lpc=S.sbuf(f"rw_lpc{dr}", [128, 3, 512], LP, stack=st),
                S0b=S.sbuf(f"rw_S0b{dr}", [128, 4, 64], LP, stack=st),
            )
            R_.append(res)
        bc_ = [0]

        def bank():
            b_ = P[4 + bc_[0] % 4]
            bc_[0] += 1
            return b_

        ev = [0]

        def evac_copy(dst, src):
            ev[0] += 1
            S.copy("act" if ev[0] % 2 else "dve", dst, src)

        def q4(v):
            return v.rearrange("p (q t) -> p q t", q=4)

        def hinfo(h):
            hp, hh = h // 2, h % 2
            return hp, hh, hh * 64, (hh + 1) * 64

        def dir_gen(dr):
            rs_ = R_[dr]
            PA, PB = P[2 * dr], P[2 * dr + 1]
            S0s, X, ex, eh, Usb = rs_["S0s"], rs_["inp"], rs_["ex"], rs_["eh"], rs_["Usb"]
            Q, R, Y, BmT, AbT, AkT, Wsb = (rs_[k_] for k_ in ("Q", "R", "Y", "BmT", "AbT", "AkT", "Wsb"))
            order = [16, 17] + list(range(16)) if dr == 0 else [17, 16] + list(range(15, -1, -1))
            if dr == 0:
                mS, mF, mSn, mAn = msk[:, 0, :], msk[:, 1, :], msk[:, 4, :], msk[:, 5, :]
            else:
                mS, mF, mSn, mAn = msk[:, 2, :], msk[:, 3, :], msk[:, 5, :], msk[:, 4, :]
            for ti in order[:RW_NT]:
                t0 = ti * 128
                S.dma("sp", X[:, 0:3, :], prep[t0:t0 + 128, 0:3, :])
                S.dma("act", X[:, 3:6, :], prep[t0:t0 + 128, 5 + 3 * dr:8 + 3 * dr, :])
                r_, v_, kk_, lw_, kd_, b_ = (X[:, j, :] for j in range(6))
                S.matmul(PA.v, dqm[:, 2 * dr, :], lw_)
                S.matmul(PB.v, dqm[:, 2 * dr + 1, :], lw_)
                S.act(ex[0].v, PA.v, AF.Exp)
                S.act(ex[1].v, PA.v, AF.Exp, scale=-1.0)
                S.act(ex[2].v, PB.v, AF.Exp)
                rq, kq, bn, kn = rs_["tm"]
                S.tt("dve", rq.v, r_, ex[0].v, ALU.mult)
                S.tt("dve", kq.v, kk_, ex[2].v, ALU.mult)
                S.tt("dve", bn.v, b_, ex[1].v, ALU.mult)
                S.tt("dve", kn.v, kd_, ex[1].v, ALU.mult)
                lpc = rs_["lpc"]
                S0b = rs_["S0b"]
                if RW_LP is F32:
                    vB, bnB, knB = v_, bn.v, kn.v
                else:
                    S.copy("act", lpc[:, 0, :], v_)
                    S.copy("act", lpc[:, 1, :], bn.v)
                    S.copy("act", lpc[:, 2, :], kn.v)
                    vB, bnB, knB = lpc[:, 0, :], lpc[:, 1, :], lpc[:, 2, :]
                yield
                rqT, kqT, bnT, knT = rs_["tT"]
                for j, (src_, dst_) in enumerate(((rq, rqT), (kq, kqT), (bn, bnT), (kn, knT))):
                    pb = (PA, PB)[j % 2]
                    for hp in range(4):
                        S.transpose(pb[:, hp * 128:(hp + 1) * 128], src_[:, hp * 128:(hp + 1) * 128], C.ident.v)
                    evac_copy(dst_.v, q4(pb.v))
                eg = bank()
                for hp in range(4):
                    S.matmul(eg[:, hp * 2:(hp + 1) * 2], lw_[:, hp * 128:(hp + 1) * 128], mv[:, dr, :])
                S.act(eh.v, eg[:, 0:8].rearrange("p (q c) -> p q c", c=2), AF.Exp)
                for hp in range(4):
                    S.ts("dve", S0s[:, hp, :], St[:, dr, hp, :], eh[:, hp, 0:1], ALU.mult)
                if RW_LP is F32:
                    S0m = S0s
                else:
                    S.copy("act", S0b.v, S0s.v)
                    S0m = S0b
                yield
                for half in range(2):
                    hs = [2 * i_ + half for i_ in range(4)]
                    specs = ((bnT, kqT, Q[0][half], mSn), (kqT, bnT, R[0][half], mAn), (knT, kqT, BmT[half], mS),
                             (bnT, rqT, AbT[half], mF), (knT, rqT, AkT[half], mF))
                    for (LT, RT, dst, mk_) in specs:
                        g = bank()
                        for i, h in enumerate(hs):
                            hp, hh, pl, ph = hinfo(h)
                            S.matmul(g[:, i * 128:(i + 1) * 128], LT[pl:ph, hp, :], RT[pl:ph, hp, :])
                        S.tt("dve", dst.v, g.v, mk_, ALU.mult)
                    S.tt("dve", Y[0][half].v, Q[0][half].v, msk[:, 6, :], ALU.add)
                    yield
                for half in range(2):
                    g = bank()
                    for i, h in enumerate([2 * i_ + half for i_ in range(4)]):
                        hp, hh, pl, ph = hinfo(h)
                        S.matmul(g[:, i * 64:(i + 1) * 64], kqT[pl:ph, hp, :], S0m[pl:ph, hp, :], start=True, stop=False)
                        S.matmul(g[:, i * 64:(i + 1) * 64], BmT[half][:, i * 128:(i + 1) * 128], vB[:, h * 64:(h + 1) * 64],
                                 start=False, stop=True)
                    evac_copy(Wsb[half].v, g[:, 0:256])
                yield
                cur = 0
                for lev in range(1, 7):
                    nxt = 1 - cur
                    for half in range(2):
                        if lev < 6:
                            g = bank()
                            for i in range(4):
                                sl = slice(i * 128, (i + 1) * 128)
                                S.matmul(g[:, sl], R[cur][half][:, sl], Q[cur][half][:, sl])
                            evac_copy(Q[nxt][half].v, g.v)
                        g = bank()
                        for i in range(4):
                            sl = slice(i * 128, (i + 1) * 128)
                            S.matmul(g[:, sl], Q[cur][half][:, sl], R[cur][half][:, sl])
                        evac_copy(R[nxt][half].v, g.v)
                    yield
                    for half in range(2):
                        g = bank()
                        for i in range(4):
                            sl = slice(i * 128, (i + 1) * 128)
                            S.matmul(g[:, sl], R[nxt][half][:, sl], Y[cur][half][:, sl])
                        S.tt("dve", Y[nxt][half].v, Y[cur][half].v, g.v, ALU.add)
                    cur = nxt
                    yield
                for half in range(2):
                    g = bank()
                    for i in range(4):
                        S.matmul(g[:, i * 64:(i + 1) * 64], Y[cur][half][:, i * 128:(i + 1) * 128], Wsb[half][:, i * 64:(i + 1) * 64])
                    S.ts("dve", Usb[:, :, half * 64:(half + 1) * 64], g[:, 0:256].rearrange("p (a b) -> p a b", a=4), -1.0, ALU.mult)
                yield
                for h in range(8):
                    hp, hh, pl, ph = hinfo(h)
                    half, i = h % 2, h // 2
                    oc = PB[:, h * 64:(h + 1) * 64]
                    S.matmul(oc, rqT[pl:ph, hp, :], S0m[pl:ph, hp, :], start=True, stop=False)
                    S.matmul(oc, AbT[half][:, i * 128:(i + 1) * 128], Usb[:, hp, hh * 64:(hh + 1) * 64], start=False, stop=False)
                    S.matmul(oc, AkT[half][:, i * 128:(i + 1) * 128], vB[:, h * 64:(h + 1) * 64], start=False, stop=True)
                S.copy("act", rs_["ob"].v, PB.v)
                S.dma("sp", d["rw_o"][dr, t0:t0 + 128, :], rs_["ob"].v)
                yield
                g = bank()
                for hp in range(4):
                    sl = slice(hp * 128, (hp + 1) * 128)
                    S.matmul(g[:, sl], bnB[:, sl], Usb[:, hp, :], start=True, stop=False)
                    S.matmul(g[:, sl], knB[:, sl], vB[:, sl], start=False, stop=True)
                for hp in range(4):
                    for hh in range(2):
                        pl, ph = hh * 64, (hh + 1) * 64
                        S.tt("dve", St[pl:ph, dr, hp, :], S0s[pl:ph, hp, :], g[pl:ph, hp * 128 + hh * 64:hp * 128 + (hh + 1) * 64], ALU.add)
                for hp in range(4):
                    S.ts("dve", St[:, dr, hp, :], St[:, dr, hp, :], eh[:, hp, 1:2], ALU.mult)
                yield

        run_interleaved([dir_gen(0), dir_gen(1)])
    S.barrier()
    with ExitStack() as st:
        rw_phase3(C, l, st, None)
    S.barrier()


def rw_phase3(C, l, st, oacc):
    S = C.S
    d = C.d
    prep = d["rw_prep"]
    out = d["brT"][0].rearrange("(k p) t -> p k t", p=128)
    P = C.ps
    lg = S.sbuf("rw_lg", [128, 2, 512], stack=st)
    load_bcast(S, "sp", lg[:, 0, :], d["rw_ln_g"][l])
    load_bcast(S, "sp", lg[:, 1, :], d["rw_ln_b"][l])
    gb = [S.sbuf(f"rw_gb{i}", [128, 2, 512], stack=st) for i in range(2)]
    cen = [S.sbuf(f"rw_cen{i}", [128, 512], stack=st) for i in range(2)]
    sq = S.sbuf("rw_sq3", [128, 512], stack=st)
    mn = [S.sbuf(f"rw_mn{i}", [128, 8], stack=st) for i in range(2)]
    vr = [S.sbuf(f"rw_vr{i}", [128, 8], stack=st) for i in range(2)]
    ot = [S.sbuf(f"rw_o3{i}", [128, 4, 128], BF16, stack=st) for i in range(2)]

    def h8(v):
        return v.rearrange("p (h e) -> p h e", h=8)

    tiles = C.rw_tiles
    of = [S.sbuf(f"rw_of{i}", [128, 2, 512], stack=st) for i in range(2)]
    for n, ti in enumerate(tiles):
        k = n % 2
        t0 = ti * 128
        S.dma("sp", gb[k].v, prep[t0:t0 + 128, 3:5, :])
        S.dma("act", of[k][:, 0, :], d["rw_o"][0, t0:t0 + 128, :])
        S.dma("act", of[k][:, 1, :], d["rw_o"][1, t0:t0 + 128, :])
        S.tt("dve", of[k][:, 0, :], of[k][:, 0, :], of[k][:, 1, :], ALU.add)
        o = of[k][:, 0, :]
        S.reduce("dve", mn[k].v, h8(o), ALU.add)
        S.ts("dve", mn[k].v, mn[k].v, 1.0 / 64, ALU.mult)
        c = cen[k]
        S.tt("dve", h8(c.v), h8(o), bc_last(mn[k].v, 64), ALU.subtract)
        S.act(sq.v, c.v, AF.Square)
        S.reduce("dve", vr[k].v, h8(sq.v), ALU.add)
        S.ts("dve", vr[k].v, vr[k].v, 1.0 / 64, ALU.mult, RW_GN_EPS, ALU.add)
        S.act(vr[k].v, vr[k].v, AF.Ln)
        S.act(vr[k].v, vr[k].v, AF.Exp, scale=-0.5)
        S.tt("dve", h8(c.v), h8(c.v), bc_last(vr[k].v, 64), ALU.mult)
        S.tt("dve", c.v, c.v, lg[:, 0, :], ALU.mult)
        S.tt("dve", c.v, c.v, lg[:, 1, :], ALU.add)
        S.tt("dve", c.v, c.v, gb[k][:, 1, :], ALU.add)
        S.tt("dve", c.v, c.v, gb[k][:, 0, :], ALU.mult)
        for q in range(4):
            S.transpose(P[0][:, q * 128:(q + 1) * 128], c[:, q * 128:(q + 1) * 128], C.ident.v)
        S.copy("act", ot[k].v, P[0].v.rearrange("p (q t) -> p q t", q=4))
        S.dma("sp", out[:, :, t0:t0 + 128], ot[k].v)


def stage_rwkv(C, l, ctx_out):
    C.rw_tiles = list(range(NT)) if ctx_out else list(range(16))
    rw_phase1(C, l)
    rw_phase2(C, l)


DEPTH = 2
IN_TOTAL = 6544

INPUT_SHAPES = {
    "xT": [D_MODEL, T],
    "cc": [128, 16, 2],
    "ident": [128, 128],
    "ada_w": [DEPTH, D_MODEL, 6 * D_MODEL],
    "ada_b": [DEPTH, 128, 96],
    "norm1_g": [DEPTH, 128, 16],
    "norm2_g": [DEPTH, 128, 16],
    "w_in": [DEPTH, D_MODEL, IN_TOTAL],
    "w_gate": [DEPTH, 4, D_MODEL, D_MODEL],
    "gate_b": [DEPTH, 128, 4, 16],
    "w_br": [DEPTH, 4, 512, D_MODEL],
    "w_out": [DEPTH, D_MODEL, D_MODEL],
    "ffn_up": [DEPTH, D_MODEL, 2 * D_FF],
    "ffn_conv_w": [DEPTH, 128, 3, 88],
    "ffn_conv_b": [DEPTH, 128, 88],
    "ffn_down": [DEPTH, 16, 128, 44, 128],
    "final_norm_g": [128, 16],
    "rope_cos": [128, NLAT],
    "rope_sin": [128, NLAT],
    "rope_perm": [128, 128],
    "triF": [128, 128],
    "triB": [128, 128],
    "ssd_conv_w": [DEPTH, 3, 1024],
    "ssd_conv_b": [DEPTH, 1024],
    "ssd_dt_bias": [DEPTH, 16],
    "ssd_a_log": [DEPTH, 16],
    "ssd_d": [DEPTH, 8],
    "ssd_norm_g": [DEPTH, 512],
    "rw_mu": [DEPTH, 2, 1920],
    "rw_w0": [DEPTH, 2, 512],
    "rw_a0": [DEPTH, 2, 512],
    "rw_k_k": [DEPTH, 512],
    "rw_k_a": [DEPTH, 512],
    "rw_r_k": [DEPTH, 512],
    "rw_w_up": [DEPTH, 128, 512],
    "rw_a_up": [DEPTH, 128, 512],
    "rw_g_up": [DEPTH, 128, 512],
    "rw_ln_g": [DEPTH, 512],
    "rw_ln_b": [DEPTH, 512],
    "rw_masks": [128, 7, 512],
    "rw_dqc": [128, 4, 128],
    "rw_mvec": [128, 2, 2],
    "na_bias": [DEPTH, 8, 128, 14, 64],
    "na_mask": [128, 14 * 64],
    "da_lambda": [DEPTH, 4, 64],
    "da_subln_g": [DEPTH, 128, 1],
}

SCRATCH_SHAPES = {
    "hT": [D_MODEL, T],
    "p_rw": [T, 1920],
    "p_ssd": [T, 1552],
    "da_qkT": [1024, T],
    "da_v": [T, 512],
    "na_qkT": [1024, T],
    "na_v": [T, 512],
    "modout": [128, 192],
    "rw_prep": [T, 11, 512],
    "rw_o": [2, T, 512],
    "brT": [4, 512, T],
    "mgT": [D_MODEL, T],
    "guT": [D_FF, T],
    "xres": [D_MODEL, T],
    "outT": [D_MODEL, NLAT],
}


ANNOTATE = False
BF16_SCRATCH = {"hT", "brT", "mgT", "guT"}


def host_inputs(inp, b):
    f = np.float32
    o = {}
    o["xT"] = np.ascontiguousarray(np.concatenate([inp["x"][b].T, inp["ctx"][b].T], axis=1), dtype=f)
    cc = np.stack([inp["c"][b], inp["c_ctx"]], axis=-1)
    o["cc"] = np.ascontiguousarray(cc.reshape(16, 128, 2).transpose(1, 0, 2), dtype=f)
    o["ident"] = np.eye(128, dtype=f)
    o["ada_w"] = inp["ada_w"]
    o["ada_b"] = np.ascontiguousarray(inp["ada_b"].reshape(DEPTH, 96, 128).transpose(0, 2, 1), dtype=f)
    o["norm1_g"] = np.ascontiguousarray(inp["norm1_g"].reshape(DEPTH, 16, 128).transpose(0, 2, 1), dtype=f)
    o["norm2_g"] = np.ascontiguousarray(inp["norm2_g"].reshape(DEPTH, 16, 128).transpose(0, 2, 1), dtype=f)
    o["w_in"] = inp["w_in"]
    o["w_gate"] = inp["w_gate"]
    o["gate_b"] = np.ascontiguousarray(inp["gate_b"].reshape(DEPTH, 4, 16, 128).transpose(0, 3, 1, 2), dtype=f)
    o["w_br"] = inp["w_br"]
    o["w_out"] = inp["w_out"]
    fu = inp["ffn_up"].reshape(DEPTH, D_MODEL, 2, 44, 128).transpose(0, 1, 3, 2, 4)
    o["ffn_up"] = np.ascontiguousarray(fu.reshape(DEPTH, D_MODEL, 2 * D_FF), dtype=f)
    o["ffn_conv_w"] = np.ascontiguousarray(inp["ffn_conv_w"].reshape(DEPTH, 3, 88, 128).transpose(0, 3, 1, 2), dtype=f)
    o["ffn_conv_b"] = np.ascontiguousarray(inp["ffn_conv_b"].reshape(DEPTH, 88, 128).transpose(0, 2, 1), dtype=f)
    o["ffn_down"] = np.ascontiguousarray(inp["ffn_down"].reshape(DEPTH, 44, 128, 16, 128).transpose(0, 3, 2, 1, 4), dtype=f)
    o["rope_cos"], o["rope_sin"], o["rope_perm"] = rope_tables()
    o["triF"], o["triB"] = tri_consts()
    o["ssd_conv_w"] = inp["ssd_conv_w"]
    o["ssd_conv_b"] = inp["ssd_conv_b"]
    o["ssd_dt_bias"] = np.ascontiguousarray(inp["ssd_dt_bias"].reshape(DEPTH, 16), dtype=f)
    o["ssd_a_log"] = np.ascontiguousarray(inp["ssd_a_log"].reshape(DEPTH, 16), dtype=f)
    o["ssd_d"] = inp["ssd_d"]
    o["ssd_norm_g"] = inp["ssd_norm_g"]
    for k_ in ["rw_mu", "rw_w0", "rw_a0", "rw_k_k", "rw_k_a", "rw_ln_g", "rw_ln_b"]:
        o[k_] = inp[k_]
    o["rw_r_k"] = np.ascontiguousarray(inp["rw_r_k"].reshape(DEPTH, 512), dtype=f)
    o["rw_w_up"] = np.ascontiguousarray(inp["rw_w_up"].reshape(DEPTH, 128, 512), dtype=f)
    o["rw_a_up"] = np.ascontiguousarray(inp["rw_a_up"].reshape(DEPTH, 128, 512), dtype=f)
    o["rw_g_up"] = inp["rw_g_up"]
    o["rw_masks"], o["rw_dqc"], o["rw_mvec"] = rw_consts()
    o["na_bias"], o["na_mask"] = na_tables(inp["na_rpb"])
    o["da_lambda"] = inp["da_lambda"]
    o["da_subln_g"] = np.ascontiguousarray(inp["da_subln_g"].reshape(DEPTH, 128, 1), dtype=f)
    o["final_norm_g"] = np.ascontiguousarray(inp["final_norm_g"].reshape(16, 128).T, dtype=f)
    return o


def make_program(stage_fn, ext_in, ext_out):
    nc = bass.Bass("TRN2", target_bir_lowering=False)
    C = Ctx()
    C.nc = nc
    C.d = {}
    allshapes = dict(INPUT_SHAPES)
    allshapes.update(SCRATCH_SHAPES)
    for name, shp in allshapes.items():
        if name in ext_in:
            kind = "ExternalInput"
        elif name in ext_out:
            kind = "ExternalOutput"
        elif name in SCRATCH_SHAPES:
            kind = "Internal"
        else:
            continue
        dtp = BF16 if name in BF16_SCRATCH else F32
        C.d[name] = nc.dram_tensor(name, list(shp), dtp, kind=kind).ap()
    S = Sched(nc)
    S.annotate = ANNOTATE
    C.S = S
    with S.stack:
        init_common(C)
        stage_fn(C)
        S.barrier()
        S.emit()
    return nc, S


def _tagged(C, name, fn, *a, **k):
    C.S.stage = name
    fn(C, *a, **k)
    C.S.stage = None


def full_stages(C):
    d = C.d
    S = C.S
    for l in range(DEPTH):
        last = (l == DEPTH - 1)
        ctx_out = not last
        _tagged(C, f"L{l}_mod", stage_mod, l)
        xsrc = d["xT"] if l == 0 else d["xres"]
        _tagged(C, f"L{l}_norm1", stage_norm, 1, xsrc, d["hT"], norm_chunks(True))
        _tagged(C, f"L{l}_win", stage_win, l)
        _tagged(C, f"L{l}_rwkv", stage_rwkv, l, ctx_out)
        _tagged(C, f"L{l}_ssd", stage_ssd, l, ctx_out)
        _tagged(C, f"L{l}_da", stage_da, l, ctx_out)
        _tagged(C, f"L{l}_na", stage_na, l, ctx_out)
        _tagged(C, f"L{l}_merge", stage_merge, l, ctx_out)
        _tagged(C, f"L{l}_wout", stage_wout, l, ctx_out)
        _tagged(C, f"L{l}_norm2", stage_norm, 2, d["xres"], d["hT"], norm_chunks(ctx_out))
        _tagged(C, f"L{l}_ffnup", stage_ffn_up, l, ctx_out)
        _tagged(C, f"L{l}_ffndn", stage_ffn_down, l, ctx_out)
    with ExitStack() as st:
        fg = S.sbuf("fin_g", [128, 16], stack=st)
        S.dma("sp", fg.v, d["final_norm_g"])
        stage_norm(C, 0, d["xres"], d["outT"], [(c0, w, 0) for (c0, w) in chunks(NLAT, 256)], final_g=fg)


_PROG = {}


def kernel(**inputs):
    inp = {k: np.asarray(v) for k, v in inputs.items()}
    n = 8
    shared = None
    in_maps = []
    for b in range(n):
        hi = host_inputs(inp, b) if shared is None else None
        if shared is None:
            shared = hi
            in_maps.append(hi)
        else:
            m = dict(shared)
            m["xT"] = np.ascontiguousarray(np.concatenate([inp["x"][b].T, inp["ctx"][b].T], axis=1), dtype=np.float32)
            cc = np.stack([inp["c"][b], inp["c_ctx"]], axis=-1)
            m["cc"] = np.ascontiguousarray(cc.reshape(16, 128, 2).transpose(1, 0, 2), dtype=np.float32)
            in_maps.append(m)
    if "nc" not in _PROG:
        nc, S = make_program(full_stages, set(INPUT_SHAPES.keys()), {"outT"})
        _PROG["nc"] = nc
    nc = _PROG["nc"]
    res = run_bass_kernel_spmd(nc, in_maps, core_ids=list(range(n)))
    out = np.stack([np.ascontiguousarray(res.results[b]["outT"].T) for b in range(n)], axis=0)
    return out.astype(np.float32)
```

```python
import numpy as np
from contextlib import ExitStack
import concourse.bass as bass
import concourse.mybir as mybir
from concourse.bass_utils import run_bass_kernel_spmd

F32 = mybir.dt.float32
BF16 = mybir.dt.bfloat16
ALU = mybir.AluOpType
AF = mybir.ActivationFunctionType
AX = mybir.AxisListType


class Buf:
    __slots__ = ("name", "last_w", "last_w_deps", "readers", "dsem", "dcnt", "is_psum")

    def __init__(self, name):
        self.name = name
        self.last_w = None
        self.last_w_deps = []
        self.readers = []
        self.dsem = None
        self.dcnt = 0
        self.is_psum = False


class V:
    __slots__ = ("ap", "buf")

    def __init__(self, ap, buf):
        self.ap = ap
        self.buf = buf

    def __getitem__(self, idx):
        return V(self.ap[idx], self.buf)

    def rearrange(self, s, **kw):
        return V(self.ap.rearrange(s, **kw), self.buf)

    def bc(self, shape):
        return V(self.ap.broadcast_to(list(shape)), self.buf)

    @property
    def shape(self):
        return self.ap.shape


class Tile:
    def __init__(self, handle, buf):
        self.h = handle
        self.buf = buf

    def __getitem__(self, idx):
        return V(self.h[idx], self.buf)

    @property
    def v(self):
        return V(self.h[:], self.buf)


class Op:
    __slots__ = ("eng", "fn", "deps", "signal", "dma_buf", "dma_cnt", "dma_sem", "idx", "val", "waits", "tag")

    def __init__(self, eng, fn):
        self.eng = eng
        self.fn = fn
        self.deps = []
        self.signal = False
        self.dma_buf = None
        self.dma_cnt = 0
        self.dma_sem = -1
        self.idx = -1
        self.val = 0
        self.waits = []


ENGS = ("pe", "act", "dve", "pool", "sp")


class Sched:
    def __init__(self, nc):
        self.nc = nc
        self.ops = {e: [] for e in ENGS}
        self.stack = ExitStack()
        self.bufs = []
        self.dma_pending = {}
        self.sems = {}
        self.nsem = 0
        self.sem_tot = []
        self.sem_free = []
        self.annotate = False
        self.stage = None

    def sbuf(self, name, shape, dtype=F32, stack=None):
        st = stack if stack is not None else self.stack
        h = st.enter_context(self.nc.sbuf_tensor(self._uniq("s_" + name), list(shape), dtype))
        b = Buf(name)
        self.bufs.append(b)
        return Tile(h, b)

    def psum(self, name, shape, dtype=F32, stack=None):
        st = stack if stack is not None else self.stack
        h = st.enter_context(self.nc.psum_tensor(self._uniq("q_" + name), list(shape), dtype))
        b = Buf(name)
        b.is_psum = True
        self.bufs.append(b)
        return Tile(h, b)

    def subviews(self, tile, n, psum=False):
        out = []
        for j in range(n):
            b = Buf(f"{tile.buf.name}_{j}")
            b.is_psum = psum
            self.bufs.append(b)
            out.append(V(tile.h[:, j], b))
        return out

    def _uniq(self, name):
        self.uid = getattr(self, "uid", 0) + 1
        return f"{name}_{self.uid}"

    def _sem(self, name):
        s = self.stack.enter_context(self.nc.semaphore(self._uniq(name)))
        self.nsem += 1
        return s

    def _add(self, eng, fn, reads, writes, dma_buf=None):
        op = Op(eng, fn)
        op.tag = getattr(self, "stage", None)
        op.idx = len(self.ops[eng])
        is_dma = dma_buf is not None
        deps = []
        raw = set()
        for b in reads:
            if b is None:
                continue
            if b.last_w is not None:
                deps.append(b.last_w)
                raw.add(id(b.last_w))
            if b.is_psum:
                for rd_ in b.readers:
                    if rd_.eng != eng:
                        deps.append(rd_)
        wdeps = {}
        for b in writes:
            if b is None:
                continue
            mine = []
            if b.last_w is not None:
                lw = b.last_w
                if is_dma and lw.dma_buf is not None:
                    mine.extend(b.last_w_deps)
                else:
                    mine.append(lw)
            mine.extend(b.readers)
            wdeps[id(b)] = mine
            deps.extend(mine)
        out = []
        seen = set()
        for d in deps:
            if id(d) in seen or d is op:
                continue
            seen.add(id(d))
            if d.dma_buf is None and d.eng == eng and not is_dma:
                if eng == "pe" or id(d) not in raw:
                    continue
            out.append(d)
        op.deps = out
        if is_dma:
            op.dma_buf = dma_buf
            if dma_buf.dsem is None:
                if self.sem_free:
                    dma_buf.dsem = self.sem_free.pop()
                else:
                    dma_buf.dsem = len(self.sem_tot)
                    self.sem_tot.append(0)
            self.sem_tot[dma_buf.dsem] += 16
            op.dma_sem = dma_buf.dsem
            op.dma_cnt = self.sem_tot[dma_buf.dsem]
            self.dma_pending[id(dma_buf)] = op
        for b in reads:
            if b is not None:
                b.readers.append(op)
        for b in writes:
            if b is not None:
                b.last_w = op
                b.last_w_deps = wdeps[id(b)]
                b.readers = []
        self.ops[eng].append(op)
        return op

    def barrier(self):
        lasts = []
        for e in ENGS:
            for o in reversed(self.ops[e]):
                if o.dma_buf is None and o.fn is not None:
                    lasts.append(o)
                    break
        pend = list(self.dma_pending.values())
        for e in ENGS:
            op = Op(e, None)
            op.idx = len(self.ops[e])
            op.deps = list(lasts) + pend
            self.ops[e].append(op)
        self.dma_pending = {}
        for b in self.bufs:
            b.last_w = None
            b.last_w_deps = []
            b.readers = []
            if b.dsem is not None:
                self.sem_free.append(b.dsem)
                b.dsem = None

    @staticmethod
    def _bufs(*vs):
        return [v.buf for v in vs if isinstance(v, V)]

    @staticmethod
    def _a(v):
        return v.ap if isinstance(v, V) else v

    def matmul(self, out, lhsT, rhs, start=True, stop=True):
        o, l, r = out.ap, lhsT.ap, rhs.ap
        rd = self._bufs(lhsT, rhs)
        if not start:
            rd = rd + [out.buf]
        return self._add("pe", lambda e: e.matmul(o, lhsT=l, rhs=r, start=start, stop=stop), rd, [out.buf])

    def transpose(self, out, in_, ident):
        o, i, d = out.ap, in_.ap, ident.ap
        return self._add("pe", lambda e: e.transpose(o, i, d), self._bufs(in_, ident), [out.buf])

    def act(self, out, in_, func, bias=0.0, scale=1.0, accum=None):
        o, i, b, s = out.ap, in_.ap, self._a(bias), self._a(scale)
        ac = accum.ap if accum is not None else None
        w = [out.buf] + ([accum.buf] if accum is not None else [])
        if ac is None:
            fn = lambda e: e.activation(o, i, func, bias=b, scale=s)
        else:
            fn = lambda e: e.activation(o, i, func, bias=b, scale=s, accum_out=ac)
        return self._add("act", fn, self._bufs(in_, bias, scale), w)

    def tt(self, eng, out, a, b, op):
        o, x, y = out.ap, a.ap, b.ap
        return self._add(eng, lambda e: e.tensor_tensor(o, x, y, op), self._bufs(a, b), [out.buf])

    def ts(self, eng, out, a, s1, op0, s2=None, op1=None, accum=None):
        o, x, p, q = out.ap, a.ap, self._a(s1), self._a(s2)
        ac = accum.ap if accum is not None else None
        w = [out.buf] + ([accum.buf] if accum is not None else [])
        kw = {}
        if op1 is not None:
            kw["op1"] = op1
        if ac is not None:
            kw["accum_out"] = ac
        return self._add(eng, lambda e: e.tensor_scalar(o, x, p, q, op0, **kw), self._bufs(a, s1, s2), w)

    def stt(self, eng, out, in0, scalar, in1, op0, op1):
        o, x, s, y = out.ap, in0.ap, self._a(scalar), in1.ap
        return self._add(eng, lambda e: e.scalar_tensor_tensor(o, x, s, y, op0, op1), self._bufs(in0, scalar, in1), [out.buf])

    def copy(self, eng, out, in_):
        o, i = out.ap, in_.ap
        if eng == "act":
            return self._add("act", lambda e: e.copy(o, i), [in_.buf], [out.buf])
        return self._add(eng, lambda e: e.tensor_copy(o, i), [in_.buf], [out.buf])

    def memset(self, eng, out, val):
        o = out.ap
        return self._add(eng, lambda e: e.memset(o, val), [], [out.buf])

    def reduce(self, eng, out, in_, op, axis=AX.X):
        o, i = out.ap, in_.ap
        return self._add(eng, lambda e: e.tensor_reduce(o, i, axis, op), [in_.buf], [out.buf])

    def recip(self, out, in_):
        o, i = out.ap, in_.ap
        return self._add("dve", lambda e: e.reciprocal(o, i), [in_.buf], [out.buf])

    def dma(self, q, out, in_, extra_reads=()):
        o = out.ap if isinstance(out, V) else out
        i = in_.ap if isinstance(in_, V) else in_
        ob = out.buf if isinstance(out, V) else None
        ib = in_.buf if isinstance(in_, V) else None
        sb = ob if ob is not None else ib
        assert sb is not None
        reads = ([ib] if ib is not None else []) + list(extra_reads)
        writes = [ob] if ob is not None else []
        return self._add(q, lambda e: e.dma_start(out=o, in_=i), reads, writes, dma_buf=sb)

    def emit(self):
        nc = self.nc
        for e in ENGS:
            for op in self.ops[e]:
                for d in op.deps:
                    d.signal = True
        esem = {}
        for e in ENGS:
            cnt = 0
            for op in self.ops[e]:
                if op.dma_buf is None and op.signal and op.fn is not None:
                    cnt += 1
                    op.val = cnt
                elif op.dma_buf is None and op.fn is None and op.signal:
                    raise RuntimeError("barrier op signalled")
            if cnt:
                esem[e] = self._sem("e_" + e)
        dsems = [self._sem(f"d{i}") for i in range(len(self.sem_tot))]
        nwait = 0
        for e in ENGS:
            seen = {}
            for op in self.ops[e]:
                ws = {}
                for d in op.deps:
                    if d.dma_buf is not None:
                        key = ("d", d.dma_sem)
                        sem, val = dsems[d.dma_sem], d.dma_cnt
                    else:
                        key = ("e", d.eng)
                        sem, val = esem[d.eng], d.val
                    if seen.get(key, 0) >= val:
                        continue
                    if key not in ws or ws[key][1] < val:
                        ws[key] = (sem, val)
                for key, (sem, val) in ws.items():
                    seen[key] = val
                op.waits = list(ws.values())
                nwait += len(op.waits)
        self.stats = {e: len(self.ops[e]) for e in ENGS}
        self.stats["waits"] = nwait
        self.stats["sems"] = self.nsem
        engmap = {"pe": "tensor", "act": "scalar", "dve": "vector", "pool": "gpsimd", "sp": "sync"}
        with nc.Block() as block:
            for e in ENGS:
                ops = self.ops[e]
                if not ops:
                    continue
                sem_e = esem.get(e)

                def body(eng, ops=ops, sem_e=sem_e, dsems=dsems, self=self):
                    for op in ops:
                        for (sem, val) in op.waits:
                            eng.wait_ge(sem, val)
                        if op.fn is None:
                            continue
                        ins = op.fn(eng)
                        if self.annotate and op.tag:
                            ins.annotate(op.tag)
                        if op.dma_buf is not None:
                            ins.then_inc(dsems[op.dma_sem], 16)
                        elif op.signal:
                            ins.then_inc(sem_e, 1)
                getattr(block, engmap[e])(body)


D_MODEL = 2048
NLAT = 2048
NCTX = 256
T = NLAT + NCTX
D_FF = 5632
NORM_EPS = 1e-6
KD = 16


class Ctx:
    pass


def chunks(total, size):
    return [(i, min(size, total - i)) for i in range(0, total, size)]


def fm(ap):
    return ap.rearrange("(k p) n -> p k n", p=128)


def init_common(C):
    S = C.S
    C.ps = [S.psum(f"ps{i}", [128, 512]) for i in range(8)]
    C.psi = 0
    C.ident = S.sbuf("ident", [128, 128])
    C.ones = S.sbuf("ones", [128, 128])
    S.dma("sp", C.ident.v, C.d["ident"])
    S.memset("dve", C.ones.v, 1.0)
    C.mod = S.sbuf("mod", [128, 96, 2])
    C.g1s = S.sbuf("g1s", [128, 16, 2])
    C.g2s = S.sbuf("g2s", [128, 16, 2])
    C.dq = 0


def nextps(C):
    p = C.ps[C.psi % 8]
    C.psi += 1
    return p


def ldq(C):
    C.dq += 1
    return ("sp", "pool")[C.dq % 2]


def stage_mod(C, l):
    S = C.S
    with ExitStack() as st:
        wbuf = [S.sbuf(f"adaw{i}", [128, 16, 512], BF16, stack=st) for i in range(3)]
        cc = S.sbuf("cc", [128, 16, 2], stack=st)
        sc = S.sbuf("sc", [128, 16, 2], BF16, stack=st)
        adab = S.sbuf("adab", [128, 96], stack=st)
        ng = S.sbuf("ng", [128, 2, 16], stack=st)
        S.dma("sp", cc.v, C.d["cc"])
        S.dma("sp", adab.v, C.d["ada_b"][l])
        S.dma("sp", ng[:, 0, :], C.d["norm1_g"][l])
        S.dma("sp", ng[:, 1, :], C.d["norm2_g"][l])
        S.act(sc.v, cc.v, AF.Silu)
        W = fm(C.d["ada_w"][l])
        for fb in range(24):
            w = wbuf[fb % 3]
            S.dma("pool", w.v, W[:, :, fb * 512:(fb + 1) * 512])
            p = nextps(C)
            for j in range(4):
                for k in range(16):
                    S.matmul(p[:, j * 2:(j + 1) * 2], w[:, k, j * 128:(j + 1) * 128], sc[:, k, :],
                             start=(k == 0), stop=(k == 15))
            for col in range(2):
                S.tt("dve", C.mod[:, fb * 4:(fb + 1) * 4, col],
                     p[:, 0:8].rearrange("p (j c) -> p j c", c=2)[:, :, col],
                     adab[:, fb * 4:(fb + 1) * 4], ALU.add)
        for col in range(2):
            S.stt("dve", C.g1s[:, :, col], C.mod[:, 16:32, col], 1.0, ng[:, 0, :], ALU.add, ALU.mult)
            S.stt("dve", C.g2s[:, :, col], C.mod[:, 64:80, col], 1.0, ng[:, 1, :], ALU.add, ALU.mult)
    S.barrier()


def stage_norm(C, which, src, dst, col_chunks, final_g=None):
    S = C.S
    srcv, dstv = fm(src), fm(dst)
    with ExitStack() as st:
        xb = [S.sbuf(f"nx{i}", [128, 16, 256], stack=st) for i in range(2)]
        sqb = [S.sbuf(f"nsq{i}", [128, 16, 256], stack=st) for i in range(2)]
        hb = [S.sbuf(f"nh{i}", [128, 16, 256], F32 if which == 0 else BF16, stack=st) for i in range(2)]
        hf = S.sbuf("nhf", [128, 16, 256], stack=st) if which != 0 else None
        rs = [S.sbuf(f"nr{i}", [128, 256], stack=st) for i in range(2)]
        def stats(ci):
            c0, w, col = col_chunks[ci]
            x, r = xb[ci % 2], rs[ci % 2]
            sq = sqb[ci % 2]
            S.dma("sp", x[:, :, 0:w], srcv[:, :, c0:c0 + w])
            S.act(sq[:, :, 0:w], x[:, :, 0:w], AF.Square)
            p = nextps(C)
            for k in range(16):
                S.matmul(p[:, 0:w], C.ones.v, sq[:, k, 0:w], start=(k == 0), stop=(k == 15))
            S.ts("dve", r[:, 0:w], p[:, 0:w], 1.0 / D_MODEL, ALU.mult, NORM_EPS, ALU.add)
            S.act(r[:, 0:w], r[:, 0:w], AF.Ln)
            S.act(r[:, 0:w], r[:, 0:w], AF.Exp, scale=-0.5)

        hsub = [S.subviews(t_, 16) for t_ in hb]
        hfsub = S.subviews(hf, 16) if hf is not None else None

        def apply(ci):
            c0, w, col = col_chunks[ci]
            x, h, r = xb[ci % 2], hb[ci % 2], rs[ci % 2]
            hs = hsub[ci % 2]
            for k in range(16):
                if which == 0:
                    S.stt("dve", hs[k][:, 0:w], x[:, k, 0:w], final_g[:, k:k + 1], r[:, 0:w], ALU.mult, ALU.mult)
                else:
                    gs = C.g1s if which == 1 else C.g2s
                    sh0 = 0 if which == 1 else 48
                    S.stt("dve", hfsub[k][:, 0:w], x[:, k, 0:w], gs[:, k, col:col + 1], r[:, 0:w], ALU.mult, ALU.mult)
                    S.act(hs[k][:, 0:w], hfsub[k][:, 0:w], AF.Identity, bias=C.mod[:, sh0 + k, col:col + 1])
            S.dma("pool", dstv[:, :, c0:c0 + w], h[:, :, 0:w], extra_reads=[v_.buf for v_ in hs])

        n_ = len(col_chunks)
        stats(0)
        for ci in range(n_):
            if ci + 1 < n_:
                stats(ci + 1)
            apply(ci)
    S.barrier()


def norm_chunks(with_ctx=True):
    cs = [(c0, w, 0) for (c0, w) in chunks(NLAT, 256)]
    if with_ctx:
        cs += [(NLAT, 256, 1)]
    return cs


def seg_chunks(g0, gw, size=512):
    out = []
    a = g0
    end = g0 + gw
    while a < end:
        b = min(a + size, end)
        if a < NLAT < b:
            b = NLAT
        out.append((a - g0, b - a))
        a = b
    return out


def gemm(C, inT, K, W, tok_groups, blocks, epi, group_hook=None, wblock=None, resident=None):
    S = C.S
    nk = K // 128
    if resident is None:
        resident = {}
    res = [resident.get(g, g) for g in tok_groups]
    gwmax = max(rw for _, rw in res)
    nwmax = max(nw for _, nw, _ in blocks)
    Wv = fm(W) if W is not None else None
    inv = fm(inT)
    with ExitStack() as st:
        act = S.sbuf("g_act", [128, nk, gwmax], BF16, stack=st)
        NWB = 3
        wb = [S.sbuf(f"g_w{i}", [128, nk, nwmax], BF16, stack=st) for i in range(NWB)]
        bi_glob = 0
        for gi, (gg0, ggw) in enumerate(tok_groups):
            g0, gw = res[gi]
            kq = max(1, nk // 4)
            for k0 in range(0, nk, kq):
                k1 = min(nk, k0 + kq)
                S.dma(("sp", "act")[(k0 // kq) % 2], act[:, k0:k1, 0:gw], inv[:, k0:k1, g0:g0 + gw])
            if group_hook is not None:
                group_hook("begin", g0, gw)
            for (n0, nw, mode) in blocks:
                wt = wb[bi_glob % NWB]
                bi_glob += 1
                wsrc = wblock(n0, nw) if wblock is not None else Wv[:, :, n0:n0 + nw]
                S.dma("pool", wt[:, :, 0:nw], wsrc)
                if mode == "T":
                    for t0 in range(0, gw, 128):
                        ps = nextps(C)
                        for k in range(nk):
                            S.matmul(ps[:, 0:nw], act[:, k, t0:t0 + 128], wt[:, k, 0:nw],
                                     start=(k == 0), stop=(k == nk - 1))
                        epi("T", ps, n0, nw, g0 + t0, 128, g0, gw)
                else:
                    for f0 in range(0, nw, 128):
                        for (t0, tw) in seg_chunks(g0, gw):
                            ps = nextps(C)
                            for k in range(nk):
                                S.matmul(ps[:, 0:tw], wt[:, k, f0:f0 + 128], act[:, k, t0:t0 + tw],
                                         start=(k == 0), stop=(k == nk - 1))
                            epi("F", ps, n0 + f0, 128, g0 + t0, tw, g0, gw)
            if group_hook is not None:
                group_hook("end", g0, gw)
    S.barrier()


class Stager:
    def __init__(self, C, st, name, n=4, width=512):
        self.C = C
        self.bufs = [C.S.sbuf(f"{name}{i}", [128, width], stack=st) for i in range(n)]
        self.i = 0

    def next(self):
        b = self.bufs[self.i % len(self.bufs)]
        self.i += 1
        return b

    def evac(self, dst_view, src_view):
        S = self.C.S
        if self.i % 2 == 0:
            S.copy("act", dst_view, src_view)
        else:
            S.copy("dve", dst_view, src_view)


def stage_win(C, l):
    S = C.S
    d = C.d
    blocks = []
    outmap = []

    def addT(n0, n1, dram):
        for (o, w) in chunks(n1 - n0, 512):
            blocks.append((n0 + o, w, "T"))
        outmap.append((n0, n1, "T", dram))

    def addF(n0, n1, dram):
        for (o, w) in chunks(n1 - n0, 512):
            blocks.append((n0 + o, w, "F"))
        outmap.append((n0, n1, "F", dram))

    addT(0, 1920, d["p_rw"])
    addT(1920, 3472, d["p_ssd"])
    addF(3472, 4496, d["da_qkT"])
    addT(4496, 5008, d["da_v"])
    addF(5008, 6032, d["na_qkT"])
    addT(6032, 6544, d["na_v"])

    def find(n0):
        for (a, b, m, dr) in outmap:
            if a <= n0 < b:
                return a, dr
        raise KeyError

    with ExitStack() as st:
        sg = Stager(C, st, "wst", 4)

        def epi(mode, ps, n0, nw, c0, cw, g0, gw):
            a, dr = find(n0)
            sb = sg.next()
            if mode == "T":
                sg.evac(sb[:, 0:nw], ps[:, 0:nw])
                S.dma("act" if sg.i % 2 else "sp", dr[c0:c0 + 128, n0 - a:n0 - a + nw], sb[:, 0:nw])
            else:
                sg.evac(sb[:, 0:cw], ps[:, 0:cw])
                S.dma("act" if sg.i % 2 else "sp", dr[n0 - a:n0 - a + 128, c0:c0 + cw], sb[:, 0:cw])

        gemm(C, d["hT"], D_MODEL, d["w_in"][l], [(0, T)], blocks, epi)


def stage_merge(C, l, with_ctx):
    S = C.S
    d = C.d
    Tt = T if with_ctx else NLAT
    NG = 2
    gw = Tt // NG
    hv = fm(d["hT"])
    bv = d["brT"].rearrange("n (k p) t -> p (n k) t", p=128)
    wg = d["w_gate"][l].rearrange("n (k p) m -> n p k m", p=128)
    wbr = d["w_br"][l].rearrange("n (k p) m -> n p k m", p=128)
    mgv = fm(d["mgT"])
    CW = 384
    with ExitStack() as st:
        hact = S.sbuf("m_h", [128, 16, gw], BF16, stack=st)
        bact = S.sbuf("m_b", [128, 16, gw], BF16, stack=st)
        wgb = [S.sbuf(f"m_wg{i}", [128, 16, 256], BF16, stack=st) for i in range(3)]
        wbb = [S.sbuf(f"m_wb{i}", [128, 4, 256], BF16, stack=st) for i in range(3)]
        gb = S.sbuf("m_gb", [128, 4, 16], stack=st)
        S.dma("sp", gb.v, d["gate_b"][l])
        sig = [S.sbuf(f"m_sig{i}", [128, CW], stack=st) for i in range(2)]
        tmp = [S.sbuf(f"m_tmp{i}", [128, CW], stack=st) for i in range(2)]
        acc = [S.sbuf(f"m_acc{i}", [128, 2, gw], stack=st) for i in range(2)]
        aout = [S.sbuf(f"m_ao{i}", [128, 2, gw], BF16, stack=st) for i in range(2)]
        wi = 0
        ai = 0
        si = 0
        for g in range(NG):
            g0 = g * gw
            for k0 in range(0, 16, 4):
                S.dma("sp", hact[:, k0:k0 + 4, :], hv[:, k0:k0 + 4, g0:g0 + gw])
                S.dma("act", bact[:, k0:k0 + 4, :], bv[:, k0:k0 + 4, g0:g0 + gw])
            tch = seg_chunks(g0, gw, CW)
            for db in range(8):
                a = acc[ai % 2]
                ao = aout[ai % 2]
                ai += 1
                for n in range(4):
                    wgt, wbt = wgb[wi % 3], wbb[wi % 3]
                    wi += 1
                    S.dma("pool", wgt.v, wg[n][:, :, db * 256:(db + 1) * 256])
                    S.dma("pool", wbt.v, wbr[n][:, :, db * 256:(db + 1) * 256])
                    for dc in range(2):
                        for (t0, tw) in tch:
                            pg = nextps(C)
                            for k in range(16):
                                S.matmul(pg[:, 0:tw], wgt[:, k, dc * 128:(dc + 1) * 128], hact[:, k, t0:t0 + tw],
                                         start=(k == 0), stop=(k == 15))
                            pb = nextps(C)
                            for k in range(4):
                                S.matmul(pb[:, 0:tw], wbt[:, k, dc * 128:(dc + 1) * 128], bact[:, n * 4 + k, t0:t0 + tw],
                                         start=(k == 0), stop=(k == 3))
                            sg = sig[si % 2]
                            tp = tmp[si % 2]
                            si += 1
                            S.act(sg[:, 0:tw], pg[:, 0:tw], AF.Sigmoid, bias=gb[:, n, db * 2 + dc:db * 2 + dc + 1])
                            if n == 0:
                                S.tt("dve", a[:, dc, t0:t0 + tw], sg[:, 0:tw], pb[:, 0:tw], ALU.mult)
                            else:
                                S.tt("dve", tp[:, 0:tw], sg[:, 0:tw], pb[:, 0:tw], ALU.mult)
                                dst = ao if n == 3 else a
                                S.tt("dve", dst[:, dc, t0:t0 + tw], a[:, dc, t0:t0 + tw], tp[:, 0:tw], ALU.add)
                S.dma("sp", mgv[:, db * 2:db * 2 + 2, g0:g0 + gw], ao.v)
    S.barrier()


def resid_epi(C, st, gate0, xsrc, xdram):
    S = C.S
    xt = [S.sbuf(f"r_x{i}", [128, 512], stack=st) for i in range(4)]
    cnt = [0]

    def epi(mode, ps, n0, nw, c0, cw, g0, gw):
        assert mode == "F"
        x = xt[cnt[0] % 4]
        cnt[0] += 1
        col = 0 if c0 < NLAT else 1
        k = n0 // 128
        S.dma("sp", x[:, 0:cw], xsrc[n0:n0 + 128, c0:c0 + cw])
        S.stt("dve", x[:, 0:cw], ps[:, 0:cw], C.mod[:, gate0 + k, col:col + 1], x[:, 0:cw], ALU.mult, ALU.add)
        S.dma("act", xdram[n0:n0 + 128, c0:c0 + cw], x[:, 0:cw])
    return epi


def stage_wout(C, l, with_ctx):
    d = C.d
    xsrc = d["xT"] if l == 0 else d["xres"]
    Tt = T if with_ctx else NLAT
    groups = [(0, Tt)]
    blocks = [(n0, 512, "F") for n0 in range(0, D_MODEL, 512)]
    with ExitStack() as st:
        epi = resid_epi(C, st, 32, xsrc, d["xres"])
        gemm(C, d["mgT"], D_MODEL, d["w_out"][l], groups, blocks, epi)


def stage_ffn_up(C, l, with_ctx):
    S = C.S
    d = C.d
    Tt = T if with_ctx else NLAT
    half = Tt
    groups = [(0, Tt)]
    resident = {}
    blocks = [(j * 256, 256, "F") for j in range(44)]
    guv = d["guT"]
    with ExitStack() as st:
        cw_t = S.sbuf("f_cw", [128, 3, 88], stack=st)
        cb_t = S.sbuf("f_cb", [128, 88], stack=st)
        S.dma("sp", cw_t.v, d["ffn_conv_w"][l])
        S.dma("sp", cb_t.v, d["ffn_conv_b"][l])
        RW = Tt
        ub = [[S.sbuf(f"f_u{i}{j}", [128, RW], stack=st) for j in range(2)] for i in range(2)]
        yb = [[S.sbuf(f"f_y{i}{j}", [128, RW], stack=st) for j in range(2)] for i in range(2)]
        gob = [S.sbuf(f"f_go{i}", [128, RW], BF16, stack=st) for i in range(2)]
        state = {"i": 0}

        def epi(mode, ps, n0, nw, c0, cw, g0, gw):
            j = n0 // 256
            isval = (n0 % 256) // 128
            u = ub[j % 2][isval]
            S.copy("act" if isval else "dve", u[:, c0 - g0:c0 - g0 + cw], ps[:, 0:cw])
            last_chunk = (c0 + cw == g0 + gw)
            if not (isval and last_chunk):
                return
            o0 = g0
            o1 = g0 + gw
            pieces = []
            a = o0
            while a < o1:
                b = o1
                if a < NLAT < b:
                    b = NLAT
                pieces.append((a, b))
                a = b
            for (a, b) in pieces:
                seg0, seg1 = (0, NLAT) if a < NLAT else (NLAT, T)
                if not with_ctx:
                    seg0, seg1 = 0, NLAT
                for v_ in range(2):
                    u = ub[j % 2][v_]
                    y = yb[j % 2][v_]
                    ch = j + 44 * v_
                    la, lb = a - g0, b - g0
                    S.act(y[:, la:lb], u[:, la:lb], AF.Identity, bias=cb_t[:, ch:ch + 1], scale=cw_t[:, 1, ch:ch + 1])
                    lo = la + 1 if a == seg0 else la
                    S.stt("dve", y[:, lo:lb], u[:, lo - 1:lb - 1], cw_t[:, 0, ch:ch + 1], y[:, lo:lb], ALU.mult, ALU.add)
                    hi = lb - 1 if b == seg1 else lb
                    S.stt("dve", y[:, la:hi], u[:, la + 1:hi + 1], cw_t[:, 2, ch:ch + 1], y[:, la:hi], ALU.mult, ALU.add)
                yg, yv = yb[j % 2][0], yb[j % 2][1]
                la, lb = a - g0, b - g0
                S.act(ub[j % 2][0][:, la:lb], yg[:, la:lb], AF.Silu)
                S.tt("dve", gob[j % 2][:, la:lb], yv[:, la:lb], ub[j % 2][0][:, la:lb], ALU.mult)
                S.dma("sp", guv[j * 128:(j + 1) * 128, a:b], gob[j % 2][:, la:lb])

        gemm(C, d["hT"], D_MODEL, d["ffn_up"][l], groups, blocks, epi, resident=resident)


def stage_ffn_down(C, l, with_ctx):
    d = C.d
    Tt = T if with_ctx else NLAT
    gw = Tt // 2
    groups = [(g * gw, gw) for g in range(2)]
    blocks = [(n0, 128, "F") for n0 in range(0, D_MODEL, 128)]
    wv = d["ffn_down"][l]

    def wblock(n0, nw):
        return wv[n0 // 128]
    with ExitStack() as st:
        epi = resid_epi(C, st, 80, d["xres"], d["xres"])
        gemm(C, d["guT"], D_FF, None, groups, blocks, epi, wblock=wblock)

import math

DA_SUBLN_EPS = 1e-5


def load_bcast(S, q, tile_v, dram_ap, n=128):
    return S.dma(q, tile_v, dram_ap.partition_broadcast(n))


def stage_da(C, l, ctx_out):
    S = C.S
    d = C.d
    lam_init = 0.8 - 0.6 * math.exp(-0.3 * l)
    qk = d["da_qkT"]
    vd = d["da_v"]
    out = d["brT"][2]
    acc = C.ps[0:4]
    rot = C.ps[4:7]
    misc = C.ps[7]
    NR = 3
    with ExitStack() as st:
        cos = S.sbuf("da_cos", [128, NLAT], stack=st)
        sin = S.sbuf("da_sin", [128, NLAT], stack=st)
        perm = S.sbuf("da_perm", [128, 128], stack=st)
        S.dma("sp", cos.v, d["rope_cos"])
        S.dma("act", sin.v, d["rope_sin"])
        S.dma("sp", perm.v, d["rope_perm"])
        lp = S.sbuf("da_lp", [128, 4, 64], stack=st)
        load_bcast(S, "sp", lp.v, d["da_lambda"][l])
        pr = S.sbuf("da_pr", [128, 2, 64], stack=st)
        sm = S.sbuf("da_sm", [128, 2], stack=st)
        S.tt("dve", pr[:, 0, :], lp[:, 0, :], lp[:, 1, :], ALU.mult)
        S.tt("dve", pr[:, 1, :], lp[:, 2, :], lp[:, 3, :], ALU.mult)
        S.reduce("dve", sm.v, pr.v, ALU.add)
        S.act(sm.v, sm.v, AF.Exp)
        nlam = S.sbuf("da_nlam", [128, 1], stack=st)
        S.tt("dve", nlam.v, sm[:, 1:2], sm[:, 0:1], ALU.subtract)
        S.ts("dve", nlam.v, nlam.v, -lam_init, ALU.add)
        gsub = S.sbuf("da_g", [128, 1], stack=st)
        S.dma("sp", gsub.v, d["da_subln_g"][l])
        S.ts("dve", gsub.v, gsub.v, 1.0 - lam_init, ALU.mult)

        qT = S.sbuf("da_q", [128, T], stack=st)
        kT = S.sbuf("da_k", [128, T], stack=st)
        qB = S.sbuf("da_qb", [128, T], BF16, stack=st)
        kB = S.sbuf("da_kb", [128, T], BF16, stack=st)
        kZ = [S.sbuf(f"da_kz{m}", [128, T], BF16, stack=st) for m in range(2)]
        S.memset("dve", kZ[0].v, 0.0)
        S.memset("dve", kZ[1].v, 0.0)
        vt = S.sbuf("da_vt", [128, 18, 128], BF16, stack=st)
        onesb = S.sbuf("da_1b", [128, 128], BF16, stack=st)
        S.memset("pool", onesb.v, 1.0)
        tmp = [S.sbuf(f"da_tmp{i}", [128, 512], stack=st) for i in range(2)]
        et = [S.sbuf(f"da_e{i}", [128, 512], BF16, stack=st) for i in range(5)]
        rr = [S.sbuf(f"da_r{i}", [128, 512], stack=st) for i in range(2)]
        oo = [S.sbuf(f"da_o{i}", [128, 512], stack=st) for i in range(2)]
        sq = S.sbuf("da_sq", [128, 512], stack=st)
        es = S.sbuf("da_es", [128, 512], stack=st)
        esb = S.sbuf("da_esb", [128, 512], BF16, stack=st)
        rs = S.sbuf("da_rs", [128, 512], stack=st)
        ob = [S.sbuf(f"da_ob{i}", [128, 512], BF16, stack=st) for i in range(2)]
        ei = 0
        oi = 0
        for h in range(4):
            S.dma("sp", qT.v, qk[h * 128:(h + 1) * 128, :])
            S.dma("act", kT.v, qk[512 + h * 128:512 + (h + 1) * 128, :])
            S.dma("pool", vt.v, vd[:, h * 128:(h + 1) * 128].rearrange("(b p) e -> p b e", p=128))
            ti = 0
            for X, XB in ((qT, qB), (kT, kB)):
                for c0 in range(0, NLAT, 512):
                    S.matmul(misc.v, perm.v, X[:, c0:c0 + 512])
                    t = tmp[ti % 2]
                    ti += 1
                    S.tt("dve", t.v, misc.v, sin[:, c0:c0 + 512], ALU.mult)
                    S.tt("dve", X[:, c0:c0 + 512], X[:, c0:c0 + 512], cos[:, c0:c0 + 512], ALU.mult)
                    S.tt("dve", XB[:, c0:c0 + 512], X[:, c0:c0 + 512], t.v, ALU.add)
                S.copy("act", XB[:, NLAT:T], X[:, NLAT:T])
            S.copy("act", kZ[0][0:64, :], kB[0:64, :])
            S.copy("dve", kZ[1][64:128, :], kB[64:128, :])
            qchunks = [(c0, 512, list(range(18))) for c0 in range(0, NLAT, 512)]
            if ctx_out:
                qchunks.append((NLAT, 256, [16, 17]))
            items = []
            for (c0, cw, kbs) in qchunks:
                for m in range(2):
                    for i, kb in enumerate(kbs):
                        items.append((c0, cw, m, i, kb, len(kbs)))
            slots = {}

            def emitA(n):
                c0, cw, m, i, kb, nk_ = items[n]
                ps = rot[n % NR]
                e = et[n % NR]
                S.matmul(ps[:, 0:cw], kZ[m][:, kb * 128:(kb + 1) * 128], qB[:, c0:c0 + cw])
                S.act(e[:, 0:cw], ps[:, 0:cw], AF.Exp, scale=0.125)

            def emitB(n):
                nonlocal oi
                c0, cw, m, i, kb, nk_ = items[n]
                e = et[n % NR]
                OT, SM = acc[m], acc[2 + m]
                S.matmul(OT[:, 0:cw], vt[:, kb, :], e[:, 0:cw], start=(i == 0), stop=(i == nk_ - 1))
                S.matmul(SM[:, 0:cw], onesb.v, e[:, 0:cw], start=(i == 0), stop=(i == nk_ - 1))
                if i != nk_ - 1:
                    return
                S.recip(rr[m][:, 0:cw], SM[:, 0:cw])
                S.tt("dve", oo[m][:, 0:cw], OT[:, 0:cw], rr[m][:, 0:cw], ALU.mult)
                if m != 1:
                    return
                o = oo[0]
                S.stt("dve", o[:, 0:cw], oo[1][:, 0:cw], nlam.v, o[:, 0:cw], ALU.mult, ALU.add)
                S.act(sq[:, 0:cw], o[:, 0:cw], AF.Square)
                S.matmul(misc[:, 0:cw], C.ones.v, sq[:, 0:cw])
                S.ts("dve", rs[:, 0:cw], misc[:, 0:cw], 1.0 / 128, ALU.mult, DA_SUBLN_EPS, ALU.add)
                S.act(rs[:, 0:cw], rs[:, 0:cw], AF.Ln)
                S.act(rs[:, 0:cw], rs[:, 0:cw], AF.Exp, scale=-0.5)
                b = ob[oi % 2]
                oi += 1
                S.stt("dve", b[:, 0:cw], o[:, 0:cw], gsub.v, rs[:, 0:cw], ALU.mult, ALU.mult)
                S.dma("sp", out[h * 128:(h + 1) * 128, c0:c0 + cw], b[:, 0:cw])

            LA = 2
            for n in range(len(items) + LA):
                if n < len(items):
                    emitA(n)
                if n - LA >= 0:
                    emitB(n - LA)
    S.barrier()


def rope_tables():
    GRID_W = 64
    t = np.arange(NLAT)
    pos = np.stack([t // GRID_W, t % GRID_W], axis=-1).astype(np.float32)
    n_freq = 16
    inv = (np.float32(10000.0) ** (-np.arange(n_freq, dtype=np.float32) / np.float32(n_freq))).astype(np.float32)
    ang = (pos[:, :, None] * inv).astype(np.float32)
    cosv, sinv = np.cos(ang).astype(np.float32), np.sin(ang).astype(np.float32)
    cos = np.zeros((128, NLAT), np.float32)
    sin = np.zeros((128, NLAT), np.float32)
    perm = np.zeros((128, 128), np.float32)
    for p in range(128):
        dd = p % 64
        half, j, f = dd // 32, (dd % 32) // 16, dd % 16
        cos[p] = cosv[:, half, f]
        if j == 0:
            sin[p] = -sinv[:, half, f]
            partner = p + 16
        else:
            sin[p] = sinv[:, half, f]
            partner = p - 16
        perm[partner, p] = 1.0
    return cos, sin, perm


NT = T // 128
SEG_FIRST = {0: True, 16: True}
SEG_LAST = {15: True, 17: True}


def bc_last(v, n):
    shp = list(v.ap.shape)
    return V(v.ap.unsqueeze(len(shp)).broadcast_to(shp + [n]), v.buf)


def load_shifted(S, st_tiles, src, t0, width, c0, first, last):
    xm, x0, xp = st_tiles
    S.dma("sp", x0[:, 0:width], src[t0:t0 + 128, c0:c0 + width])
    if first:
        S.memset("pool", xm[0:32, 0:width], 0.0)
        S.dma("pool", xm[1:128, 0:width], src[t0:t0 + 127, c0:c0 + width])
    else:
        S.dma("pool", xm[:, 0:width], src[t0 - 1:t0 + 127, c0:c0 + width])
    if last:
        S.memset("pool", xp[96:128, 0:width], 0.0)
        S.dma("act", xp[0:127, 0:width], src[t0 + 1:t0 + 128, c0:c0 + width])
    else:
        S.dma("act", xp[:, 0:width], src[t0 + 1:t0 + 129, c0:c0 + width])


def stage_ssd(C, l, ctx_out):
    S = C.S
    d = C.d
    src = d["p_ssd"]
    out = d["brT"][1].rearrange("(k p) t -> p k t", p=128)
    P = C.ps
    with ExitStack() as st:
        triF = S.sbuf("tr_f", [128, 128], stack=st)
        triB = S.sbuf("tr_b", [128, 128], stack=st)
        S.dma("sp", triF.v, d["triF"])
        S.dma("sp", triB.v, d["triB"])
        dtb = S.sbuf("s_dtb", [128, 16], stack=st)
        load_bcast(S, "sp", dtb.v, d["ssd_dt_bias"][l])
        negA = S.sbuf("s_negA", [128, 16], stack=st)
        load_bcast(S, "sp", negA.v, d["ssd_a_log"][l])
        S.act(negA.v, negA.v, AF.Exp)
        S.ts("dve", negA.v, negA.v, -1.0, ALU.mult)
        dsk = S.sbuf("s_dsk", [128, 8], stack=st)
        load_bcast(S, "sp", dsk.v, d["ssd_d"][l])
        ngb = S.sbuf("s_ng", [128, 512], stack=st)
        load_bcast(S, "pool", ngb.v, d["ssd_norm_g"][l])

        xall = S.sbuf("s_xall", [128, NT, 768], stack=st)
        bct = S.sbuf("s_bct", [128, NT, 4, 128], stack=st)
        dta = S.sbuf("s_dta", [128, NT, 2, 16], stack=st)
        H = S.sbuf("s_H", [128, 2, 2, 256], stack=st)
        S.memset("dve", H.v, 0.0)
        st2 = ExitStack()
        cw = S.sbuf("s_cw", [128, 4, 1024], stack=st2)
        load_bcast(S, "sp", cw[:, 0:3, :], d["ssd_conv_w"][l])
        load_bcast(S, "pool", cw[:, 3, :], d["ssd_conv_b"][l])
        sh = [[S.sbuf(f"s_sh{i}{j}", [128, 1024], stack=st2) for j in range(3)] for i in range(2)]
        ta = [S.sbuf(f"s_ta{i}", [128, 1024], stack=st2) for i in range(2)]
        tb = [S.sbuf(f"s_tb{i}", [128, 1024], stack=st2) for i in range(2)]
        dtr = [S.sbuf(f"s_dtr{i}", [128, 16], stack=st2) for i in range(2)]
        for ti in range(NT):
            t0 = ti * 128
            tl = sh[ti % 2]
            load_shifted(S, tl, src, t0, 1024, 512, ti in SEG_FIRST, ti in SEG_LAST)
            a, b = ta[ti % 2], tb[ti % 2]
            S.tt("dve", a.v, tl[0].v, cw[:, 0, :], ALU.mult)
            S.tt("dve", b.v, tl[1].v, cw[:, 1, :], ALU.mult)
            S.tt("dve", a.v, a.v, b.v, ALU.add)
            S.tt("dve", b.v, tl[2].v, cw[:, 2, :], ALU.mult)
            S.tt("dve", a.v, a.v, b.v, ALU.add)
            S.tt("dve", a.v, a.v, cw[:, 3, :], ALU.add)
            S.act(b.v, a.v, AF.Silu)
            S.copy("act", xall[:, ti, :], b[:, 0:768])
            r = dtr[ti % 2]
            S.dma("sp", r.v, src[t0:t0 + 128, 1536:1552])
            S.tt("dve", r.v, r.v, dtb.v, ALU.add)
            S.act(r.v, r.v, AF.Exp)
            S.act(dta[:, ti, 0, :], r.v, AF.Ln, bias=1.0)
            S.tt("dve", dta[:, ti, 1, :], dta[:, ti, 0, :], negA.v, ALU.mult)
            for q in range(4):
                S.transpose(P[0][:, q * 128:(q + 1) * 128], b[:, 512 + q * 128:512 + (q + 1) * 128], C.ident.v)
            S.copy("dve", bct[:, ti, :, :], P[0].v.rearrange("p (q t) -> p q t", q=4))

        S.barrier()
        st2.close()
        yacc = S.sbuf("s_yacc", [128, NT, 512], stack=st)
        ct = [S.sbuf(f"s_ct{i}", [128, 16], stack=st) for i in range(2)]
        te = [S.sbuf(f"s_te{i}", [128, 8], stack=st) for i in range(2)]
        sc = [S.sbuf(f"s_sc{i}", [128, 8], stack=st) for i in range(2)]
        ee = [S.sbuf(f"s_ee{i}", [128, 16], stack=st) for i in range(2)]
        xs = [S.sbuf(f"s_xs{i}", [128, 512], stack=st) for i in range(2)]
        sm = [S.sbuf(f"s_sm{i}", [128, 2, 128], stack=st) for i in range(2)]
        atr4 = [S.sbuf(f"s_atr{i}", [128, 4, 128], stack=st) for i in range(2)]
        sg4 = [S.sbuf(f"s_sg{i}", [128, 4, 128], stack=st) for i in range(2)]
        mt4 = [S.sbuf(f"s_mt{i}", [128, 4, 128], stack=st) for i in range(2)]

        def bc_mid4(v):
            return V(v.ap.unsqueeze(1).broadcast_to([128, 4, 128]), v.buf)
        yt = [S.sbuf(f"s_yt{i}", [128, 512], stack=st) for i in range(2)]
        iters = []
        for dr in range(2):
            order = [16, 17] + list(range(16)) if dr == 0 else [17, 16] + list(range(15, -1, -1))
            for ti in order:
                iters.append((dr, ti))
        ih = [0]

        def partA(n):
            dr, ti = iters[n]
            k = n % 2
            tri = triF if dr == 0 else triB
            yo = P[3] if n % 2 == 0 else P[0]
            dt = dta[:, ti, 0, dr * 8:(dr + 1) * 8]
            aa = dta[:, ti, 1, dr * 8:(dr + 1) * 8]
            X = xall[:, ti, 0:512]
            Bm = xall[:, ti, 512:768]
            S.matmul(P[1][:, 0:8], tri.v, aa)
            S.matmul(P[1][:, 8:16], C.ones.v, aa)
            S.copy("dve", ct[k].v, P[1][:, 0:16])
            cum, tot = ct[k][:, 0:8], ct[k][:, 8:16]
            S.tt("dve", te[k].v, tot, cum, ALU.subtract)
            S.act(te[k].v, te[k].v, AF.Exp)
            S.tt("dve", sc[k].v, te[k].v, dt, ALU.mult)
            S.act(ee[k].v, ct[k].v, AF.Exp)
            S.tt("dve", xs[k].v.rearrange("p (h e) -> p h e", h=8), X.rearrange("p (h e) -> p h e", h=8),
                 bc_last(sc[k].v, 64), ALU.mult)
            for g in range(2):
                S.matmul(yo[:, g * 256:(g + 1) * 256], bct[:, ti, 2 + g, :], H[:, dr, g, :])
                S.matmul(P[5][:, g * 128:(g + 1) * 128], bct[:, ti, g, :], bct[:, ti, 2 + g, :])
            for g in range(2):
                S.tt("dve", sm[k][:, g, :], P[5][:, g * 128:(g + 1) * 128], tri.v, ALU.mult)
            for g in range(2):
                S.matmul(P[2][:, g * 256:(g + 1) * 256], Bm[:, g * 128:(g + 1) * 128], xs[k][:, g * 256:(g + 1) * 256])
            for g in range(2):
                Hg = H[:, dr, g, :].rearrange("p (h e) -> p h e", h=4)
                S.tt("dve", Hg, Hg, bc_last(ee[k][:, 8 + g * 4:8 + (g + 1) * 4], 64), ALU.mult)
                S.tt("dve", H[:, dr, g, :], H[:, dr, g, :], P[2][:, g * 256:(g + 1) * 256], ALU.add)

        def partB(n):
            dr, ti = iters[n]
            k = n % 2
            tri = triF if dr == 0 else triB
            yo = P[3] if n % 2 == 0 else P[0]
            dt = dta[:, ti, 0, dr * 8:(dr + 1) * 8]
            aa = dta[:, ti, 1, dr * 8:(dr + 1) * 8]
            X = xall[:, ti, 0:512]
            cum = ct[k][:, 0:8]
            for g in range(2):
                j = ih[0] % 2
                pb = P[6 + ih[0] % 2]
                ih[0] += 1
                hs = slice(g * 4, (g + 1) * 4)
                S.tt("dve", atr4[j].v, bc_mid4(tri.v), bc_last(aa[:, hs], 128), ALU.mult)
                S.matmul(pb.v, C.ones.v, atr4[j].v.rearrange("p h i -> p (h i)"))
                S.tt("dve", sg4[j].v, pb.v.rearrange("p (h i) -> p h i", h=4), bc_last(cum[:, hs], 128), ALU.subtract)
                S.ts("dve", sg4[j].v, sg4[j].v, 0.0, ALU.min)
                S.act(sg4[j].v, sg4[j].v, AF.Exp)
                S.tt("dve", sg4[j].v, sg4[j].v, bc_last(dt[:, hs], 128), ALU.mult)
                S.tt("dve", mt4[j].v, sg4[j].v, bc_mid4(sm[k][:, g, :]), ALU.mult)
                for hh in range(4):
                    h = g * 4 + hh
                    S.matmul(P[4][:, h * 64:(h + 1) * 64], mt4[j][:, hh, :], X[:, h * 64:(h + 1) * 64])
            y = yt[k]
            S.tt("dve", y.v.rearrange("p (h e) -> p h e", h=8), yo.v.rearrange("p (h e) -> p h e", h=8),
                 bc_last(ee[k][:, 0:8], 64), ALU.mult)
            if dr == 0:
                S.tt("dve", yacc[:, ti, :], y.v, P[4].v, ALU.add)
            else:
                S.tt("dve", y.v, y.v, P[4].v, ALU.add)
                S.tt("dve", yacc[:, ti, :], yacc[:, ti, :], y.v, ALU.add)

        partA(0)
        for n in range(len(iters)):
            if n + 1 < len(iters):
                partA(n + 1)
            partB(n)
        zt = [S.sbuf(f"s_z{i}", [128, 512], stack=st) for i in range(2)]
        y2 = [S.sbuf(f"s_y2{i}", [128, 512], stack=st) for i in range(2)]
        ssq = [S.sbuf(f"s_ssq{i}", [128, 1], stack=st) for i in range(2)]
        junk = S.sbuf("s_junk", [128, 512], stack=st)
        ot = [S.sbuf(f"s_ot{i}", [128, 4, 128], BF16, stack=st) for i in range(2)]
        tiles = list(range(NT)) if ctx_out else list(range(16))
        for n, ti in enumerate(tiles):
            k = n % 2
            t0 = ti * 128
            S.dma("sp", zt[k].v, src[t0:t0 + 128, 0:512])
            S.act(zt[k].v, zt[k].v, AF.Silu)
            y = y2[k]
            S.tt("dve", y.v.rearrange("p (h e) -> p h e", h=8), xall[:, ti, 0:512].rearrange("p (h e) -> p h e", h=8),
                 bc_last(dsk.v, 64), ALU.mult)
            S.tt("dve", y.v, y.v, yacc[:, ti, :], ALU.add)
            S.tt("dve", y.v, y.v, zt[k].v, ALU.mult)
            S.memset("pool", ssq[k].v, 0.0)
            S.act(junk.v, y.v, AF.Square, accum=ssq[k].v)
            S.ts("dve", ssq[k].v, ssq[k].v, 1.0 / 512, ALU.mult, NORM_EPS, ALU.add)
            S.act(ssq[k].v, ssq[k].v, AF.Ln)
            S.act(ssq[k].v, ssq[k].v, AF.Exp, scale=-0.5)
            S.stt("dve", y.v, y.v, ssq[k].v, ngb.v, ALU.mult, ALU.mult)
            for q in range(4):
                S.transpose(P[0][:, q * 128:(q + 1) * 128], y[:, q * 128:(q + 1) * 128], C.ident.v)
            S.copy("act", ot[k].v, P[0].v.rearrange("p (q t) -> p q t", q=4))
            S.dma("pool", out[:, :, t0:t0 + 128], ot[k].v)
    S.barrier()


def tri_consts():
    tf = np.triu(np.ones((128, 128), np.float32))
    return tf, np.ascontiguousarray(tf.T)


GRID_W = 64
NROWS = 32


def na_tables(rpb):
    kc = np.arange(64)[:, None]
    c = np.arange(64)[None, :]
    ci = np.clip(kc - c, -15, 15) + 15
    col_start = np.clip(np.arange(64) - 8, 0, 48)
    in_win = (kc >= col_start[None, :]) & (kc < col_start[None, :] + 16)
    dr0 = np.arange(14)
    wl = np.arange(2)
    ri = dr0[None, :] + wl[:, None]
    b = rpb[:, :, ri[:, None, :, None], ci[None, :, None, :]]
    b = np.ascontiguousarray(b.reshape(rpb.shape[0], 8, 128, 14, 64), dtype=np.float32)
    m = np.broadcast_to(in_win[None, :, None, :], (2, 64, 14, 64)).reshape(128, 14 * 64)
    return b, np.ascontiguousarray(m, dtype=np.float32)


def stage_na(C, l, ctx_out):
    S = C.S
    d = C.d
    qk = d["na_qkT"]
    vd = d["na_v"]
    out = d["brT"][3]
    P = C.ps
    with ExitStack() as st:
        mask = S.sbuf("na_mask", [128, 14 * 64], stack=st)
        S.dma("sp", mask.v, d["na_mask"])
        qT = S.sbuf("na_q", [128, T], stack=st)
        kT = S.sbuf("na_k", [128, T], stack=st)
        qB = S.sbuf("na_qb", [128, T], BF16, stack=st)
        kB = S.sbuf("na_kb", [128, T], BF16, stack=st)
        ve = S.sbuf("na_ve", [128, 18, 128], BF16, stack=st)
        vo = S.sbuf("na_vo", [128, 15, 128], BF16, stack=st)
        kZ = [S.sbuf(f"na_kz{m}", [128, T], BF16, stack=st) for m in range(2)]
        S.memset("dve", kZ[0].v, 0.0)
        S.memset("dve", kZ[1].v, 0.0)
        ones64 = S.sbuf("na_1b", [128, 64], BF16, stack=st)
        S.memset("pool", ones64.v, 1.0)
        eb = [S.sbuf(f"na_eb{i}", [128, 14 * 64], stack=st) for i in range(2)]
        et = [S.sbuf(f"na_e{i}", [128, 384], BF16, stack=st) for i in range(4)]
        ef = [S.sbuf(f"na_ef{i}", [128, 256], stack=st) for i in range(4)]
        ec = [S.sbuf(f"na_ec{i}", [128, 256], BF16, stack=st) for i in range(2)]
        rr = [S.sbuf(f"na_r{i}", [64, 256], stack=st) for i in range(2)]
        ob = [S.sbuf(f"na_ob{i}", [64, T], BF16, stack=st) for i in range(2)]
        ei = 0
        ri_ = 0
        for hp in range(4):
            S.dma("sp", qT.v, qk[hp * 128:(hp + 1) * 128, :])
            S.dma("act", kT.v, qk[512 + hp * 128:512 + (hp + 1) * 128, :])
            S.dma("pool", ve.v, vd[:, hp * 128:(hp + 1) * 128].rearrange("(b p) e -> p b e", p=128))
            S.dma("pool", vo.v, vd[64:64 + 15 * 128, hp * 128:(hp + 1) * 128].rearrange("(b p) e -> p b e", p=128))
            S.copy("dve", qB.v, qT.v)
            S.copy("act", kB.v, kT.v)
            S.copy("act", kZ[0][0:64, :], kT[0:64, :])
            S.copy("dve", kZ[1][64:128, :], kT[64:128, :])
            for hh in range(2):
                h = hp * 2 + hh
                ebt = eb[h % 2]
                o = ob[h % 2]
                S.dma("sp", ebt.v, d["na_bias"][l][h].rearrange("p a c -> p (a c)"))
                S.act(ebt.v, ebt.v, AF.Exp)
                S.tt("dve", ebt.v, ebt.v, mask.v, ALU.mult)
                ebv = ebt.v.rearrange("p (a c) -> p a c", c=64)
                pl, ph = hh * 64, (hh + 1) * 64
                def emitA(r):
                    rs = min(max(r - 4, 0), NROWS - 8)
                    q = qB[:, r * 64:(r + 1) * 64]
                    ps = P[4 + r % 4]
                    e = et[r % 4]
                    f = ef[r % 4]
                    for i, w0 in enumerate((0, 2, 4, 6)):
                        kr = rs + w0
                        S.matmul(ps[:, i * 64:(i + 1) * 64], kZ[hh][:, kr * 64:kr * 64 + 128], q)
                    for cb in range(2):
                        S.matmul(ps[:, 256 + cb * 64:256 + (cb + 1) * 64], kZ[hh][:, NLAT + cb * 128:NLAT + (cb + 1) * 128], q)
                    dr0 = rs - r + 7
                    S.act(f.v, ps[:, 0:256], AF.Exp, scale=0.125)
                    S.act(e[:, 256:384], ps[:, 256:384], AF.Exp, scale=0.125)
                    S.tt("dve", e[:, 0:256].rearrange("p (a c) -> p a c", c=64), f.v.rearrange("p (a c) -> p a c", c=64),
                         ebv[:, dr0:dr0 + 7:2, :], ALU.mult)

                def emitB(r):
                    nonlocal ri_
                    rs = min(max(r - 4, 0), NROWS - 8)
                    acc = P[r % 2]
                    accs = P[2 + r % 2]
                    e = et[r % 4]
                    vts = []
                    for w0 in (0, 2, 4, 6):
                        kr = rs + w0
                        vts.append(ve[:, kr // 2, pl:ph] if kr % 2 == 0 else vo[:, (kr - 1) // 2, pl:ph])
                    for cb in range(2):
                        vts.append(ve[:, 16 + cb, pl:ph])
                    for i in range(6):
                        S.matmul(acc[0:64, 0:64], vts[i], e[:, i * 64:(i + 1) * 64], start=(i == 0), stop=(i == 5))
                    for i in range(6):
                        S.matmul(accs[0:64, 0:64], ones64.v, e[:, i * 64:(i + 1) * 64], start=(i == 0), stop=(i == 5))
                    rt = rr[ri_ % 2]
                    ri_ += 1
                    S.recip(rt[:, 0:64], accs[0:64, 0:64])
                    S.tt("dve", o[:, r * 64:(r + 1) * 64], acc[0:64, 0:64], rt[:, 0:64], ALU.mult)

                LA = 3
                for r in range(NROWS + LA):
                    if r < NROWS:
                        emitA(r)
                    if r - LA >= 0:
                        emitB(r - LA)
                if ctx_out:
                    acc = P[0]
                    accs = P[2]
                    q = qB[pl:ph, NLAT:T]
                    for cb in range(2):
                        ps = P[4 + ei % 4]
                        ei += 1
                        e = ec[cb]
                        S.matmul(ps[:, 0:256], kB[pl:ph, NLAT + cb * 128:NLAT + (cb + 1) * 128], q)
                        S.act(e.v, ps[:, 0:256], AF.Exp, scale=0.125)
                    for cb in range(2):
                        S.matmul(acc[0:64, 0:256], ve[:, 16 + cb, pl:ph], ec[cb].v, start=(cb == 0), stop=(cb == 1))
                    for cb in range(2):
                        S.matmul(accs[0:64, 0:256], ones64.v, ec[cb].v, start=(cb == 0), stop=(cb == 1))
                    rt = rr[ri_ % 2]
                    ri_ += 1
                    S.recip(rt.v, accs[0:64, 0:256])
                    S.tt("dve", o[:, NLAT:T], acc[0:64, 0:256], rt.v, ALU.mult)
                    S.dma("sp", out[h * 64:(h + 1) * 64, :], o.v)
                else:
                    S.dma("sp", out[h * 64:(h + 1) * 64, 0:NLAT], o[:, 0:NLAT])
    S.barrier()

import os
RW_STOP = os.environ.get('RW_STOP', '')
RW_NT = int(os.environ.get('RW_NT', '99'))
RW_LP = BF16 if os.environ.get('RW_LP', 'f32') == 'bf16' else F32

RW_GN_EPS = 64e-5
EXPM05 = 0.6065306597126334


def rw_consts():
    s = np.arange(128)[:, None]
    t = np.arange(128)[None, :]
    f = np.float32
    US = (s < t).astype(f)
    UF = (s <= t).astype(f)
    LS = (s > t).astype(f)
    LF = (s >= t).astype(f)
    I_ = np.eye(128, dtype=f)
    masks = np.stack([np.tile(m_, (1, 4)) for m_ in (US, UF, LS, LF, -US, -LS, I_)], 1)
    lo = (s <= 63).astype(f)
    hi = (s >= 64).astype(f)
    dq = np.stack([UF - lo, US - lo, LF - hi, LS - hi], 1)
    mvec = np.stack([np.concatenate([lo, hi], 1), np.concatenate([hi, lo], 1)], 1)
    return np.ascontiguousarray(masks), np.ascontiguousarray(dq), np.ascontiguousarray(mvec.astype(f))


class _OV:
    def __init__(self, views):
        self.views = views

    def __getitem__(self, idx):
        assert idx[0] == slice(None) and isinstance(idx[1], int)
        v = self.views[idx[1]]
        return v if idx[2] == slice(None) else v[:, idx[2]]


def rw_phase1(C, l):
    S = C.S
    d = C.d
    src = d["p_rw"]
    prep = d["rw_prep"]
    P = C.ps
    with ExitStack() as st:
        mu = S.sbuf("rw_mu", [128, 3, 1920], stack=st)
        load_bcast(S, "sp", mu[:, 0:2, :], d["rw_mu"][l])
        S.tt("dve", mu[:, 2, :], mu[:, 0, :], mu[:, 1, :], ALU.add)
        S.ts("dve", mu[:, 2, :], mu[:, 2, :], -1.0, ALU.mult, 1.0, ALU.add)
        w0b = S.sbuf("rw_w0b", [128, 2, 512], stack=st)
        a0b = S.sbuf("rw_a0b", [128, 2, 512], stack=st)
        load_bcast(S, "pool", w0b.v, d["rw_w0"][l])
        load_bcast(S, "pool", a0b.v, d["rw_a0"][l])
        kkb = S.sbuf("rw_kkb", [128, 512], stack=st)
        kab = S.sbuf("rw_kab", [128, 512], stack=st)
        rkb = S.sbuf("rw_rkb", [128, 512], stack=st)
        load_bcast(S, "sp", kkb.v, d["rw_k_k"][l])
        load_bcast(S, "sp", kab.v, d["rw_k_a"][l])
        load_bcast(S, "sp", rkb.v, d["rw_r_k"][l])
        wup = S.sbuf("rw_wup", [128, 512], stack=st)
        aup = S.sbuf("rw_aup", [128, 512], stack=st)
        gup = S.sbuf("rw_gup", [128, 512], stack=st)
        S.dma("sp", wup.v, d["rw_w_up"][l])
        S.dma("sp", aup.v, d["rw_a_up"][l])
        S.dma("sp", gup.v, d["rw_g_up"][l])
        sh = [[S.sbuf(f"rw_sh{i}{j}", [128, 1920], stack=st) for j in range(3)] for i in range(2)]
        sb = [S.sbuf(f"rw_s{i}", [128, 1920], stack=st) for i in range(2)]
        t1 = S.sbuf("rw_t1", [128, 1920], stack=st)
        th = S.sbuf("rw_th", [128, 2, 128], stack=st)
        thT = S.sbuf("rw_thT", [128, 3, 128], stack=st)
        ot = [S.sbuf(f"rw_ot{i}", [128, 11, 512], stack=st) for i in range(2)]
        otv = [S.subviews(t_, 11) for t_ in ot]
        wk = [S.sbuf(f"rw_wk{i}", [128, 512], stack=st) for i in range(8)]
        sm = [S.sbuf(f"rw_sm{i}", [128, 8], stack=st) for i in range(4)]

        def h8(v):
            return v.rearrange("p (h e) -> p h e", h=8)

        for ti in range(NT):
            t0 = ti * 128
            tl = sh[ti % 2]
            s = sb[ti % 2]
            oT = ot[ti % 2]
            o = _OV(otv[ti % 2])
            load_shifted(S, tl, src, t0, 1920, 0, ti in SEG_FIRST, ti in SEG_LAST)
            S.tt("dve", s.v, tl[1].v, mu[:, 2, :], ALU.mult)
            S.tt("dve", t1.v, tl[0].v, mu[:, 0, :], ALU.mult)
            S.tt("dve", s.v, s.v, t1.v, ALU.add)
            S.tt("dve", t1.v, tl[2].v, mu[:, 1, :], ALU.mult)
            S.tt("dve", s.v, s.v, t1.v, ALU.add)
            r, k, v = s[:, 0:512], s[:, 512:1024], s[:, 1024:1536]
            S.copy("act", o[:, 0, :], r)
            S.copy("act", o[:, 1, :], v)
            S.act(th[:, 0, :], s[:, 1536:1664], AF.Tanh)
            S.act(th[:, 1, :], s[:, 1792:1920], AF.Sigmoid)
            S.transpose(P[0][:, 0:128], th[:, 0, :], C.ident.v)
            S.transpose(P[0][:, 128:256], s[:, 1664:1792], C.ident.v)
            S.transpose(P[0][:, 256:384], th[:, 1, :], C.ident.v)
            S.copy("dve", thT.v, P[0][:, 0:384].rearrange("p (q t) -> p q t", q=3))
            a_t = [wk[0], wk[1]]
            for dr in range(2):
                pl, ph = dr * 64, (dr + 1) * 64
                S.matmul(P[1 + dr].v, thT[pl:ph, 0, :], wup[pl:ph, :])
                lw = o[:, 5 + 3 * dr, :]
                S.tt("dve", lw, P[1 + dr].v, w0b[:, dr, :], ALU.add)
                S.act(lw, lw, AF.Sigmoid)
                S.ts("dve", lw, lw, -EXPM05, ALU.mult)
                S.matmul(P[3 + dr].v, thT[pl:ph, 1, :], aup[pl:ph, :])
                S.tt("dve", a_t[dr].v, P[3 + dr].v, a0b[:, dr, :], ALU.add)
                S.act(a_t[dr].v, a_t[dr].v, AF.Sigmoid)
            S.matmul(P[5].v, thT[:, 2, :], gup.v)
            S.copy("act", o[:, 3, :], P[5].v)
            kk = o[:, 2, :]
            S.tt("dve", kk, k, kkb.v, ALU.mult)
            S.tt("dve", wk[2].v, kk, kk, ALU.mult)
            S.reduce("dve", sm[0].v, h8(wk[2].v), ALU.add)
            S.act(sm[0].v, sm[0].v, AF.Sqrt)
            S.ts("dve", sm[0].v, sm[0].v, 1e-12, ALU.max)
            S.recip(sm[0].v, sm[0].v)
            S.tt("dve", h8(kk), h8(kk), bc_last(sm[0].v, 64), ALU.mult)
            S.tt("dve", wk[3].v, r, rkb.v, ALU.mult)
            for dr in range(2):
                kd = o[:, 6 + 3 * dr, :]
                bb = o[:, 7 + 3 * dr, :]
                S.stt("dve", wk[4 + dr].v, a_t[dr].v, -1.0, kab.v, ALU.add, ALU.mult)
                S.stt("dve", kd, wk[4 + dr].v, 1.0, k, ALU.add, ALU.mult)
                S.tt("dve", bb, kk, a_t[dr].v, ALU.mult)
                S.tt("dve", wk[6 + dr].v, wk[3].v, kd, ALU.mult)
                S.reduce("dve", sm[1 + dr].v, h8(wk[6 + dr].v), ALU.add)
            S.tt("dve", sm[3].v, sm[1].v, sm[2].v, ALU.add)
            S.tt("dve", h8(o[:, 4, :]), h8(v), bc_last(sm[3].v, 64), ALU.mult)
            S.dma("act", prep[t0:t0 + 128, :, :], V(oT.h[:], oT.buf), extra_reads=[v_.buf for v_ in otv[ti % 2]])
    S.barrier()


def run_interleaved(gens):
    gens = list(gens)
    while gens:
        for g in list(gens):
            try:
                next(g)
            except StopIteration:
                gens.remove(g)


def rw_phase2(C, l):
    S = C.S
    d = C.d
    prep = d["rw_prep"]
    P = C.ps
    with ExitStack() as st:
        msk = S.sbuf("rw_msk", [128, 7, 512], stack=st)
        dqm = S.sbuf("rw_dq", [128, 4, 128], stack=st)
        mv = S.sbuf("rw_mv", [128, 2, 2], stack=st)
        S.dma("sp", msk.v, d["rw_masks"])
        S.dma("sp", dqm.v, d["rw_dqc"])
        S.dma("sp", mv.v, d["rw_mvec"])
        St = S.sbuf("rw_St", [128, 2, 4, 64], stack=st)
        S.memset("dve", St.v, 0.0)
        R_ = []
        for dr in range(2):
            def mk(nm, w=512, n=2, dt_=F32):
                return [S.sbuf(f"rw_{nm}{dr}{h}", [128, w], dt_, stack=st) for h in range(n)]
            LP = RW_LP
            res = dict(
                S0s=S.sbuf(f"rw_S0s{dr}", [128, 4, 64], stack=st),
                inp=[S.sbuf(f"rw_in{dr}{i}", [128, 6, 512], stack=st) for i in range(2)],
                ex=mk("ex", 512, 3), tm=mk("tm", 512, 4),
                tT=[S.sbuf(f"rw_tT{dr}{j}", [128, 4, 128], LP, stack=st) for j in range(4)],
                eh=S.sbuf(f"rw_eh{dr}", [128, 4, 2], stack=st),
                Q=[mk("Qa", dt_=LP), mk("Qb", dt_=LP)], R=[mk("Ra", dt_=LP), mk("Rb", dt_=LP)],
                Y=[mk("Ya", dt_=LP), mk("Yb", dt_=LP)],
                BmT=mk("BmT", dt_=LP), AbT=mk("AbT", dt_=LP), AkT=mk("AkT", dt_=LP), Wsb=mk("Wsb", 256, dt_=LP),
                Usb=S.sbuf(f"rw_U{dr}", [128, 4, 128], LP, stack=st),
                ob=S.sbuf(f"rw_ob{dr}", [128, 512], stack=st),
                lpc=(S.sbuf(f"rw_lpc{dr}", [128, 3, 512], LP, stack=st) if LP is not F32 else None),
                S0b=(S.sbuf(f"rw_S0b{dr}", [128, 4, 64], LP, stack=st) if LP is not F32 else None),
            )
            R_.append(res)
        bc_ = [0]

        def bank():
            b_ = P[4 + bc_[0] % 4]
            bc_[0] += 1
            return b_

        ev = [0]

        def evac_copy(dst, src):
            ev[0] += 1
            S.copy("act" if ev[0] % 2 else "dve", dst, src)

        def q4(v):
            return v.rearrange("p (q t) -> p q t", q=4)

        def hinfo(h):
            hp, hh = h // 2, h % 2
            return hp, hh, hh * 64, (hh + 1) * 64

        def dir_gen(dr):
            rs_ = R_[dr]
            PA, PB = P[2 * dr], P[2 * dr + 1]
            S0s, Xb, ex, eh, Usb = rs_["S0s"], rs_["inp"], rs_["ex"], rs_["eh"], rs_["Usb"]
            Q, R, Y, BmT, AbT, AkT, Wsb = (rs_[k_] for k_ in ("Q", "R", "Y", "BmT", "AbT", "AkT", "Wsb"))
            order = [16, 17] + list(range(16)) if dr == 0 else [17, 16] + list(range(15, -1, -1))
            if dr == 0:
                mS, mF, mSn, mAn = msk[:, 0, :], msk[:, 1, :], msk[:, 4, :], msk[:, 5, :]
            else:
                mS, mF, mSn, mAn = msk[:, 2, :], msk[:, 3, :], msk[:, 5, :], msk[:, 4, :]
            order = order[:RW_NT]

            def load_x(idx):
                tj = order[idx]
                Xn = Xb[idx % 2]
                S.dma("sp", Xn[:, 0:3, :], prep[tj * 128:tj * 128 + 128, 0:3, :])
                S.dma("act", Xn[:, 3:6, :], prep[tj * 128:tj * 128 + 128, 5 + 3 * dr:8 + 3 * dr, :])

            load_x(0)
            for idx, ti in enumerate(order):
                t0 = ti * 128
                X = Xb[idx % 2]
                if idx + 1 < len(order):
                    load_x(idx + 1)
                r_, v_, kk_, lw_, kd_, b_ = (X[:, j, :] for j in range(6))
                S.matmul(PA.v, dqm[:, 2 * dr, :], lw_)
                S.matmul(PB.v, dqm[:, 2 * dr + 1, :], lw_)
                S.act(ex[0].v, PA.v, AF.Exp)
                S.act(ex[1].v, PA.v, AF.Exp, scale=-1.0)
                S.act(ex[2].v, PB.v, AF.Exp)
                rq, kq, bn, kn = rs_["tm"]
                S.tt("dve", rq.v, r_, ex[0].v, ALU.mult)
                S.tt("dve", kq.v, kk_, ex[2].v, ALU.mult)
                S.tt("dve", bn.v, b_, ex[1].v, ALU.mult)
                S.tt("dve", kn.v, kd_, ex[1].v, ALU.mult)
                lpc = rs_["lpc"]
                S0b = rs_["S0b"]
                if RW_LP is F32:
                    vB, bnB, knB = v_, bn.v, kn.v
                else:
                    S.copy("act", lpc[:, 0, :], v_)
                    S.copy("act", lpc[:, 1, :], bn.v)
                    S.copy("act", lpc[:, 2, :], kn.v)
                    vB, bnB, knB = lpc[:, 0, :], lpc[:, 1, :], lpc[:, 2, :]
                yield
                rqT, kqT, bnT, knT = rs_["tT"]
                for j, (src_, dst_) in enumerate(((rq, rqT), (kq, kqT), (bn, bnT), (kn, knT))):
                    pb = (PA, PB)[j % 2]
                    for hp in range(4):
                        S.transpose(pb[:, hp * 128:(hp + 1) * 128], src_[:, hp * 128:(hp + 1) * 128], C.ident.v)
                    evac_copy(dst_.v, q4(pb.v))
                eg = bank()
                for hp in range(4):
                    S.matmul(eg[:, hp * 2:(hp + 1) * 2], lw_[:, hp * 128:(hp + 1) * 128], mv[:, dr, :])
                S.act(eh.v, eg[:, 0:8].rearrange("p (q c) -> p q c", c=2), AF.Exp)
                for hp in range(4):
                    S.ts("dve", S0s[:, hp, :], St[:, dr, hp, :], eh[:, hp, 0:1], ALU.mult)
                if RW_LP is F32:
                    S0m = S0s
                else:
                    S.copy("act", S0b.v, S0s.v)
                    S0m = S0b
                yield
                specs = ((bnT, kqT, Q[0], mSn), (kqT, bnT, R[0], mAn), (knT, kqT, BmT, mS),
                         (bnT, rqT, AbT, mF), (knT, rqT, AkT, mF))
                for (LT, RT, dsts, mk_) in specs:
                    gs_ = (bank(), bank())
                    for i in range(4):
                        for half in range(2):
                            hp, hh, pl, ph = hinfo(2 * i + half)
                            S.matmul(gs_[half][:, i * 128:(i + 1) * 128], LT[pl:ph, hp, :], RT[pl:ph, hp, :])
                    for half in range(2):
                        S.tt("dve", dsts[half].v, gs_[half].v, mk_, ALU.mult)
                for half in range(2):
                    S.tt("dve", Y[0][half].v, Q[0][half].v, msk[:, 6, :], ALU.add)
                yield
                for half in range(2):
                    g = bank()
                    for i, h in enumerate([2 * i_ + half for i_ in range(4)]):
                        hp, hh, pl, ph = hinfo(h)
                        S.matmul(g[:, i * 64:(i + 1) * 64], kqT[pl:ph, hp, :], S0m[pl:ph, hp, :], start=True, stop=False)
                        S.matmul(g[:, i * 64:(i + 1) * 64], BmT[half][:, i * 128:(i + 1) * 128], vB[:, h * 64:(h + 1) * 64],
                                 start=False, stop=True)
                    evac_copy(Wsb[half].v, g[:, 0:256])
                yield
                cur = 0
                for lev in range(1, 7):
                    nxt = 1 - cur
                    for half in range(2):
                        if lev < 6:
                            g = bank()
                            for i in range(4):
                                sl = slice(i * 128, (i + 1) * 128)
                                S.matmul(g[:, sl], R[cur][half][:, sl], Q[cur][half][:, sl])
                            evac_copy(Q[nxt][half].v, g.v)
                        g = bank()
                        for i in range(4):
                            sl = slice(i * 128, (i + 1) * 128)
                            S.matmul(g[:, sl], Q[cur][half][:, sl], R[cur][half][:, sl])
                        evac_copy(R[nxt][half].v, g.v)
                    yield
                    for half in range(2):
                        g = bank()
                        for i in range(4):
                            sl = slice(i * 128, (i + 1) * 128)
                            S.matmul(g[:, sl], R[nxt][half][:, sl], Y[cur][half][:, sl])
                        S.tt("dve", Y[nxt][half].v, Y[cur][half].v, g.v, ALU.add)
                    cur = nxt
                    yield
                for half in range(2):
                    g = bank()
                    for i in range(4):
                        S.matmul(g[:, i * 64:(i + 1) * 64], Y[cur][half][:, i * 128:(i + 1) * 128], Wsb[half][:, i * 64:(i + 1) * 64])
                    S.ts("dve", Usb[:, :, half * 64:(half + 1) * 64], g[:, 0:256].rearrange("p (a b) -> p a b", a=4), -1.0, ALU.mult)
                yield
                for h in range(8):
                    hp, hh, pl, ph = hinfo(h)
                    half, i = h % 2, h // 2
                    oc = PB[:, h * 64:(h + 1) * 64]
                    S.matmul(oc, rqT[pl:ph, hp, :], S0m[pl:ph, hp, :], start=True, stop=False)
                    S.matmul(oc, AbT[half][:, i * 128:(i + 1) * 128], Usb[:, hp, hh * 64:(hh + 1) * 64], start=False, stop=False)
                    S.matmul(oc, AkT[half][:, i * 128:(i + 1) * 128], vB[:, h * 64:(h + 1) * 64], start=False, stop=True)
                S.copy("act", rs_["ob"].v, PB.v)
                S.dma("pool", d["rw_o"][dr, t0:t0 + 128, :], rs_["ob"].v)
                yield
                g = bank()
                for hp in range(4):
                    sl = slice(hp * 128, (hp + 1) * 128)
                    S.matmul(g[:, sl], bnB[:, sl], Usb[:, hp, :], start=True, stop=False)
                    S.matmul(g[:, sl], knB[:, sl], vB[:, sl], start=False, stop=True)
                for hp in range(4):
                    for hh in range(2):
                        pl, ph = hh * 64, (hh + 1) * 64
                        S.tt("dve", St[pl:ph, dr, hp, :], S0s[pl:ph, hp, :], g[pl:ph, hp * 128 + hh * 64:hp * 128 + (hh + 1) * 64], ALU.add)
                for hp in range(4):
                    S.ts("dve", St[:, dr, hp, :], St[:, dr, hp, :], eh[:, hp, 1:2], ALU.mult)
                yield

        run_interleaved([dir_gen(0), dir_gen(1)])
    S.barrier()
    with ExitStack() as st:
        rw_phase3(C, l, st, None)
    S.barrier()


def rw_phase3(C, l, st, oacc):
    S = C.S
    d = C.d
    prep = d["rw_prep"]
    out = d["brT"][0].rearrange("(k p) t -> p k t", p=128)
    P = C.ps
    lg = S.sbuf("rw_lg", [128, 2, 512], stack=st)
    load_bcast(S, "sp", lg[:, 0, :], d["rw_ln_g"][l])
    load_bcast(S, "sp", lg[:, 1, :], d["rw_ln_b"][l])
    gb = [S.sbuf(f"rw_gb{i}", [128, 2, 512], stack=st) for i in range(2)]
    cen = [S.sbuf(f"rw_cen{i}", [128, 512], stack=st) for i in range(2)]
    sq = S.sbuf("rw_sq3", [128, 512], stack=st)
    mn = [S.sbuf(f"rw_mn{i}", [128, 8], stack=st) for i in range(2)]
    vr = [S.sbuf(f"rw_vr{i}", [128, 8], stack=st) for i in range(2)]
    ot = [S.sbuf(f"rw_o3{i}", [128, 4, 128], BF16, stack=st) for i in range(2)]

    def h8(v):
        return v.rearrange("p (h e) -> p h e", h=8)

    tiles = C.rw_tiles
    of = [S.sbuf(f"rw_of{i}", [128, 2, 512], stack=st) for i in range(2)]
    for n, ti in enumerate(tiles):
        k = n % 2
        t0 = ti * 128
        S.dma("sp", gb[k].v, prep[t0:t0 + 128, 3:5, :])
        S.dma("act", of[k][:, 0, :], d["rw_o"][0, t0:t0 + 128, :])
        S.dma("act", of[k][:, 1, :], d["rw_o"][1, t0:t0 + 128, :])
        S.tt("dve", of[k][:, 0, :], of[k][:, 0, :], of[k][:, 1, :], ALU.add)
        o = of[k][:, 0, :]
        S.reduce("dve", mn[k].v, h8(o), ALU.add)
        S.ts("dve", mn[k].v, mn[k].v, 1.0 / 64, ALU.mult)
        c = cen[k]
        S.tt("dve", h8(c.v), h8(o), bc_last(mn[k].v, 64), ALU.subtract)
        S.act(sq.v, c.v, AF.Square)
        S.reduce("dve", vr[k].v, h8(sq.v), ALU.add)
        S.ts("dve", vr[k].v, vr[k].v, 1.0 / 64, ALU.mult, RW_GN_EPS, ALU.add)
        S.act(vr[k].v, vr[k].v, AF.Ln)
        S.act(vr[k].v, vr[k].v, AF.Exp, scale=-0.5)
        S.tt("dve", h8(c.v), h8(c.v), bc_last(vr[k].v, 64), ALU.mult)
        S.tt("dve", c.v, c.v, lg[:, 0, :], ALU.mult)
        S.tt("dve", c.v, c.v, lg[:, 1, :], ALU.add)
        S.tt("dve", c.v, c.v, gb[k][:, 1, :], ALU.add)
        S.tt("dve", c.v, c.v, gb[k][:, 0, :], ALU.mult)
        for q in range(4):
            S.transpose(P[0][:, q * 128:(q + 1) * 128], c[:, q * 128:(q + 1) * 128], C.ident.v)
        S.copy("act", ot[k].v, P[0].v.rearrange("p (q t) -> p q t", q=4))
        S.dma("sp", out[:, :, t0:t0 + 128], ot[k].v)


def stage_rwkv(C, l, ctx_out):
    C.rw_tiles = list(range(NT)) if ctx_out else list(range(16))
    rw_phase1(C, l)
    rw_phase2(C, l)


DEPTH = 2
IN_TOTAL = 6544

INPUT_SHAPES = {
    "xT": [D_MODEL, T],
    "cc": [128, 16, 2],
    "ident": [128, 128],
    "ada_w": [DEPTH, D_MODEL, 6 * D_MODEL],
    "ada_b": [DEPTH, 128, 96],
    "norm1_g": [DEPTH, 128, 16],
    "norm2_g": [DEPTH, 128, 16],
    "w_in": [DEPTH, D_MODEL, IN_TOTAL],
    "w_gate": [DEPTH, 4, D_MODEL, D_MODEL],
    "gate_b": [DEPTH, 128, 4, 16],
    "w_br": [DEPTH, 4, 512, D_MODEL],
    "w_out": [DEPTH, D_MODEL, D_MODEL],
    "ffn_up": [DEPTH, D_MODEL, 2 * D_FF],
    "ffn_conv_w": [DEPTH, 128, 3, 88],
    "ffn_conv_b": [DEPTH, 128, 88],
    "ffn_down": [DEPTH, 16, 128, 44, 128],
    "final_norm_g": [128, 16],
    "rope_cos": [128, NLAT],
    "rope_sin": [128, NLAT],
    "rope_perm": [128, 128],
    "triF": [128, 128],
    "triB": [128, 128],
    "ssd_conv_w": [DEPTH, 3, 1024],
    "ssd_conv_b": [DEPTH, 1024],
    "ssd_dt_bias": [DEPTH, 16],
    "ssd_a_log": [DEPTH, 16],
    "ssd_d": [DEPTH, 8],
    "ssd_norm_g": [DEPTH, 512],
    "rw_mu": [DEPTH, 2, 1920],
    "rw_w0": [DEPTH, 2, 512],
    "rw_a0": [DEPTH, 2, 512],
    "rw_k_k": [DEPTH, 512],
    "rw_k_a": [DEPTH, 512],
    "rw_r_k": [DEPTH, 512],
    "rw_w_up": [DEPTH, 128, 512],
    "rw_a_up": [DEPTH, 128, 512],
    "rw_g_up": [DEPTH, 128, 512],
    "rw_ln_g": [DEPTH, 512],
    "rw_ln_b": [DEPTH, 512],
    "rw_masks": [128, 7, 512],
    "rw_dqc": [128, 4, 128],
    "rw_mvec": [128, 2, 2],
    "na_bias": [DEPTH, 8, 128, 14, 64],
    "na_mask": [128, 14 * 64],
    "da_lambda": [DEPTH, 4, 64],
    "da_subln_g": [DEPTH, 128, 1],
}

SCRATCH_SHAPES = {
    "hT": [D_MODEL, T],
    "p_rw": [T, 1920],
    "p_ssd": [T, 1552],
    "da_qkT": [1024, T],
    "da_v": [T, 512],
    "na_qkT": [1024, T],
    "na_v": [T, 512],
    "modout": [128, 192],
    "rw_prep": [T, 11, 512],
    "rw_o": [2, T, 512],
    "brT": [4, 512, T],
    "mgT": [D_MODEL, T],
    "guT": [D_FF, T],
    "xres": [D_MODEL, T],
    "outT": [D_MODEL, NLAT],
}


ANNOTATE = False
BF16_SCRATCH = {"hT", "brT", "mgT", "guT"}


def host_inputs(inp, b):
    f = np.float32
    o = {}
    o["xT"] = np.ascontiguousarray(np.concatenate([inp["x"][b].T, inp["ctx"][b].T], axis=1), dtype=f)
    cc = np.stack([inp["c"][b], inp["c_ctx"]], axis=-1)
    o["cc"] = np.ascontiguousarray(cc.reshape(16, 128, 2).transpose(1, 0, 2), dtype=f)
    o["ident"] = np.eye(128, dtype=f)
    o["ada_w"] = inp["ada_w"]
    o["ada_b"] = np.ascontiguousarray(inp["ada_b"].reshape(DEPTH, 96, 128).transpose(0, 2, 1), dtype=f)
    o["norm1_g"] = np.ascontiguousarray(inp["norm1_g"].reshape(DEPTH, 16, 128).transpose(0, 2, 1), dtype=f)
    o["norm2_g"] = np.ascontiguousarray(inp["norm2_g"].reshape(DEPTH, 16, 128).transpose(0, 2, 1), dtype=f)
    o["w_in"] = inp["w_in"]
    o["w_gate"] = inp["w_gate"]
    o["gate_b"] = np.ascontiguousarray(inp["gate_b"].reshape(DEPTH, 4, 16, 128).transpose(0, 3, 1, 2), dtype=f)
    o["w_br"] = inp["w_br"]
    o["w_out"] = inp["w_out"]
    fu = inp["ffn_up"].reshape(DEPTH, D_MODEL, 2, 44, 128).transpose(0, 1, 3, 2, 4)
    o["ffn_up"] = np.ascontiguousarray(fu.reshape(DEPTH, D_MODEL, 2 * D_FF), dtype=f)
    o["ffn_conv_w"] = np.ascontiguousarray(inp["ffn_conv_w"].reshape(DEPTH, 3, 88, 128).transpose(0, 3, 1, 2), dtype=f)
    o["ffn_conv_b"] = np.ascontiguousarray(inp["ffn_conv_b"].reshape(DEPTH, 88, 128).transpose(0, 2, 1), dtype=f)
    o["ffn_down"] = np.ascontiguousarray(inp["ffn_down"].reshape(DEPTH, 44, 128, 16, 128).transpose(0, 3, 2, 1, 4), dtype=f)
    o["rope_cos"], o["rope_sin"], o["rope_perm"] = rope_tables()
    o["triF"], o["triB"] = tri_consts()
    o["ssd_conv_w"] = inp["ssd_conv_w"]
    o["ssd_conv_b"] = inp["ssd_conv_b"]
    o["ssd_dt_bias"] = np.ascontiguousarray(inp["ssd_dt_bias"].reshape(DEPTH, 16), dtype=f)
    o["ssd_a_log"] = np.ascontiguousarray(inp["ssd_a_log"].reshape(DEPTH, 16), dtype=f)
    o["ssd_d"] = inp["ssd_d"]
    o["ssd_norm_g"] = inp["ssd_norm_g"]
    for k_ in ["rw_mu", "rw_w0", "rw_a0", "rw_k_k", "rw_k_a", "rw_ln_g", "rw_ln_b"]:
        o[k_] = inp[k_]
    o["rw_r_k"] = np.ascontiguousarray(inp["rw_r_k"].reshape(DEPTH, 512), dtype=f)
    o["rw_w_up"] = np.ascontiguousarray(inp["rw_w_up"].reshape(DEPTH, 128, 512), dtype=f)
    o["rw_a_up"] = np.ascontiguousarray(inp["rw_a_up"].reshape(DEPTH, 128, 512), dtype=f)
    o["rw_g_up"] = inp["rw_g_up"]
    o["rw_masks"], o["rw_dqc"], o["rw_mvec"] = rw_consts()
    o["na_bias"], o["na_mask"] = na_tables(inp["na_rpb"])
    o["da_lambda"] = inp["da_lambda"]
    o["da_subln_g"] = np.ascontiguousarray(inp["da_subln_g"].reshape(DEPTH, 128, 1), dtype=f)
    o["final_norm_g"] = np.ascontiguousarray(inp["final_norm_g"].reshape(16, 128).T, dtype=f)
    return o


def make_program(stage_fn, ext_in, ext_out):
    nc = bass.Bass("TRN2", target_bir_lowering=False)
    C = Ctx()
    C.nc = nc
    C.d = {}
    allshapes = dict(INPUT_SHAPES)
    allshapes.update(SCRATCH_SHAPES)
    for name, shp in allshapes.items():
        if name in ext_in:
            kind = "ExternalInput"
        elif name in ext_out:
            kind = "ExternalOutput"
        elif name in SCRATCH_SHAPES:
            kind = "Internal"
        else:
            continue
        dtp = BF16 if name in BF16_SCRATCH else F32
        C.d[name] = nc.dram_tensor(name, list(shp), dtp, kind=kind).ap()
    S = Sched(nc)
    S.annotate = ANNOTATE
    C.S = S
    with S.stack:
        init_common(C)
        stage_fn(C)
        S.barrier()
        S.emit()
    return nc, S


def _tagged(C, name, fn, *a, **k):
    C.S.stage = name
    fn(C, *a, **k)
    C.S.stage = None


def full_stages(C):
    d = C.d
    S = C.S
    for l in range(DEPTH):
        last = (l == DEPTH - 1)
        ctx_out = not last
        _tagged(C, f"L{l}_mod", stage_mod, l)
        xsrc = d["xT"] if l == 0 else d["xres"]
        _tagged(C, f"L{l}_norm1", stage_norm, 1, xsrc, d["hT"], norm_chunks(True))
        _tagged(C, f"L{l}_win", stage_win, l)
        _tagged(C, f"L{l}_rwkv", stage_rwkv, l, ctx_out)
        _tagged(C, f"L{l}_ssd", stage_ssd, l, ctx_out)
        _tagged(C, f"L{l}_da", stage_da, l, ctx_out)
        _tagged(C, f"L{l}_na", stage_na, l, ctx_out)
        _tagged(C, f"L{l}_merge", stage_merge, l, ctx_out)
        _tagged(C, f"L{l}_wout", stage_wout, l, ctx_out)
        _tagged(C, f"L{l}_norm2", stage_norm, 2, d["xres"], d["hT"], norm_chunks(ctx_out))
        _tagged(C, f"L{l}_ffnup", stage_ffn_up, l, ctx_out)
        _tagged(C, f"L{l}_ffndn", stage_ffn_down, l, ctx_out)
    with ExitStack() as st:
        fg = S.sbuf("fin_g", [128, 16], stack=st)
        S.dma("sp", fg.v, d["final_norm_g"])
        stage_norm(C, 0, d["xres"], d["outT"], [(c0, w, 0) for (c0, w) in chunks(NLAT, 256)], final_g=fg)


_PROG = {}


def kernel(**inputs):
    inp = {k: np.asarray(v) for k, v in inputs.items()}
    n = 8
    shared = None
    in_maps = []
    for b in range(n):
        hi = host_inputs(inp, b) if shared is None else None
        if shared is None:
            shared = hi
            in_maps.append(hi)
        else:
            m = dict(shared)
            m["xT"] = np.ascontiguousarray(np.concatenate([inp["x"][b].T, inp["ctx"][b].T], axis=1), dtype=np.float32)
            cc = np.stack([inp["c"][b], inp["c_ctx"]], axis=-1)
            m["cc"] = np.ascontiguousarray(cc.reshape(16, 128, 2).transpose(1, 0, 2), dtype=np.float32)
            in_maps.append(m)
    if "nc" not in _PROG:
        nc, S = make_program(full_stages, set(INPUT_SHAPES.keys()), {"outT"})
        _PROG["nc"] = nc
    nc = _PROG["nc"]
    res = run_bass_kernel_spmd(nc, in_maps, core_ids=list(range(n)))
    out = np.stack([np.ascontiguousarray(res.results[b]["outT"].T) for b in range(n)], axis=0)
    return out.astype(np.float32)
```

```python
import numpy as np
from contextlib import ExitStack
import concourse.bass as bass
import concourse.mybir as mybir
from concourse.bass_utils import run_bass_kernel_spmd

F32 = mybir.dt.float32
BF16 = mybir.dt.bfloat16
ALU = mybir.AluOpType
AF = mybir.ActivationFunctionType
AX = mybir.AxisListType


class Buf:
    __slots__ = ("name", "last_w", "last_w_deps", "readers", "dsem", "dcnt", "is_psum")

    def __init__(self, name):
        self.name = name
        self.last_w = None
        self.last_w_deps = []
        self.readers = []
        self.dsem = None
        self.dcnt = 0
        self.is_psum = False


class V:
    __slots__ = ("ap", "buf")

    def __init__(self, ap, buf):
        self.ap = ap
        self.buf = buf

    def __getitem__(self, idx):
        return V(self.ap[idx], self.buf)

    def rearrange(self, s, **kw):
        return V(self.ap.rearrange(s, **kw), self.buf)

    def bc(self, shape):
        return V(self.ap.broadcast_to(list(shape)), self.buf)

    @property
    def shape(self):
        return self.ap.shape


class Tile:
    def __init__(self, handle, buf):
        self.h = handle
        self.buf = buf

    def __getitem__(self, idx):
        return V(self.h[idx], self.buf)

    @property
    def v(self):
        return V(self.h[:], self.buf)


class Op:
    __slots__ = ("eng", "fn", "deps", "signal", "dma_buf", "dma_cnt", "dma_sem", "idx", "val", "waits", "tag")

    def __init__(self, eng, fn):
        self.eng = eng
        self.fn = fn
        self.deps = []
        self.signal = False
        self.dma_buf = None
        self.dma_cnt = 0
        self.dma_sem = -1
        self.idx = -1
        self.val = 0
        self.waits = []


ENGS = ("pe", "act", "dve", "pool", "sp")


class Sched:
    def __init__(self, nc):
        self.nc = nc
        self.ops = {e: [] for e in ENGS}
        self.stack = ExitStack()
        self.bufs = []
        self.dma_pending = {}
        self.sems = {}
        self.nsem = 0
        self.sem_tot = []
        self.sem_free = []
        self.annotate = False
        self.stage = None

    def sbuf(self, name, shape, dtype=F32, stack=None):
        st = stack if stack is not None else self.stack
        h = st.enter_context(self.nc.sbuf_tensor(self._uniq("s_" + name), list(shape), dtype))
        b = Buf(name)
        self.bufs.append(b)
        return Tile(h, b)

    def psum(self, name, shape, dtype=F32, stack=None):
        st = stack if stack is not None else self.stack
        h = st.enter_context(self.nc.psum_tensor(self._uniq("q_" + name), list(shape), dtype))
        b = Buf(name)
        b.is_psum = True
        self.bufs.append(b)
        return Tile(h, b)

    def subviews(self, tile, n, psum=False):
        out = []
        for j in range(n):
            b = Buf(f"{tile.buf.name}_{j}")
            b.is_psum = psum
            self.bufs.append(b)
            out.append(V(tile.h[:, j], b))
        return out

    def _uniq(self, name):
        self.uid = getattr(self, "uid", 0) + 1
        return f"{name}_{self.uid}"

    def _sem(self, name):
        s = self.stack.enter_context(self.nc.semaphore(self._uniq(name)))
        self.nsem += 1
        return s

    def _add(self, eng, fn, reads, writes, dma_buf=None):
        op = Op(eng, fn)
        op.tag = getattr(self, "stage", None)
        op.idx = len(self.ops[eng])
        is_dma = dma_buf is not None
        deps = []
        raw = set()
        for b in reads:
            if b is None:
                continue
            if b.last_w is not None:
                deps.append(b.last_w)
                raw.add(id(b.last_w))
            if b.is_psum:
                for rd_ in b.readers:
                    if rd_.eng != eng:
                        deps.append(rd_)
        wdeps = {}
        for b in writes:
            if b is None:
                continue
            mine = []
            if b.last_w is not None:
                lw = b.last_w
                if is_dma and lw.dma_buf is not None:
                    mine.extend(b.last_w_deps)
                else:
                    mine.append(lw)
            mine.extend(b.readers)
            wdeps[id(b)] = mine
            deps.extend(mine)
        out = []
        seen = set()
        for d in deps:
            if id(d) in seen or d is op:
                continue
            seen.add(id(d))
            if d.dma_buf is None and d.eng == eng and not is_dma:
                if eng == "pe" or id(d) not in raw:
                    continue
            out.append(d)
        op.deps = out
        if is_dma:
            op.dma_buf = dma_buf
            if dma_buf.dsem is None:
                if self.sem_free:
                    dma_buf.dsem = self.sem_free.pop()
                else:
                    dma_buf.dsem = len(self.sem_tot)
                    self.sem_tot.append(0)
            self.sem_tot[dma_buf.dsem] += 16
            op.dma_sem = dma_buf.dsem
            op.dma_cnt = self.sem_tot[dma_buf.dsem]
            self.dma_pending[id(dma_buf)] = op
        for b in reads:
            if b is not None:
                b.readers.append(op)
        for b in writes:
            if b is not None:
                b.last_w = op
                b.last_w_deps = wdeps[id(b)]
                b.readers = []
        self.ops[eng].append(op)
        return op

    def barrier(self):
        lasts = []
        for e in ENGS:
            for o in reversed(self.ops[e]):
                if o.dma_buf is None and o.fn is not None:
                    lasts.append(o)
                    break
        pend = list(self.dma_pending.values())
        for e in ENGS:
            op = Op(e, None)
            op.idx = len(self.ops[e])
            op.deps = list(lasts) + pend
            self.ops[e].append(op)
        self.dma_pending = {}
        for b in self.bufs:
            b.last_w = None
            b.last_w_deps = []
            b.readers = []
            if b.dsem is not None:
                self.sem_free.append(b.dsem)
                b.dsem = None

    @staticmethod
    def _bufs(*vs):
        return [v.buf for v in vs if isinstance(v, V)]

    @staticmethod
    def _a(v):
        return v.ap if isinstance(v, V) else v

    def matmul(self, out, lhsT, rhs, start=True, stop=True):
        o, l, r = out.ap, lhsT.ap, rhs.ap
        rd = self._bufs(lhsT, rhs)
        if not start:
            rd = rd + [out.buf]
        return self._add("pe", lambda e: e.matmul(o, lhsT=l, rhs=r, start=start, stop=stop), rd, [out.buf])

    def transpose(self, out, in_, ident):
        o, i, d = out.ap, in_.ap, ident.ap
        return self._add("pe", lambda e: e.transpose(o, i, d), self._bufs(in_, ident), [out.buf])

    def act(self, out, in_, func, bias=0.0, scale=1.0, accum=None):
        o, i, b, s = out.ap, in_.ap, self._a(bias), self._a(scale)
        ac = accum.ap if accum is not None else None
        w = [out.buf] + ([accum.buf] if accum is not None else [])
        if ac is None:
            fn = lambda e: e.activation(o, i, func, bias=b, scale=s)
        else:
            fn = lambda e: e.activation(o, i, func, bias=b, scale=s, accum_out=ac)
        return self._add("act", fn, self._bufs(in_, bias, scale), w)

    def tt(self, eng, out, a, b, op):
        o, x, y = out.ap, a.ap, b.ap
        return self._add(eng, lambda e: e.tensor_tensor(o, x, y, op), self._bufs(a, b), [out.buf])

    def ts(self, eng, out, a, s1, op0, s2=None, op1=None, accum=None):
        o, x, p, q = out.ap, a.ap, self._a(s1), self._a(s2)
        ac = accum.ap if accum is not None else None
        w = [out.buf] + ([accum.buf] if accum is not None else [])
        kw = {}
        if op1 is not None:
            kw["op1"] = op1
        if ac is not None:
            kw["accum_out"] = ac
        return self._add(eng, lambda e: e.tensor_scalar(o, x, p, q, op0, **kw), self._bufs(a, s1, s2), w)

    def stt(self, eng, out, in0, scalar, in1, op0, op1):
        o, x, s, y = out.ap, in0.ap, self._a(scalar), in1.ap
        return self._add(eng, lambda e: e.scalar_tensor_tensor(o, x, s, y, op0, op1), self._bufs(in0, scalar, in1), [out.buf])

    def copy(self, eng, out, in_):
        o, i = out.ap, in_.ap
        if eng == "act":
            return self._add("act", lambda e: e.copy(o, i), [in_.buf], [out.buf])
        return self._add(eng, lambda e: e.tensor_copy(o, i), [in_.buf], [out.buf])

    def memset(self, eng, out, val):
        o = out.ap
        return self._add(eng, lambda e: e.memset(o, val), [], [out.buf])

    def reduce(self, eng, out, in_, op, axis=AX.X):
        o, i = out.ap, in_.ap
        return self._add(eng, lambda e: e.tensor_reduce(o, i, axis, op), [in_.buf], [out.buf])

    def recip(self, out, in_):
        o, i = out.ap, in_.ap
        return self._add("dve", lambda e: e.reciprocal(o, i), [in_.buf], [out.buf])

    def dma(self, q, out, in_, extra_reads=()):
        o = out.ap if isinstance(out, V) else out
        i = in_.ap if isinstance(in_, V) else in_
        ob = out.buf if isinstance(out, V) else None
        ib = in_.buf if isinstance(in_, V) else None
        sb = ob if ob is not None else ib
        assert sb is not None
        reads = ([ib] if ib is not None else []) + list(extra_reads)
        writes = [ob] if ob is not None else []
        return self._add(q, lambda e: e.dma_start(out=o, in_=i), reads, writes, dma_buf=sb)

    def emit(self):
        nc = self.nc
        for e in ENGS:
            for op in self.ops[e]:
                for d in op.deps:
                    d.signal = True
        esem = {}
        for e in ENGS:
            cnt = 0
            for op in self.ops[e]:
                if op.dma_buf is None and op.signal and op.fn is not None:
                    cnt += 1
                    op.val = cnt
                elif op.dma_buf is None and op.fn is None and op.signal:
                    raise RuntimeError("barrier op signalled")
            if cnt:
                esem[e] = self._sem("e_" + e)
        dsems = [self._sem(f"d{i}") for i in range(len(self.sem_tot))]
        nwait = 0
        for e in ENGS:
            seen = {}
            for op in self.ops[e]:
                ws = {}
                for d in op.deps:
                    if d.dma_buf is not None:
                        key = ("d", d.dma_sem)
                        sem, val = dsems[d.dma_sem], d.dma_cnt
                    else:
                        key = ("e", d.eng)
                        sem, val = esem[d.eng], d.val
                    if seen.get(key, 0) >= val:
                        continue
                    if key not in ws or ws[key][1] < val:
                        ws[key] = (sem, val)
                for key, (sem, val) in ws.items():
                    seen[key] = val
                op.waits = list(ws.values())
                nwait += len(op.waits)
        self.stats = {e: len(self.ops[e]) for e in ENGS}
        self.stats["waits"] = nwait
        self.stats["sems"] = self.nsem
        engmap = {"pe": "tensor", "act": "scalar", "dve": "vector", "pool": "gpsimd", "sp": "sync"}
        with nc.Block() as block:
            for e in ENGS:
                ops = self.ops[e]
                if not ops:
                    continue
                sem_e = esem.get(e)

                def body(eng, ops=ops, sem_e=sem_e, dsems=dsems, self=self):
                    for op in ops:
                        for (sem, val) in op.waits:
                            eng.wait_ge(sem, val)
                        if op.fn is None:
                            continue
                        ins = op.fn(eng)
                        if self.annotate and op.tag:
                            ins.annotate(op.tag)
                        if op.dma_buf is not None:
                            ins.then_inc(dsems[op.dma_sem], 16)
                        elif op.signal:
                            ins.then_inc(sem_e, 1)
                getattr(block, engmap[e])(body)


D_MODEL = 2048
NLAT = 2048
NCTX = 256
T = NLAT + NCTX
D_FF = 5632
NORM_EPS = 1e-6
KD = 16


class Ctx:
    pass


def chunks(total, size):
    return [(i, min(size, total - i)) for i in range(0, total, size)]


def fm(ap):
    return ap.rearrange("(k p) n -> p k n", p=128)


def init_common(C):
    S = C.S
    C.ps = [S.psum(f"ps{i}", [128, 512]) for i in range(8)]
    C.psi = 0
    C.ident = S.sbuf("ident", [128, 128])
    C.ones = S.sbuf("ones", [128, 128])
    S.dma("sp", C.ident.v, C.d["ident"])
    S.memset("dve", C.ones.v, 1.0)
    C.mod = S.sbuf("mod", [128, 96, 2])
    C.g1s = S.sbuf("g1s", [128, 16, 2])
    C.g2s = S.sbuf("g2s", [128, 16, 2])
    C.dq = 0


def nextps(C):
    p = C.ps[C.psi % 8]
    C.psi += 1
    return p


def ldq(C):
    C.dq += 1
    return ("sp", "pool")[C.dq % 2]


def stage_mod(C, l):
    S = C.S
    with ExitStack() as st:
        wbuf = [S.sbuf(f"adaw{i}", [128, 16, 512], BF16, stack=st) for i in range(3)]
        cc = S.sbuf("cc", [128, 16, 2], stack=st)
        sc = S.sbuf("sc", [128, 16, 2], BF16, stack=st)
        adab = S.sbuf("adab", [128, 96], stack=st)
        ng = S.sbuf("ng", [128, 2, 16], stack=st)
        S.dma("sp", cc.v, C.d["cc"])
        S.dma("sp", adab.v, C.d["ada_b"][l])
        S.dma("sp", ng[:, 0, :], C.d["norm1_g"][l])
        S.dma("sp", ng[:, 1, :], C.d["norm2_g"][l])
        S.act(sc.v, cc.v, AF.Silu)
        W = fm(C.d["ada_w"][l])
        for fb in range(24):
            w = wbuf[fb % 3]
            S.dma("pool", w.v, W[:, :, fb * 512:(fb + 1) * 512])
            p = nextps(C)
            for j in range(4):
                for k in range(16):
                    S.matmul(p[:, j * 2:(j + 1) * 2], w[:, k, j * 128:(j + 1) * 128], sc[:, k, :],
                             start=(k == 0), stop=(k == 15))
            for col in range(2):
                S.tt("dve", C.mod[:, fb * 4:(fb + 1) * 4, col],
                     p[:, 0:8].rearrange("p (j c) -> p j c", c=2)[:, :, col],
                     adab[:, fb * 4:(fb + 1) * 4], ALU.add)
        for col in range(2):
            S.stt("dve", C.g1s[:, :, col], C.mod[:, 16:32, col], 1.0, ng[:, 0, :], ALU.add, ALU.mult)
            S.stt("dve", C.g2s[:, :, col], C.mod[:, 64:80, col], 1.0, ng[:, 1, :], ALU.add, ALU.mult)
    S.barrier()


def stage_norm(C, which, src, dst, col_chunks, final_g=None):
    S = C.S
    srcv, dstv = fm(src), fm(dst)
    with ExitStack() as st:
        xb = [S.sbuf(f"nx{i}", [128, 16, 256], stack=st) for i in range(2)]
        sqb = [S.sbuf(f"nsq{i}", [128, 16, 256], stack=st) for i in range(2)]
        hb = [S.sbuf(f"nh{i}", [128, 16, 256], F32 if which == 0 else BF16, stack=st) for i in range(2)]
        hf = S.sbuf("nhf", [128, 16, 256], stack=st) if which != 0 else None
        rs = [S.sbuf(f"nr{i}", [128, 256], stack=st) for i in range(2)]
        def stats(ci):
            c0, w, col = col_chunks[ci]
            x, r = xb[ci % 2], rs[ci % 2]
            sq = sqb[ci % 2]
            S.dma("sp", x[:, :, 0:w], srcv[:, :, c0:c0 + w])
            S.act(sq[:, :, 0:w], x[:, :, 0:w], AF.Square)
            p = nextps(C)
            for k in range(16):
                S.matmul(p[:, 0:w], C.ones.v, sq[:, k, 0:w], start=(k == 0), stop=(k == 15))
            S.ts("dve", r[:, 0:w], p[:, 0:w], 1.0 / D_MODEL, ALU.mult, NORM_EPS, ALU.add)
            S.act(r[:, 0:w], r[:, 0:w], AF.Ln)
            S.act(r[:, 0:w], r[:, 0:w], AF.Exp, scale=-0.5)

        hsub = [S.subviews(t_, 16) for t_ in hb]
        hfsub = S.subviews(hf, 16) if hf is not None else None

        def apply(ci):
            c0, w, col = col_chunks[ci]
            x, h, r = xb[ci % 2], hb[ci % 2], rs[ci % 2]
            hs = hsub[ci % 2]
            for k in range(16):
                if which == 0:
                    S.stt("dve", hs[k][:, 0:w], x[:, k, 0:w], final_g[:, k:k + 1], r[:, 0:w], ALU.mult, ALU.mult)
                else:
                    gs = C.g1s if which == 1 else C.g2s
                    sh0 = 0 if which == 1 else 48
                    S.stt("dve", hfsub[k][:, 0:w], x[:, k, 0:w], gs[:, k, col:col + 1], r[:, 0:w], ALU.mult, ALU.mult)
                    S.act(hs[k][:, 0:w], hfsub[k][:, 0:w], AF.Identity, bias=C.mod[:, sh0 + k, col:col + 1])
            S.dma("pool", dstv[:, :, c0:c0 + w], h[:, :, 0:w], extra_reads=[v_.buf for v_ in hs])

        n_ = len(col_chunks)
        stats(0)
        for ci in range(n_):
            if ci + 1 < n_:
                stats(ci + 1)
            apply(ci)
    S.barrier()


def norm_chunks(with_ctx=True):
    cs = [(c0, w, 0) for (c0, w) in chunks(NLAT, 256)]
    if with_ctx:
        cs += [(NLAT, 256, 1)]
    return cs


def seg_chunks(g0, gw, size=512):
    out = []
    a = g0
    end = g0 + gw
    while a < end:
        b = min(a + size, end)
        if a < NLAT < b:
            b = NLAT
        out.append((a - g0, b - a))
        a = b
    return out


def gemm(C, inT, K, W, tok_groups, blocks, epi, group_hook=None, wblock=None, resident=None):
    S = C.S
    nk = K // 128
    if resident is None:
        resident = {}
    res = [resident.get(g, g) for g in tok_groups]
    gwmax = max(rw for _, rw in res)
    nwmax = max(nw for _, nw, _ in blocks)
    Wv = fm(W) if W is not None else None
    inv = fm(inT)
    with ExitStack() as st:
        act = S.sbuf("g_act", [128, nk, gwmax], BF16, stack=st)
        NWB = 3
        wb = [S.sbuf(f"g_w{i}", [128, nk, nwmax], BF16, stack=st) for i in range(NWB)]
        bi_glob = 0
        for gi, (gg0, ggw) in enumerate(tok_groups):
            g0, gw = res[gi]
            kq = max(1, nk // 4)
            for k0 in range(0, nk, kq):
                k1 = min(nk, k0 + kq)
                S.dma(("sp", "act")[(k0 // kq) % 2], act[:, k0:k1, 0:gw], inv[:, k0:k1, g0:g0 + gw])
            if group_hook is not None:
                group_hook("begin", g0, gw)
            for (n0, nw, mode) in blocks:
                wt = wb[bi_glob % NWB]
                bi_glob += 1
                wsrc = wblock(n0, nw) if wblock is not None else Wv[:, :, n0:n0 + nw]
                S.dma("pool", wt[:, :, 0:nw], wsrc)
                if mode == "T":
                    for t0 in range(0, gw, 128):
                        ps = nextps(C)
                        for k in range(nk):
                            S.matmul(ps[:, 0:nw], act[:, k, t0:t0 + 128], wt[:, k, 0:nw],
                                     start=(k == 0), stop=(k == nk - 1))
                        epi("T", ps, n0, nw, g0 + t0, 128, g0, gw)
                else:
                    for f0 in range(0, nw, 128):
                        for (t0, tw) in seg_chunks(g0, gw):
                            ps = nextps(C)
                            for k in range(nk):
                                S.matmul(ps[:, 0:tw], wt[:, k, f0:f0 + 128], act[:, k, t0:t0 + tw],
                                         start=(k == 0), stop=(k == nk - 1))
                            epi("F", ps, n0 + f0, 128, g0 + t0, tw, g0, gw)
            if group_hook is not None:
                group_hook("end", g0, gw)
    S.barrier()


class Stager:
    def __init__(self, C, st, name, n=4, width=512):
        self.C = C
        self.bufs = [C.S.sbuf(f"{name}{i}", [128, width], stack=st) for i in range(n)]
        self.i = 0

    def next(self):
        b = self.bufs[self.i % len(self.bufs)]
        self.i += 1
        return b

    def evac(self, dst_view, src_view):
        S = self.C.S
        if self.i % 2 == 0:
            S.copy("act", dst_view, src_view)
        else:
            S.copy("dve", dst_view, src_view)


def stage_win(C, l):
    S = C.S
    d = C.d
    blocks = []
    outmap = []

    def addT(n0, n1, dram):
        for (o, w) in chunks(n1 - n0, 512):
            blocks.append((n0 + o, w, "T"))
        outmap.append((n0, n1, "T", dram))

    def addF(n0, n1, dram):
        for (o, w) in chunks(n1 - n0, 512):
            blocks.append((n0 + o, w, "F"))
        outmap.append((n0, n1, "F", dram))

    addT(0, 1920, d["p_rw"])
    addT(1920, 3472, d["p_ssd"])
    addF(3472, 4496, d["da_qkT"])
    addT(4496, 5008, d["da_v"])
    addF(5008, 6032, d["na_qkT"])
    addT(6032, 6544, d["na_v"])

    def find(n0):
        for (a, b, m, dr) in outmap:
            if a <= n0 < b:
                return a, dr
        raise KeyError

    with ExitStack() as st:
        sg = Stager(C, st, "wst", 4)

        def epi(mode, ps, n0, nw, c0, cw, g0, gw):
            a, dr = find(n0)
            sb = sg.next()
            if mode == "T":
                sg.evac(sb[:, 0:nw], ps[:, 0:nw])
                S.dma("act" if sg.i % 2 else "sp", dr[c0:c0 + 128, n0 - a:n0 - a + nw], sb[:, 0:nw])
            else:
                sg.evac(sb[:, 0:cw], ps[:, 0:cw])
                S.dma("act" if sg.i % 2 else "sp", dr[n0 - a:n0 - a + 128, c0:c0 + cw], sb[:, 0:cw])

        gemm(C, d["hT"], D_MODEL, d["w_in"][l], [(0, T)], blocks, epi)


def stage_merge(C, l, with_ctx):
    S = C.S
    d = C.d
    Tt = T if with_ctx else NLAT
    NG = 2
    gw = Tt // NG
    hv = fm(d["hT"])
    bv = d["brT"].rearrange("n (k p) t -> p (n k) t", p=128)
    wg = d["w_gate"][l].rearrange("n (k p) m -> n p k m", p=128)
    wbr = d["w_br"][l].rearrange("n (k p) m -> n p k m", p=128)
    mgv = fm(d["mgT"])
    CW = 384
    with ExitStack() as st:
        hact = S.sbuf("m_h", [128, 16, gw], BF16, stack=st)
        bact = S.sbuf("m_b", [128, 16, gw], BF16, stack=st)
        wgb = [S.sbuf(f"m_wg{i}", [128, 16, 256], BF16, stack=st) for i in range(3)]
        wbb = [S.sbuf(f"m_wb{i}", [128, 4, 256], BF16, stack=st) for i in range(3)]
        gb = S.sbuf("m_gb", [128, 4, 16], stack=st)
        S.dma("sp", gb.v, d["gate_b"][l])
        sig = [S.sbuf(f"m_sig{i}", [128, CW], stack=st) for i in range(2)]
        tmp = [S.sbuf(f"m_tmp{i}", [128, CW], stack=st) for i in range(2)]
        acc = [S.sbuf(f"m_acc{i}", [128, 2, gw], stack=st) for i in range(2)]
        aout = [S.sbuf(f"m_ao{i}", [128, 2, gw], BF16, stack=st) for i in range(2)]
        wi = 0
        ai = 0
        si = 0
        for g in range(NG):
            g0 = g * gw
            for k0 in range(0, 16, 4):
                S.dma("sp", hact[:, k0:k0 + 4, :], hv[:, k0:k0 + 4, g0:g0 + gw])
                S.dma("act", bact[:, k0:k0 + 4, :], bv[:, k0:k0 + 4, g0:g0 + gw])
            tch = seg_chunks(g0, gw, CW)
            for db in range(8):
                a = acc[ai % 2]
                ao = aout[ai % 2]
                ai += 1
                for n in range(4):
                    wgt, wbt = wgb[wi % 3], wbb[wi % 3]
                    wi += 1
                    S.dma("pool", wgt.v, wg[n][:, :, db * 256:(db + 1) * 256])
                    S.dma("pool", wbt.v, wbr[n][:, :, db * 256:(db + 1) * 256])
                    for dc in range(2):
                        for (t0, tw) in tch:
                            pg = nextps(C)
                            for k in range(16):
                                S.matmul(pg[:, 0:tw], wgt[:, k, dc * 128:(dc + 1) * 128], hact[:, k, t0:t0 + tw],
                                         start=(k == 0), stop=(k == 15))
                            pb = nextps(C)
                            for k in range(4):
                                S.matmul(pb[:, 0:tw], wbt[:, k, dc * 128:(dc + 1) * 128], bact[:, n * 4 + k, t0:t0 + tw],
                                         start=(k == 0), stop=(k == 3))
                            sg = sig[si % 2]
                            tp = tmp[si % 2]
                            si += 1
                            S.act(sg[:, 0:tw], pg[:, 0:tw], AF.Sigmoid, bias=gb[:, n, db * 2 + dc:db * 2 + dc + 1])
                            if n == 0:
                                S.tt("dve", a[:, dc, t0:t0 + tw], sg[:, 0:tw], pb[:, 0:tw], ALU.mult)
                            else:
                                S.tt("dve", tp[:, 0:tw], sg[:, 0:tw], pb[:, 0:tw], ALU.mult)
                                dst = ao if n == 3 else a
                                S.tt("dve", dst[:, dc, t0:t0 + tw], a[:, dc, t0:t0 + tw], tp[:, 0:tw], ALU.add)
                S.dma("sp", mgv[:, db * 2:db * 2 + 2, g0:g0 + gw], ao.v)
    S.barrier()


def resid_epi(C, st, gate0, xsrc, xdram):
    S = C.S
    xt = [S.sbuf(f"r_x{i}", [128, 512], stack=st) for i in range(4)]
    cnt = [0]

    def epi(mode, ps, n0, nw, c0, cw, g0, gw):
        assert mode == "F"
        x = xt[cnt[0] % 4]
        cnt[0] += 1
        col = 0 if c0 < NLAT else 1
        k = n0 // 128
        S.dma("sp", x[:, 0:cw], xsrc[n0:n0 + 128, c0:c0 + cw])
        S.stt("dve", x[:, 0:cw], ps[:, 0:cw], C.mod[:, gate0 + k, col:col + 1], x[:, 0:cw], ALU.mult, ALU.add)
        S.dma("act", xdram[n0:n0 + 128, c0:c0 + cw], x[:, 0:cw])
    return epi


def stage_wout(C, l, with_ctx):
    d = C.d
    xsrc = d["xT"] if l == 0 else d["xres"]
    Tt = T if with_ctx else NLAT
    groups = [(0, Tt)]
    blocks = [(n0, 512, "F") for n0 in range(0, D_MODEL, 512)]
    with ExitStack() as st:
        epi = resid_epi(C, st, 32, xsrc, d["xres"])
        gemm(C, d["mgT"], D_MODEL, d["w_out"][l], groups, blocks, epi)


def stage_ffn_up(C, l, with_ctx):
    S = C.S
    d = C.d
    Tt = T if with_ctx else NLAT
    half = Tt
    groups = [(0, Tt)]
    resident = {}
    blocks = [(j * 256, 256, "F") for j in range(44)]
    guv = d["guT"]
    with ExitStack() as st:
        cw_t = S.sbuf("f_cw", [128, 3, 88], stack=st)
        cb_t = S.sbuf("f_cb", [128, 88], stack=st)
        S.dma("sp", cw_t.v, d["ffn_conv_w"][l])
        S.dma("sp", cb_t.v, d["ffn_conv_b"][l])
        RW = Tt
        ub = [[S.sbuf(f"f_u{i}{j}", [128, RW], stack=st) for j in range(2)] for i in range(2)]
        yb = [[S.sbuf(f"f_y{i}{j}", [128, RW], stack=st) for j in range(2)] for i in range(2)]
        gob = [S.sbuf(f"f_go{i}", [128, RW], BF16, stack=st) for i in range(2)]
        state = {"i": 0}

        def epi(mode, ps, n0, nw, c0, cw, g0, gw):
            j = n0 // 256
            isval = (n0 % 256) // 128
            u = ub[j % 2][isval]
            S.copy("act" if isval else "dve", u[:, c0 - g0:c0 - g0 + cw], ps[:, 0:cw])
            last_chunk = (c0 + cw == g0 + gw)
            if not (isval and last_chunk):
                return
            o0 = g0
            o1 = g0 + gw
            pieces = []
            a = o0
            while a < o1:
                b = o1
                if a < NLAT < b:
                    b = NLAT
                pieces.append((a, b))
                a = b
            for (a, b) in pieces:
                seg0, seg1 = (0, NLAT) if a < NLAT else (NLAT, T)
                if not with_ctx:
                    seg0, seg1 = 0, NLAT
                for v_ in range(2):
                    u = ub[j % 2][v_]
                    y = yb[j % 2][v_]
                    ch = j + 44 * v_
                    la, lb = a - g0, b - g0
                    S.act(y[:, la:lb], u[:, la:lb], AF.Identity, bias=cb_t[:, ch:ch + 1], scale=cw_t[:, 1, ch:ch + 1])
                    lo = la + 1 if a == seg0 else la
                    S.stt("dve", y[:, lo:lb], u[:, lo - 1:lb - 1], cw_t[:, 0, ch:ch + 1], y[:, lo:lb], ALU.mult, ALU.add)
                    hi = lb - 1 if b == seg1 else lb
                    S.stt("dve", y[:, la:hi], u[:, la + 1:hi + 1], cw_t[:, 2, ch:ch + 1], y[:, la:hi], ALU.mult, ALU.add)
                yg, yv = yb[j % 2][0], yb[j % 2][1]
                la, lb = a - g0, b - g0
                S.act(ub[j % 2][0][:, la:lb], yg[:, la:lb], AF.Silu)
                S.tt("dve", gob[j % 2][:, la:lb], yv[:, la:lb], ub[j % 2][0][:, la:lb], ALU.mult)
                S.dma("sp", guv[j * 128:(j + 1) * 128, a:b], gob[j % 2][:, la:lb])

        gemm(C, d["hT"], D_MODEL, d["ffn_up"][l], groups, blocks, epi, resident=resident)


def stage_ffn_down(C, l, with_ctx):
    d = C.d
    Tt = T if with_ctx else NLAT
    gw = Tt // 2
    groups = [(g * gw, gw) for g in range(2)]
    blocks = [(n0, 128, "F") for n0 in range(0, D_MODEL, 128)]
    wv = d["ffn_down"][l]

    def wblock(n0, nw):
        return wv[n0 // 128]
    with ExitStack() as st:
        epi = resid_epi(C, st, 80, d["xres"], d["xres"])
        gemm(C, d["guT"], D_FF, None, groups, blocks, epi, wblock=wblock)

import math

DA_SUBLN_EPS = 1e-5


def load_bcast(S, q, tile_v, dram_ap, n=128):
    return S.dma(q, tile_v, dram_ap.partition_broadcast(n))


def stage_da(C, l, ctx_out):
    S = C.S
    d = C.d
    lam_init = 0.8 - 0.6 * math.exp(-0.3 * l)
    qk = d["da_qkT"]
    vd = d["da_v"]
    out = d["brT"][2]
    acc = C.ps[0:4]
    rot = C.ps[4:7]
    misc = C.ps[7]
    NR = 3
    with ExitStack() as st:
        cos = S.sbuf("da_cos", [128, NLAT], stack=st)
        sin = S.sbuf("da_sin", [128, NLAT], stack=st)
        perm = S.sbuf("da_perm", [128, 128], stack=st)
        S.dma("sp", cos.v, d["rope_cos"])
        S.dma("act", sin.v, d["rope_sin"])
        S.dma("sp", perm.v, d["rope_perm"])
        lp = S.sbuf("da_lp", [128, 4, 64], stack=st)
        load_bcast(S, "sp", lp.v, d["da_lambda"][l])
        pr = S.sbuf("da_pr", [128, 2, 64], stack=st)
        sm = S.sbuf("da_sm", [128, 2], stack=st)
        S.tt("dve", pr[:, 0, :], lp[:, 0, :], lp[:, 1, :], ALU.mult)
        S.tt("dve", pr[:, 1, :], lp[:, 2, :], lp[:, 3, :], ALU.mult)
        S.reduce("dve", sm.v, pr.v, ALU.add)
        S.act(sm.v, sm.v, AF.Exp)
        nlam = S.sbuf("da_nlam", [128, 1], stack=st)
        S.tt("dve", nlam.v, sm[:, 1:2], sm[:, 0:1], ALU.subtract)
        S.ts("dve", nlam.v, nlam.v, -lam_init, ALU.add)
        gsub = S.sbuf("da_g", [128, 1], stack=st)
        S.dma("sp", gsub.v, d["da_subln_g"][l])
        S.ts("dve", gsub.v, gsub.v, 1.0 - lam_init, ALU.mult)

        qT = S.sbuf("da_q", [128, T], stack=st)
        kT = S.sbuf("da_k", [128, T], stack=st)
        qB = S.sbuf("da_qb", [128, T], BF16, stack=st)
        kB = S.sbuf("da_kb", [128, T], BF16, stack=st)
        kZ = [S.sbuf(f"da_kz{m}", [128, T], BF16, stack=st) for m in range(2)]
        S.memset("dve", kZ[0].v, 0.0)
        S.memset("dve", kZ[1].v, 0.0)
        vt = S.sbuf("da_vt", [128, 18, 128], BF16, stack=st)
        onesb = S.sbuf("da_1b", [128, 128], BF16, stack=st)
        S.memset("pool", onesb.v, 1.0)
        tmp = [S.sbuf(f"da_tmp{i}", [128, 512], stack=st) for i in range(2)]
        et = [S.sbuf(f"da_e{i}", [128, 512], BF16, stack=st) for i in range(5)]
        rr = [S.sbuf(f"da_r{i}", [128, 512], stack=st) for i in range(2)]
        oo = [S.sbuf(f"da_o{i}", [128, 512], stack=st) for i in range(2)]
        sq = S.sbuf("da_sq", [128, 512], stack=st)
        es = S.sbuf("da_es", [128, 512], stack=st)
        esb = S.sbuf("da_esb", [128, 512], BF16, stack=st)
        rs = S.sbuf("da_rs", [128, 512], stack=st)
        ob = [S.sbuf(f"da_ob{i}", [128, 512], BF16, stack=st) for i in range(2)]
        ei = 0
        oi = 0
        for h in range(4):
            S.dma("sp", qT.v, qk[h * 128:(h + 1) * 128, :])
            S.dma("act", kT.v, qk[512 + h * 128:512 + (h + 1) * 128, :])
            S.dma("pool", vt.v, vd[:, h * 128:(h + 1) * 128].rearrange("(b p) e -> p b e", p=128))
            ti = 0
            for X, XB in ((qT, qB), (kT, kB)):
                for c0 in range(0, NLAT, 512):
                    S.matmul(misc.v, perm.v, X[:, c0:c0 + 512])
                    t = tmp[ti % 2]
                    ti += 1
                    S.tt("dve", t.v, misc.v, sin[:, c0:c0 + 512], ALU.mult)
                    S.tt("dve", X[:, c0:c0 + 512], X[:, c0:c0 + 512], cos[:, c0:c0 + 512], ALU.mult)
                    S.tt("dve", XB[:, c0:c0 + 512], X[:, c0:c0 + 512], t.v, ALU.add)
                S.copy("act", XB[:, NLAT:T], X[:, NLAT:T])
            S.copy("act", kZ[0][0:64, :], kB[0:64, :])
            S.copy("dve", kZ[1][64:128, :], kB[64:128, :])
            qchunks = [(c0, 512, list(range(18))) for c0 in range(0, NLAT, 512)]
            if ctx_out:
                qchunks.append((NLAT, 256, [16, 17]))
            items = []
            for (c0, cw, kbs) in qchunks:
                for m in range(2):
                    for i, kb in enumerate(kbs):
                        items.append((c0, cw, m, i, kb, len(kbs)))
            slots = {}

            def emitA(n):
                c0, cw, m, i, kb, nk_ = items[n]
                ps = rot[n % NR]
                e = et[n % NR]
                S.matmul(ps[:, 0:cw], kZ[m][:, kb * 128:(kb + 1) * 128], qB[:, c0:c0 + cw])
                S.act(e[:, 0:cw], ps[:, 0:cw], AF.Exp, scale=0.125)

            def emitB(n):
                nonlocal oi
                c0, cw, m, i, kb, nk_ = items[n]
                e = et[n % NR]
                OT, SM = acc[m], acc[2 + m]
                S.matmul(OT[:, 0:cw], vt[:, kb, :], e[:, 0:cw], start=(i == 0), stop=(i == nk_ - 1))
                S.matmul(SM[:, 0:cw], onesb.v, e[:, 0:cw], start=(i == 0), stop=(i == nk_ - 1))
                if i != nk_ - 1:
                    return
                S.recip(rr[m][:, 0:cw], SM[:, 0:cw])
                S.tt("dve", oo[m][:, 0:cw], OT[:, 0:cw], rr[m][:, 0:cw], ALU.mult)
                if m != 1:
                    return
                o = oo[0]
                S.stt("dve", o[:, 0:cw], oo[1][:, 0:cw], nlam.v, o[:, 0:cw], ALU.mult, ALU.add)
                S.act(sq[:, 0:cw], o[:, 0:cw], AF.Square)
                S.matmul(misc[:, 0:cw], C.ones.v, sq[:, 0:cw])
                S.ts("dve", rs[:, 0:cw], misc[:, 0:cw], 1.0 / 128, ALU.mult, DA_SUBLN_EPS, ALU.add)
                S.act(rs[:, 0:cw], rs[:, 0:cw], AF.Ln)
                S.act(rs[:, 0:cw], rs[:, 0:cw], AF.Exp, scale=-0.5)
                b = ob[oi % 2]
                oi += 1
                S.stt("dve", b[:, 0:cw], o[:, 0:cw], gsub.v, rs[:, 0:cw], ALU.mult, ALU.mult)
                S.dma("sp", out[h * 128:(h + 1) * 128, c0:c0 + cw], b[:, 0:cw])

            LA = 2
            for n in range(len(items) + LA):
                if n < len(items):
                    emitA(n)
                if n - LA >= 0:
                    emitB(n - LA)
    S.barrier()


def rope_tables():
    GRID_W = 64
    t = np.arange(NLAT)
    pos = np.stack([t // GRID_W, t % GRID_W], axis=-1).astype(np.float32)
    n_freq = 16
    inv = (np.float32(10000.0) ** (-np.arange(n_freq, dtype=np.float32) / np.float32(n_freq))).astype(np.float32)
    ang = (pos[:, :, None] * inv).astype(np.float32)
    cosv, sinv = np.cos(ang).astype(np.float32), np.sin(ang).astype(np.float32)
    cos = np.zeros((128, NLAT), np.float32)
    sin = np.zeros((128, NLAT), np.float32)
    perm = np.zeros((128, 128), np.float32)
    for p in range(128):
        dd = p % 64
        half, j, f = dd // 32, (dd % 32) // 16, dd % 16
        cos[p] = cosv[:, half, f]
        if j == 0:
            sin[p] = -sinv[:, half, f]
            partner = p + 16
        else:
            sin[p] = sinv[:, half, f]
            partner = p - 16
        perm[partner, p] = 1.0
    return cos, sin, perm


NT = T // 128
SEG_FIRST = {0: True, 16: True}
SEG_LAST = {15: True, 17: True}


def bc_last(v, n):
    shp = list(v.ap.shape)
    return V(v.ap.unsqueeze(len(shp)).broadcast_to(shp + [n]), v.buf)


def load_shifted(S, st_tiles, src, t0, width, c0, first, last):
    xm, x0, xp = st_tiles
    S.dma("sp", x0[:, 0:width], src[t0:t0 + 128, c0:c0 + width])
    if first:
        S.memset("pool", xm[0:32, 0:width], 0.0)
        S.dma("pool", xm[1:128, 0:width], src[t0:t0 + 127, c0:c0 + width])
    else:
        S.dma("pool", xm[:, 0:width], src[t0 - 1:t0 + 127, c0:c0 + width])
    if last:
        S.memset("pool", xp[96:128, 0:width], 0.0)
        S.dma("act", xp[0:127, 0:width], src[t0 + 1:t0 + 128, c0:c0 + width])
    else:
        S.dma("act", xp[:, 0:width], src[t0 + 1:t0 + 129, c0:c0 + width])


def stage_ssd(C, l, ctx_out):
    S = C.S
    d = C.d
    src = d["p_ssd"]
    out = d["brT"][1].rearrange("(k p) t -> p k t", p=128)
    P = C.ps
    with ExitStack() as st:
        triF = S.sbuf("tr_f", [128, 128], stack=st)
        triB = S.sbuf("tr_b", [128, 128], stack=st)
        S.dma("sp", triF.v, d["triF"])
        S.dma("sp", triB.v, d["triB"])
        dtb = S.sbuf("s_dtb", [128, 16], stack=st)
        load_bcast(S, "sp", dtb.v, d["ssd_dt_bias"][l])
        negA = S.sbuf("s_negA", [128, 16], stack=st)
        load_bcast(S, "sp", negA.v, d["ssd_a_log"][l])
        S.act(negA.v, negA.v, AF.Exp)
        S.ts("dve", negA.v, negA.v, -1.0, ALU.mult)
        dsk = S.sbuf("s_dsk", [128, 8], stack=st)
        load_bcast(S, "sp", dsk.v, d["ssd_d"][l])
        ngb = S.sbuf("s_ng", [128, 512], stack=st)
        load_bcast(S, "pool", ngb.v, d["ssd_norm_g"][l])

        xall = S.sbuf("s_xall", [128, NT, 768], stack=st)
        bct = S.sbuf("s_bct", [128, NT, 4, 128], stack=st)
        dta = S.sbuf("s_dta", [128, NT, 2, 16], stack=st)
        H = S.sbuf("s_H", [128, 2, 2, 256], stack=st)
        S.memset("dve", H.v, 0.0)
        st2 = ExitStack()
        cw = S.sbuf("s_cw", [128, 4, 1024], stack=st2)
        load_bcast(S, "sp", cw[:, 0:3, :], d["ssd_conv_w"][l])
        load_bcast(S, "pool", cw[:, 3, :], d["ssd_conv_b"][l])
        sh = [[S.sbuf(f"s_sh{i}{j}", [128, 1024], stack=st2) for j in range(3)] for i in range(2)]
        ta = [S.sbuf(f"s_ta{i}", [128, 1024], stack=st2) for i in range(2)]
        tb = [S.sbuf(f"s_tb{i}", [128, 1024], stack=st2) for i in range(2)]
        dtr = [S.sbuf(f"s_dtr{i}", [128, 16], stack=st2) for i in range(2)]
        def prepA(ti):
            t0 = ti * 128
            tl = sh[ti % 2]
            load_shifted(S, tl, src, t0, 1024, 512, ti in SEG_FIRST, ti in SEG_LAST)
            a, b = ta[ti % 2], tb[ti % 2]
            r = dtr[ti % 2]
            S.dma("sp", r.v, src[t0:t0 + 128, 1536:1552])
            S.tt("dve", a.v, tl[0].v, cw[:, 0, :], ALU.mult)
            S.tt("dve", b.v, tl[1].v, cw[:, 1, :], ALU.mult)
            S.tt("dve", a.v, a.v, b.v, ALU.add)
            S.tt("dve", b.v, tl[2].v, cw[:, 2, :], ALU.mult)
            S.tt("dve", a.v, a.v, b.v, ALU.add)
            S.tt("dve", a.v, a.v, cw[:, 3, :], ALU.add)
            S.tt("dve", r.v, r.v, dtb.v, ALU.add)

        def prepB(ti):
            a, b = ta[ti % 2], tb[ti % 2]
            r = dtr[ti % 2]
            S.act(b.v, a.v, AF.Silu)
            S.copy("act", xall[:, ti, :], b[:, 0:768])
            S.act(r.v, r.v, AF.Exp)
            S.act(dta[:, ti, 0, :], r.v, AF.Ln, bias=1.0)
            for q in range(4):
                S.transpose(P[0][:, q * 128:(q + 1) * 128], b[:, 512 + q * 128:512 + (q + 1) * 128], C.ident.v)
            S.tt("dve", dta[:, ti, 1, :], dta[:, ti, 0, :], negA.v, ALU.mult)
            S.copy("dve", bct[:, ti, :, :], P[0].v.rearrange("p (q t) -> p q t", q=4))

        prepA(0)
        for ti in range(NT):
            if ti + 1 < NT:
                prepA(ti + 1)
            prepB(ti)

        S.barrier()
        st2.close()
        yacc = S.sbuf("s_yacc", [128, NT, 512], stack=st)
        ct = [S.sbuf(f"s_ct{i}", [128, 16], stack=st) for i in range(2)]
        te = [S.sbuf(f"s_te{i}", [128, 8], stack=st) for i in range(2)]
        sc = [S.sbuf(f"s_sc{i}", [128, 8], stack=st) for i in range(2)]
        ee = [S.sbuf(f"s_ee{i}", [128, 16], stack=st) for i in range(2)]
        xs = [S.sbuf(f"s_xs{i}", [128, 512], stack=st) for i in range(2)]
        sm = [S.sbuf(f"s_sm{i}", [128, 2, 128], stack=st) for i in range(2)]
        atr4 = [S.sbuf(f"s_atr{i}", [128, 4, 128], stack=st) for i in range(2)]
        sg4 = [S.sbuf(f"s_sg{i}", [128, 4, 128], stack=st) for i in range(2)]
        mt4 = [S.sbuf(f"s_mt{i}", [128, 4, 128], stack=st) for i in range(2)]

        def bc_mid4(v):
            return V(v.ap.unsqueeze(1).broadcast_to([128, 4, 128]), v.buf)
        yt = [S.sbuf(f"s_yt{i}", [128, 512], stack=st) for i in range(2)]
        iters = []
        for dr in range(2):
            order = [16, 17] + list(range(16)) if dr == 0 else [17, 16] + list(range(15, -1, -1))
            for ti in order:
                iters.append((dr, ti))
        ih = [0]

        def partA(n):
            dr, ti = iters[n]
            k = n % 2
            tri = triF if dr == 0 else triB
            yo = P[3] if n % 2 == 0 else P[0]
            dt = dta[:, ti, 0, dr * 8:(dr + 1) * 8]
            aa = dta[:, ti, 1, dr * 8:(dr + 1) * 8]
            X = xall[:, ti, 0:512]
            Bm = xall[:, ti, 512:768]
            S.matmul(P[1][:, 0:8], tri.v, aa)
            S.matmul(P[1][:, 8:16], C.ones.v, aa)
            S.copy("dve", ct[k].v, P[1][:, 0:16])
            cum, tot = ct[k][:, 0:8], ct[k][:, 8:16]
            S.tt("dve", te[k].v, tot, cum, ALU.subtract)
            S.act(te[k].v, te[k].v, AF.Exp)
            S.tt("dve", sc[k].v, te[k].v, dt, ALU.mult)
            S.act(ee[k].v, ct[k].v, AF.Exp)
            S.tt("dve", xs[k].v.rearrange("p (h e) -> p h e", h=8), X.rearrange("p (h e) -> p h e", h=8),
                 bc_last(sc[k].v, 64), ALU.mult)
            for g in range(2):
                S.matmul(yo[:, g * 256:(g + 1) * 256], bct[:, ti, 2 + g, :], H[:, dr, g, :])
                S.matmul(P[5][:, g * 128:(g + 1) * 128], bct[:, ti, g, :], bct[:, ti, 2 + g, :])
            for g in range(2):
                S.tt("dve", sm[k][:, g, :], P[5][:, g * 128:(g + 1) * 128], tri.v, ALU.mult)
            for g in range(2):
                S.matmul(P[2][:, g * 256:(g + 1) * 256], Bm[:, g * 128:(g + 1) * 128], xs[k][:, g * 256:(g + 1) * 256])
            for g in range(2):
                Hg = H[:, dr, g, :].rearrange("p (h e) -> p h e", h=4)
                S.tt("dve", Hg, Hg, bc_last(ee[k][:, 8 + g * 4:8 + (g + 1) * 4], 64), ALU.mult)
                S.tt("dve", H[:, dr, g, :], H[:, dr, g, :], P[2][:, g * 256:(g + 1) * 256], ALU.add)

        def partB(n):
            dr, ti = iters[n]
            k = n % 2
            tri = triF if dr == 0 else triB
            yo = P[3] if n % 2 == 0 else P[0]
            dt = dta[:, ti, 0, dr * 8:(dr + 1) * 8]
            aa = dta[:, ti, 1, dr * 8:(dr + 1) * 8]
            X = xall[:, ti, 0:512]
            cum = ct[k][:, 0:8]
            for g in range(2):
                j = ih[0] % 2
                pb = P[6 + ih[0] % 2]
                ih[0] += 1
                hs = slice(g * 4, (g + 1) * 4)
                S.tt("dve", atr4[j].v, bc_mid4(tri.v), bc_last(aa[:, hs], 128), ALU.mult)
                S.matmul(pb.v, C.ones.v, atr4[j].v.rearrange("p h i -> p (h i)"))
                S.tt("dve", sg4[j].v, pb.v.rearrange("p (h i) -> p h i", h=4), bc_last(cum[:, hs], 128), ALU.subtract)
                S.ts("dve", sg4[j].v, sg4[j].v, 0.0, ALU.min)
                S.act(sg4[j].v, sg4[j].v, AF.Exp)
                S.tt("dve", sg4[j].v, sg4[j].v, bc_last(dt[:, hs], 128), ALU.mult)
                S.tt("dve", mt4[j].v, sg4[j].v, bc_mid4(sm[k][:, g, :]), ALU.mult)
                for hh in range(4):
                    h = g * 4 + hh
                    S.matmul(P[4][:, h * 64:(h + 1) * 64], mt4[j][:, hh, :], X[:, h * 64:(h + 1) * 64])
            y = yt[k]
            S.tt("dve", y.v.rearrange("p (h e) -> p h e", h=8), yo.v.rearrange("p (h e) -> p h e", h=8),
                 bc_last(ee[k][:, 0:8], 64), ALU.mult)
            if dr == 0:
                S.tt("dve", yacc[:, ti, :], y.v, P[4].v, ALU.add)
            else:
                S.tt("dve", y.v, y.v, P[4].v, ALU.add)
                S.tt("dve", yacc[:, ti, :], yacc[:, ti, :], y.v, ALU.add)

        partA(0)
        for n in range(len(iters)):
            if n + 1 < len(iters):
                partA(n + 1)
            partB(n)
        zt = [S.sbuf(f"s_z{i}", [128, 512], stack=st) for i in range(2)]
        y2 = [S.sbuf(f"s_y2{i}", [128, 512], stack=st) for i in range(2)]
        ssq = [S.sbuf(f"s_ssq{i}", [128, 1], stack=st) for i in range(2)]
        junk = S.sbuf("s_junk", [128, 512], stack=st)
        ot = [S.sbuf(f"s_ot{i}", [128, 4, 128], BF16, stack=st) for i in range(2)]
        tiles = list(range(NT)) if ctx_out else list(range(16))
        for n, ti in enumerate(tiles):
            k = n % 2
            t0 = ti * 128
            S.dma("sp", zt[k].v, src[t0:t0 + 128, 0:512])
            S.act(zt[k].v, zt[k].v, AF.Silu)
            y = y2[k]
            S.tt("dve", y.v.rearrange("p (h e) -> p h e", h=8), xall[:, ti, 0:512].rearrange("p (h e) -> p h e", h=8),
                 bc_last(dsk.v, 64), ALU.mult)
            S.tt("dve", y.v, y.v, yacc[:, ti, :], ALU.add)
            S.tt("dve", y.v, y.v, zt[k].v, ALU.mult)
            S.memset("pool", ssq[k].v, 0.0)
            S.act(junk.v, y.v, AF.Square, accum=ssq[k].v)
            S.ts("dve", ssq[k].v, ssq[k].v, 1.0 / 512, ALU.mult, NORM_EPS, ALU.add)
            S.act(ssq[k].v, ssq[k].v, AF.Ln)
            S.act(ssq[k].v, ssq[k].v, AF.Exp, scale=-0.5)
            S.stt("dve", y.v, y.v, ssq[k].v, ngb.v, ALU.mult, ALU.mult)
            for q in range(4):
                S.transpose(P[0][:, q * 128:(q + 1) * 128], y[:, q * 128:(q + 1) * 128], C.ident.v)
            S.copy("act", ot[k].v, P[0].v.rearrange("p (q t) -> p q t", q=4))
            S.dma("pool", out[:, :, t0:t0 + 128], ot[k].v)
    S.barrier()


def tri_consts():
    tf = np.triu(np.ones((128, 128), np.float32))
    return tf, np.ascontiguousarray(tf.T)


GRID_W = 64
NROWS = 32


def na_tables(rpb):
    kc = np.arange(64)[:, None]
    c = np.arange(64)[None, :]
    ci = np.clip(kc - c, -15, 15) + 15
    col_start = np.clip(np.arange(64) - 8, 0, 48)
    in_win = (kc >= col_start[None, :]) & (kc < col_start[None, :] + 16)
    dr0 = np.arange(14)
    wl = np.arange(2)
    ri = dr0[None, :] + wl[:, None]
    b = rpb[:, :, ri[:, None, :, None], ci[None, :, None, :]]
    b = np.ascontiguousarray(b.reshape(rpb.shape[0], 8, 128, 14, 64), dtype=np.float32)
    m = np.broadcast_to(in_win[None, :, None, :], (2, 64, 14, 64)).reshape(128, 14 * 64)
    return b, np.ascontiguousarray(m, dtype=np.float32)


def stage_na(C, l, ctx_out):
    S = C.S
    d = C.d
    qk = d["na_qkT"]
    vd = d["na_v"]
    out = d["brT"][3]
    P = C.ps
    with ExitStack() as st:
        mask = S.sbuf("na_mask", [128, 14 * 64], stack=st)
        S.dma("sp", mask.v, d["na_mask"])
        qT = S.sbuf("na_q", [128, T], stack=st)
        kT = S.sbuf("na_k", [128, T], stack=st)
        qB = S.sbuf("na_qb", [128, T], BF16, stack=st)
        kB = S.sbuf("na_kb", [128, T], BF16, stack=st)
        ve = S.sbuf("na_ve", [128, 18, 128], BF16, stack=st)
        vo = S.sbuf("na_vo", [128, 15, 128], BF16, stack=st)
        kZ = [S.sbuf(f"na_kz{m}", [128, T], BF16, stack=st) for m in range(2)]
        S.memset("dve", kZ[0].v, 0.0)
        S.memset("dve", kZ[1].v, 0.0)
        ones64 = S.sbuf("na_1b", [128, 64], BF16, stack=st)
        S.memset("pool", ones64.v, 1.0)
        eb = [S.sbuf(f"na_eb{i}", [128, 14 * 64], stack=st) for i in range(2)]
        et = [S.sbuf(f"na_e{i}", [128, 384], BF16, stack=st) for i in range(4)]
        ef = [S.sbuf(f"na_ef{i}", [128, 256], stack=st) for i in range(4)]
        ec = [S.sbuf(f"na_ec{i}", [128, 256], BF16, stack=st) for i in range(2)]
        rr = [S.sbuf(f"na_r{i}", [64, 256], stack=st) for i in range(2)]
        ob = [S.sbuf(f"na_ob{i}", [64, T], BF16, stack=st) for i in range(2)]
        ei = 0
        ri_ = 0
        for hp in range(4):
            S.dma("sp", qT.v, qk[hp * 128:(hp + 1) * 128, :])
            S.dma("act", kT.v, qk[512 + hp * 128:512 + (hp + 1) * 128, :])
            S.dma("pool", ve.v, vd[:, hp * 128:(hp + 1) * 128].rearrange("(b p) e -> p b e", p=128))
            S.dma("pool", vo.v, vd[64:64 + 15 * 128, hp * 128:(hp + 1) * 128].rearrange("(b p) e -> p b e", p=128))
            S.copy("dve", qB.v, qT.v)
            S.copy("act", kB.v, kT.v)
            S.copy("act", kZ[0][0:64, :], kT[0:64, :])
            S.copy("dve", kZ[1][64:128, :], kT[64:128, :])
            for hh in range(2):
                h = hp * 2 + hh
                ebt = eb[h % 2]
                o = ob[h % 2]
                S.dma("sp", ebt.v, d["na_bias"][l][h].rearrange("p a c -> p (a c)"))
                S.act(ebt.v, ebt.v, AF.Exp)
                S.tt("dve", ebt.v, ebt.v, mask.v, ALU.mult)
                ebv = ebt.v.rearrange("p (a c) -> p a c", c=64)
                pl, ph = hh * 64, (hh + 1) * 64
                def emitA(r):
                    rs = min(max(r - 4, 0), NROWS - 8)
                    q = qB[:, r * 64:(r + 1) * 64]
                    ps = P[4 + r % 4]
                    e = et[r % 4]
                    f = ef[r % 4]
                    for i, w0 in enumerate((0, 2, 4, 6)):
                        kr = rs + w0
                        S.matmul(ps[:, i * 64:(i + 1) * 64], kZ[hh][:, kr * 64:kr * 64 + 128], q)
                    for cb in range(2):
                        S.matmul(ps[:, 256 + cb * 64:256 + (cb + 1) * 64], kZ[hh][:, NLAT + cb * 128:NLAT + (cb + 1) * 128], q)
                    dr0 = rs - r + 7
                    S.act(f.v, ps[:, 0:256], AF.Exp, scale=0.125)
                    S.act(e[:, 256:384], ps[:, 256:384], AF.Exp, scale=0.125)
                    S.tt("dve", e[:, 0:256].rearrange("p (a c) -> p a c", c=64), f.v.rearrange("p (a c) -> p a c", c=64),
                         ebv[:, dr0:dr0 + 7:2, :], ALU.mult)

                def emitB(r):
                    nonlocal ri_
                    rs = min(max(r - 4, 0), NROWS - 8)
                    acc = P[r % 2]
                    accs = P[2 + r % 2]
                    e = et[r % 4]
                    vts = []
                    for w0 in (0, 2, 4, 6):
                        kr = rs + w0
                        vts.append(ve[:, kr // 2, pl:ph] if kr % 2 == 0 else vo[:, (kr - 1) // 2, pl:ph])
                    for cb in range(2):
                        vts.append(ve[:, 16 + cb, pl:ph])
                    for i in range(6):
                        S.matmul(acc[0:64, 0:64], vts[i], e[:, i * 64:(i + 1) * 64], start=(i == 0), stop=(i == 5))
                    for i in range(6):
                        S.matmul(accs[0:64, 0:64], ones64.v, e[:, i * 64:(i + 1) * 64], start=(i == 0), stop=(i == 5))
                    rt = rr[ri_ % 2]
                    ri_ += 1
                    S.recip(rt[:, 0:64], accs[0:64, 0:64])
                    S.tt("dve", o[:, r * 64:(r + 1) * 64], acc[0:64, 0:64], rt[:, 0:64], ALU.mult)

                LA = 3
                for r in range(NROWS + LA):
                    if r < NROWS:
                        emitA(r)
                    if r - LA >= 0:
                        emitB(r - LA)
                if ctx_out:
                    acc = P[0]
                    accs = P[2]
                    q = qB[pl:ph, NLAT:T]
                    for cb in range(2):
                        ps = P[4 + ei % 4]
                        ei += 1
                        e = ec[cb]
                        S.matmul(ps[:, 0:256], kB[pl:ph, NLAT + cb * 128:NLAT + (cb + 1) * 128], q)
                        S.act(e.v, ps[:, 0:256], AF.Exp, scale=0.125)
                    for cb in range(2):
                        S.matmul(acc[0:64, 0:256], ve[:, 16 + cb, pl:ph], ec[cb].v, start=(cb == 0), stop=(cb == 1))
                    for cb in range(2):
                        S.matmul(accs[0:64, 0:256], ones64.v, ec[cb].v, start=(cb == 0), stop=(cb == 1))
                    rt = rr[ri_ % 2]
                    ri_ += 1
                    S.recip(rt.v, accs[0:64, 0:256])
                    S.tt("dve", o[:, NLAT:T], acc[0:64, 0:256], rt.v, ALU.mult)
                    S.dma("sp", out[h * 64:(h + 1) * 64, :], o.v)
                else:
                    S.dma("sp", out[h * 64:(h + 1) * 64, 0:NLAT], o[:, 0:NLAT])
    S.barrier()

import os
RW_STOP = os.environ.get('RW_STOP', '')
RW_NT = int(os.environ.get('RW_NT', '99'))
RW_LP = BF16 if os.environ.get('RW_LP', 'f32') == 'bf16' else F32

RW_GN_EPS = 64e-5
EXPM05 = 0.6065306597126334


def rw_consts():
    s = np.arange(128)[:, None]
    t = np.arange(128)[None, :]
    f = np.float32
    US = (s < t).astype(f)
    UF = (s <= t).astype(f)
    LS = (s > t).astype(f)
    LF = (s >= t).astype(f)
    I_ = np.eye(128, dtype=f)
    masks = np.stack([np.tile(m_, (1, 4)) for m_ in (US, UF, LS, LF, -US, -LS, I_)], 1)
    lo = (s <= 63).astype(f)
    hi = (s >= 64).astype(f)
    dq = np.stack([UF - lo, US - lo, LF - hi, LS - hi], 1)
    mvec = np.stack([np.concatenate([lo, hi], 1), np.concatenate([hi, lo], 1)], 1)
    return np.ascontiguousarray(masks), np.ascontiguousarray(dq), np.ascontiguousarray(mvec.astype(f))


class _OV:
    def __init__(self, views):
        self.views = views

    def __getitem__(self, idx):
        assert idx[0] == slice(None) and isinstance(idx[1], int)
        v = self.views[idx[1]]
        return v if idx[2] == slice(None) else v[:, idx[2]]


def rw_phase1(C, l):
    S = C.S
    d = C.d
    src = d["p_rw"]
    prep = d["rw_prep"]
    P = C.ps
    with ExitStack() as st:
        mu = S.sbuf("rw_mu", [128, 3, 1920], stack=st)
        load_bcast(S, "sp", mu[:, 0:2, :], d["rw_mu"][l])
        S.tt("dve", mu[:, 2, :], mu[:, 0, :], mu[:, 1, :], ALU.add)
        S.ts("dve", mu[:, 2, :], mu[:, 2, :], -1.0, ALU.mult, 1.0, ALU.add)
        w0b = S.sbuf("rw_w0b", [128, 2, 512], stack=st)
        a0b = S.sbuf("rw_a0b", [128, 2, 512], stack=st)
        load_bcast(S, "pool", w0b.v, d["rw_w0"][l])
        load_bcast(S, "pool", a0b.v, d["rw_a0"][l])
        kkb = S.sbuf("rw_kkb", [128, 512], stack=st)
        kab = S.sbuf("rw_kab", [128, 512], stack=st)
        rkb = S.sbuf("rw_rkb", [128, 512], stack=st)
        load_bcast(S, "sp", kkb.v, d["rw_k_k"][l])
        load_bcast(S, "sp", kab.v, d["rw_k_a"][l])
        load_bcast(S, "sp", rkb.v, d["rw_r_k"][l])
        wup = S.sbuf("rw_wup", [128, 512], stack=st)
        aup = S.sbuf("rw_aup", [128, 512], stack=st)
        gup = S.sbuf("rw_gup", [128, 512], stack=st)
        S.dma("sp", wup.v, d["rw_w_up"][l])
        S.dma("sp", aup.v, d["rw_a_up"][l])
        S.dma("sp", gup.v, d["rw_g_up"][l])
        sh = [[S.sbuf(f"rw_sh{i}{j}", [128, 1920], stack=st) for j in range(3)] for i in range(2)]
        sb = [S.sbuf(f"rw_s{i}", [128, 1920], stack=st) for i in range(2)]
        t1 = S.sbuf("rw_t1", [128, 1920], stack=st)
        th = S.sbuf("rw_th", [128, 2, 128], stack=st)
        thT = S.sbuf("rw_thT", [128, 3, 128], stack=st)
        ot = [S.sbuf(f"rw_ot{i}", [128, 11, 512], stack=st) for i in range(2)]
        otv = [S.subviews(t_, 11) for t_ in ot]
        wk = [S.sbuf(f"rw_wk{i}", [128, 512], stack=st) for i in range(8)]
        sm = [S.sbuf(f"rw_sm{i}", [128, 8], stack=st) for i in range(4)]

        def h8(v):
            return v.rearrange("p (h e) -> p h e", h=8)

        for ti in range(NT):
            t0 = ti * 128
            tl = sh[ti % 2]
            s = sb[ti % 2]
            oT = ot[ti % 2]
            o = _OV(otv[ti % 2])
            load_shifted(S, tl, src, t0, 1920, 0, ti in SEG_FIRST, ti in SEG_LAST)
            S.tt("dve", s.v, tl[1].v, mu[:, 2, :], ALU.mult)
            S.tt("dve", t1.v, tl[0].v, mu[:, 0, :], ALU.mult)
            S.tt("dve", s.v, s.v, t1.v, ALU.add)
            S.tt("dve", t1.v, tl[2].v, mu[:, 1, :], ALU.mult)
            S.tt("dve", s.v, s.v, t1.v, ALU.add)
            r, k, v = s[:, 0:512], s[:, 512:1024], s[:, 1024:1536]
            S.copy("act", o[:, 0, :], r)
            S.copy("act", o[:, 1, :], v)
            S.act(th[:, 0, :], s[:, 1536:1664], AF.Tanh)
            S.act(th[:, 1, :], s[:, 1792:1920], AF.Sigmoid)
            S.transpose(P[0][:, 0:128], th[:, 0, :], C.ident.v)
            S.transpose(P[0][:, 128:256], s[:, 1664:1792], C.ident.v)
            S.transpose(P[0][:, 256:384], th[:, 1, :], C.ident.v)
            S.copy("dve", thT.v, P[0][:, 0:384].rearrange("p (q t) -> p q t", q=3))
            a_t = [wk[0], wk[1]]
            for dr in range(2):
                pl, ph = dr * 64, (dr + 1) * 64
                S.matmul(P[1 + dr].v, thT[pl:ph, 0, :], wup[pl:ph, :])
                lw = o[:, 5 + 3 * dr, :]
                S.tt("dve", lw, P[1 + dr].v, w0b[:, dr, :], ALU.add)
                S.act(lw, lw, AF.Sigmoid)
                S.ts("dve", lw, lw, -EXPM05, ALU.mult)
                S.matmul(P[3 + dr].v, thT[pl:ph, 1, :], aup[pl:ph, :])
                S.tt("dve", a_t[dr].v, P[3 + dr].v, a0b[:, dr, :], ALU.add)
                S.act(a_t[dr].v, a_t[dr].v, AF.Sigmoid)
            S.matmul(P[5].v, thT[:, 2, :], gup.v)
            S.copy("act", o[:, 3, :], P[5].v)
            kk = o[:, 2, :]
            S.tt("dve", kk, k, kkb.v, ALU.mult)
            S.tt("dve", wk[2].v, kk, kk, ALU.mult)
            S.reduce("dve", sm[0].v, h8(wk[2].v), ALU.add)
            S.act(sm[0].v, sm[0].v, AF.Sqrt)
            S.ts("dve", sm[0].v, sm[0].v, 1e-12, ALU.max)
            S.recip(sm[0].v, sm[0].v)
            S.tt("dve", h8(kk), h8(kk), bc_last(sm[0].v, 64), ALU.mult)
            S.tt("dve", wk[3].v, r, rkb.v, ALU.mult)
            for dr in range(2):
                kd = o[:, 6 + 3 * dr, :]
                bb = o[:, 7 + 3 * dr, :]
                S.stt("dve", wk[4 + dr].v, a_t[dr].v, -1.0, kab.v, ALU.add, ALU.mult)
                S.stt("dve", kd, wk[4 + dr].v, 1.0, k, ALU.add, ALU.mult)
                S.tt("dve", bb, kk, a_t[dr].v, ALU.mult)
                S.tt("dve", wk[6 + dr].v, wk[3].v, kd, ALU.mult)
                S.reduce("dve", sm[1 + dr].v, h8(wk[6 + dr].v), ALU.add)
            S.tt("dve", sm[3].v, sm[1].v, sm[2].v, ALU.add)
            S.tt("dve", h8(o[:, 4, :]), h8(v), bc_last(sm[3].v, 64), ALU.mult)
            S.dma("act", prep[t0:t0 + 128, :, :], V(oT.h[:], oT.buf), extra_reads=[v_.buf for v_ in otv[ti % 2]])
    S.barrier()


def run_interleaved(gens):
    gens = list(gens)
    while gens:
        for g in list(gens):
            try:
                next(g)
            except StopIteration:
                gens.remove(g)


def rw_phase2(C, l):
    S = C.S
    d = C.d
    prep = d["rw_prep"]
    P = C.ps
    with ExitStack() as st:
        msk = S.sbuf("rw_msk", [128, 7, 512], stack=st)
        dqm = S.sbuf("rw_dq", [128, 4, 128], stack=st)
        mv = S.sbuf("rw_mv", [128, 2, 2], stack=st)
        S.dma("sp", msk.v, d["rw_masks"])
        S.dma("sp", dqm.v, d["rw_dqc"])
        S.dma("sp", mv.v, d["rw_mvec"])
        St = S.sbuf("rw_St", [128, 2, 4, 64], stack=st)
        S.memset("dve", St.v, 0.0)
        R_ = []
        for dr in range(2):
            def mk(nm, w=512, n=2, dt_=F32):
                return [S.sbuf(f"rw_{nm}{dr}{h}", [128, w], dt_, stack=st) for h in range(n)]
            LP = RW_LP
            res = dict(
                S0s=S.sbuf(f"rw_S0s{dr}", [128, 4, 64], stack=st),
                inp=[S.sbuf(f"rw_in{dr}{i}", [128, 6, 512], stack=st) for i in range(2)],
                ex=mk("ex", 512, 3), tm=mk("tm", 512, 4),
                tT=[S.sbuf(f"rw_tT{dr}{j}", [128, 4, 128], LP, stack=st) for j in range(4)],
                eh=S.sbuf(f"rw_eh{dr}", [128, 4, 2], stack=st),
                Q=[mk("Qa", dt_=LP), mk("Qb", dt_=LP)], R=[mk("Ra", dt_=LP), mk("Rb", dt_=LP)],
                Y=[mk("Ya", dt_=LP), mk("Yb", dt_=LP)],
                BmT=mk("BmT", dt_=LP), AbT=mk("AbT", dt_=LP), AkT=mk("AkT", dt_=LP), Wsb=mk("Wsb", 256, dt_=LP),
                Usb=S.sbuf(f"rw_U{dr}", [128, 4, 128], LP, stack=st),
                ob=S.sbuf(f"rw_ob{dr}", [128, 512], stack=st),
                lpc=(S.sbuf(f"rw_lpc{dr}", [128, 3, 512], LP, stack=st) if LP is not F32 else None),
                S0b=(S.sbuf(f"rw_S0b{dr}", [128, 4, 64], LP, stack=st) if LP is not F32 else None),
            )
            R_.append(res)
        bc_ = [0]

        def bank():
            b_ = P[4 + bc_[0] % 4]
            bc_[0] += 1
            return b_

        ev = [0]

        def evac_copy(dst, src):
            ev[0] += 1
            S.copy("act" if ev[0] % 2 else "dve", dst, src)

        def q4(v):
            return v.rearrange("p (q t) -> p q t", q=4)

        def hinfo(h):
            hp, hh = h // 2, h % 2
            return hp, hh, hh * 64, (hh + 1) * 64

        def dir_gen(dr):
            rs_ = R_[dr]
            PA, PB = P[2 * dr], P[2 * dr + 1]
            S0s, Xb, ex, eh, Usb = rs_["S0s"], rs_["inp"], rs_["ex"], rs_["eh"], rs_["Usb"]
            Q, R, Y, BmT, AbT, AkT, Wsb = (rs_[k_] for k_ in ("Q", "R", "Y", "BmT", "AbT", "AkT", "Wsb"))
            order = [16, 17] + list(range(16)) if dr == 0 else [17, 16] + list(range(15, -1, -1))
            if dr == 0:
                mS, mF, mSn, mAn = msk[:, 0, :], msk[:, 1, :], msk[:, 4, :], msk[:, 5, :]
            else:
                mS, mF, mSn, mAn = msk[:, 2, :], msk[:, 3, :], msk[:, 5, :], msk[:, 4, :]
            order = order[:RW_NT]

            def load_x(idx):
                tj = order[idx]
                Xn = Xb[idx % 2]
                S.dma("sp", Xn[:, 0:3, :], prep[tj * 128:tj * 128 + 128, 0:3, :])
                S.dma("act", Xn[:, 3:6, :], prep[tj * 128:tj * 128 + 128, 5 + 3 * dr:8 + 3 * dr, :])

            load_x(0)
            for idx, ti in enumerate(order):
                t0 = ti * 128
                X = Xb[idx % 2]
                if idx + 1 < len(order):
                    load_x(idx + 1)
                r_, v_, kk_, lw_, kd_, b_ = (X[:, j, :] for j in range(6))
                S.matmul(PA.v, dqm[:, 2 * dr, :], lw_)
                S.matmul(PB.v, dqm[:, 2 * dr + 1, :], lw_)
                S.act(ex[0].v, PA.v, AF.Exp)
                S.act(ex[1].v, PA.v, AF.Exp, scale=-1.0)
                S.act(ex[2].v, PB.v, AF.Exp)
                rq, kq, bn, kn = rs_["tm"]
                S.tt("dve", rq.v, r_, ex[0].v, ALU.mult)
                S.tt("dve", kq.v, kk_, ex[2].v, ALU.mult)
                S.tt("dve", bn.v, b_, ex[1].v, ALU.mult)
                S.tt("dve", kn.v, kd_, ex[1].v, ALU.mult)
                lpc = rs_["lpc"]
                S0b = rs_["S0b"]
                if RW_LP is F32:
                    vB, bnB, knB = v_, bn.v, kn.v
                else:
                    S.copy("act", lpc[:, 0, :], v_)
                    S.copy("act", lpc[:, 1, :], bn.v)
                    S.copy("act", lpc[:, 2, :], kn.v)
                    vB, bnB, knB = lpc[:, 0, :], lpc[:, 1, :], lpc[:, 2, :]
                yield
                rqT, kqT, bnT, knT = rs_["tT"]
                for j, (src_, dst_) in enumerate(((rq, rqT), (kq, kqT), (bn, bnT), (kn, knT))):
                    pb = (PA, PB)[j % 2]
                    for hp in range(4):
                        S.transpose(pb[:, hp * 128:(hp + 1) * 128], src_[:, hp * 128:(hp + 1) * 128], C.ident.v)
                    evac_copy(dst_.v, q4(pb.v))
                eg = bank()
                for hp in range(4):
                    S.matmul(eg[:, hp * 2:(hp + 1) * 2], lw_[:, hp * 128:(hp + 1) * 128], mv[:, dr, :])
                S.act(eh.v, eg[:, 0:8].rearrange("p (q c) -> p q c", c=2), AF.Exp)
                for hp in range(4):
                    S.ts("dve", S0s[:, hp, :], St[:, dr, hp, :], eh[:, hp, 0:1], ALU.mult)
                if RW_LP is F32:
                    S0m = S0s
                else:
                    S.copy("act", S0b.v, S0s.v)
                    S0m = S0b
                yield
                specs = ((bnT, kqT, Q[0], mSn), (kqT, bnT, R[0], mAn), (knT, kqT, BmT, mS),
                         (bnT, rqT, AbT, mF), (knT, rqT, AkT, mF))
                for (LT, RT, dsts, mk_) in specs:
                    gs_ = (bank(), bank())
                    for i in range(4):
                        for half in range(2):
                            hp, hh, pl, ph = hinfo(2 * i + half)
                            S.matmul(gs_[half][:, i * 128:(i + 1) * 128], LT[pl:ph, hp, :], RT[pl:ph, hp, :])
                    for half in range(2):
                        S.tt("dve", dsts[half].v, gs_[half].v, mk_, ALU.mult)
                for half in range(2):
                    S.tt("dve", Y[0][half].v, Q[0][half].v, msk[:, 6, :], ALU.add)
                yield
                for half in range(2):
                    g = bank()
                    for i, h in enumerate([2 * i_ + half for i_ in range(4)]):
                        hp, hh, pl, ph = hinfo(h)
                        S.matmul(g[:, i * 64:(i + 1) * 64], kqT[pl:ph, hp, :], S0m[pl:ph, hp, :], start=True, stop=False)
                        S.matmul(g[:, i * 64:(i + 1) * 64], BmT[half][:, i * 128:(i + 1) * 128], vB[:, h * 64:(h + 1) * 64],
                                 start=False, stop=True)
                    evac_copy(Wsb[half].v, g[:, 0:256])
                yield
                cur = 0
                for lev in range(1, 7):
                    nxt = 1 - cur
                    for half in range(2):
                        if lev < 6:
                            g = bank()
                            for i in range(4):
                                sl = slice(i * 128, (i + 1) * 128)
                                S.matmul(g[:, sl], R[cur][half][:, sl], Q[cur][half][:, sl])
                            evac_copy(Q[nxt][half].v, g.v)
                        g = bank()
                        for i in range(4):
                            sl = slice(i * 128, (i + 1) * 128)
                            S.matmul(g[:, sl], Q[cur][half][:, sl], R[cur][half][:, sl])
                        evac_copy(R[nxt][half].v, g.v)
                    yield
                    for half in range(2):
                        g = bank()
                        for i in range(4):
                            sl = slice(i * 128, (i + 1) * 128)
                            S.matmul(g[:, sl], R[nxt][half][:, sl], Y[cur][half][:, sl])
                        S.tt("dve", Y[nxt][half].v, Y[cur][half].v, g.v, ALU.add)
                    cur = nxt
                    yield
                for half in range(2):
                    g = bank()
                    for i in range(4):
                        S.matmul(g[:, i * 64:(i + 1) * 64], Y[cur][half][:, i * 128:(i + 1) * 128], Wsb[half][:, i * 64:(i + 1) * 64])
                    S.ts("dve", Usb[:, :, half * 64:(half + 1) * 64], g[:, 0:256].rearrange("p (a b) -> p a b", a=4), -1.0, ALU.mult)
                yield
                for h in range(8):
                    hp, hh, pl, ph = hinfo(h)
                    half, i = h % 2, h // 2
                    oc = PB[:, h * 64:(h + 1) * 64]
                    S.matmul(oc, rqT[pl:ph, hp, :], S0m[pl:ph, hp, :], start=True, stop=False)
                    S.matmul(oc, AbT[half][:, i * 128:(i + 1) * 128], Usb[:, hp, hh * 64:(hh + 1) * 64], start=False, stop=False)
                    S.matmul(oc, AkT[half][:, i * 128:(i + 1) * 128], vB[:, h * 64:(h + 1) * 64], start=False, stop=True)
                S.copy("act", rs_["ob"].v, PB.v)
                S.dma("pool", d["rw_o"][dr, t0:t0 + 128, :], rs_["ob"].v)
                yield
                g = bank()
                for hp in range(4):
                    sl = slice(hp * 128, (hp + 1) * 128)
                    S.matmul(g[:, sl], bnB[:, sl], Usb[:, hp, :], start=True, stop=False)
                    S.matmul(g[:, sl], knB[:, sl], vB[:, sl], start=False, stop=True)
                for hp in range(4):
                    for hh in range(2):
                        pl, ph = hh * 64, (hh + 1) * 64
                        S.tt("dve", St[pl:ph, dr, hp, :], S0s[pl:ph, hp, :], g[pl:ph, hp * 128 + hh * 64:hp * 128 + (hh + 1) * 64], ALU.add)
                for hp in range(4):
                    S.ts("dve", St[:, dr, hp, :], St[:, dr, hp, :], eh[:, hp, 1:2], ALU.mult)
                yield

        run_interleaved([dir_gen(0), dir_gen(1)])
    S.barrier()
    with ExitStack() as st:
        rw_phase3(C, l, st, None)
    S.barrier()


def rw_phase3(C, l, st, oacc):
    S = C.S
    d = C.d
    prep = d["rw_prep"]
    out = d["brT"][0].rearrange("(k p) t -> p k t", p=128)
    P = C.ps
    lg = S.sbuf("rw_lg", [128, 2, 512], stack=st)
    load_bcast(S, "sp", lg[:, 0, :], d["rw_ln_g"][l])
    load_bcast(S, "sp", lg[:, 1, :], d["rw_ln_b"][l])
    gb = [S.sbuf(f"rw_gb{i}", [128, 2, 512], stack=st) for i in range(2)]
    cen = [S.sbuf(f"rw_cen{i}", [128, 512], stack=st) for i in range(2)]
    sq = S.sbuf("rw_sq3", [128, 512], stack=st)
    mn = [S.sbuf(f"rw_mn{i}", [128, 8], stack=st) for i in range(2)]
    vr = [S.sbuf(f"rw_vr{i}", [128, 8], stack=st) for i in range(2)]
    ot = [S.sbuf(f"rw_o3{i}", [128, 4, 128], BF16, stack=st) for i in range(2)]

    def h8(v):
        return v.rearrange("p (h e) -> p h e", h=8)

    tiles = C.rw_tiles
    of = [S.sbuf(f"rw_of{i}", [128, 2, 512], stack=st) for i in range(2)]
    for n, ti in enumerate(tiles):
        k = n % 2
        t0 = ti * 128
        S.dma("sp", gb[k].v, prep[t0:t0 + 128, 3:5, :])
        S.dma("act", of[k][:, 0, :], d["rw_o"][0, t0:t0 + 128, :])
        S.dma("act", of[k][:, 1, :], d["rw_o"][1, t0:t0 + 128, :])
        S.tt("dve", of[k][:, 0, :], of[k][:, 0, :], of[k][:, 1, :], ALU.add)
        o = of[k][:, 0, :]
        S.reduce("dve", mn[k].v, h8(o), ALU.add)
        S.ts("dve", mn[k].v, mn[k].v, 1.0 / 64, ALU.mult)
        c = cen[k]
        S.tt("dve", h8(c.v), h8(o), bc_last(mn[k].v, 64), ALU.subtract)
        S.act(sq.v, c.v, AF.Square)
        S.reduce("dve", vr[k].v, h8(sq.v), ALU.add)
        S.ts("dve", vr[k].v, vr[k].v, 1.0 / 64, ALU.mult, RW_GN_EPS, ALU.add)
        S.act(vr[k].v, vr[k].v, AF.Ln)
        S.act(vr[k].v, vr[k].v, AF.Exp, scale=-0.5)
        S.tt("dve", h8(c.v), h8(c.v), bc_last(vr[k].v, 64), ALU.mult)
        S.tt("dve", c.v, c.v, lg[:, 0, :], ALU.mult)
        S.tt("dve", c.v, c.v, lg[:, 1, :], ALU.add)
        S.tt("dve", c.v, c.v, gb[k][:, 1, :], ALU.add)
        S.tt("dve", c.v, c.v, gb[k][:, 0, :], ALU.mult)
        for q in range(4):
            S.transpose(P[0][:, q * 128:(q + 1) * 128], c[:, q * 128:(q + 1) * 128], C.ident.v)
        S.copy("act", ot[k].v, P[0].v.rearrange("p (q t) -> p q t", q=4))
        S.dma("sp", out[:, :, t0:t0 + 128], ot[k].v)


def stage_rwkv(C, l, ctx_out):
    C.rw_tiles = list(range(NT)) if ctx_out else list(range(16))
    rw_phase1(C, l)
    rw_phase2(C, l)


DEPTH = 2
IN_TOTAL = 6544

INPUT_SHAPES = {
    "xT": [D_MODEL, T],
    "cc": [128, 16, 2],
    "ident": [128, 128],
    "ada_w": [DEPTH, D_MODEL, 6 * D_MODEL],
    "ada_b": [DEPTH, 128, 96],
    "norm1_g": [DEPTH, 128, 16],
    "norm2_g": [DEPTH, 128, 16],
    "w_in": [DEPTH, D_MODEL, IN_TOTAL],
    "w_gate": [DEPTH, 4, D_MODEL, D_MODEL],
    "gate_b": [DEPTH, 128, 4, 16],
    "w_br": [DEPTH, 4, 512, D_MODEL],
    "w_out": [DEPTH, D_MODEL, D_MODEL],
    "ffn_up": [DEPTH, D_MODEL, 2 * D_FF],
    "ffn_conv_w": [DEPTH, 128, 3, 88],
    "ffn_conv_b": [DEPTH, 128, 88],
    "ffn_down": [DEPTH, 16, 128, 44, 128],
    "final_norm_g": [128, 16],
    "rope_cos": [128, NLAT],
    "rope_sin": [128, NLAT],
    "rope_perm": [128, 128],
    "triF": [128, 128],
    "triB": [128, 128],
    "ssd_conv_w": [DEPTH, 3, 1024],
    "ssd_conv_b": [DEPTH, 1024],
    "ssd_dt_bias": [DEPTH, 16],
    "ssd_a_log": [DEPTH, 16],
    "ssd_d": [DEPTH, 8],
    "ssd_norm_g": [DEPTH, 512],
    "rw_mu": [DEPTH, 2, 1920],
    "rw_w0": [DEPTH, 2, 512],
    "rw_a0": [DEPTH, 2, 512],
    "rw_k_k": [DEPTH, 512],
    "rw_k_a": [DEPTH, 512],
    "rw_r_k": [DEPTH, 512],
    "rw_w_up": [DEPTH, 128, 512],
    "rw_a_up": [DEPTH, 128, 512],
    "rw_g_up": [DEPTH, 128, 512],
    "rw_ln_g": [DEPTH, 512],
    "rw_ln_b": [DEPTH, 512],
    "rw_masks": [128, 7, 512],
    "rw_dqc": [128, 4, 128],
    "rw_mvec": [128, 2, 2],
    "na_bias": [DEPTH, 8, 128, 14, 64],
    "na_mask": [128, 14 * 64],
    "da_lambda": [DEPTH, 4, 64],
    "da_subln_g": [DEPTH, 128, 1],
}

SCRATCH_SHAPES = {
    "hT": [D_MODEL, T],
    "p_rw": [T, 1920],
    "p_ssd": [T, 1552],
    "da_qkT": [1024, T],
    "da_v": [T, 512],
    "na_qkT": [1024, T],
    "na_v": [T, 512],
    "modout": [128, 192],
    "rw_prep": [T, 11, 512],
    "rw_o": [2, T, 512],
    "brT": [4, 512, T],
    "mgT": [D_MODEL, T],
    "guT": [D_FF, T],
    "xres": [D_MODEL, T],
    "outT": [D_MODEL, NLAT],
}


ANNOTATE = False
BF16_SCRATCH = {"hT", "brT", "mgT", "guT"}


def host_inputs(inp, b):
    f = np.float32
    o = {}
    o["xT"] = np.ascontiguousarray(np.concatenate([inp["x"][b].T, inp["ctx"][b].T], axis=1), dtype=f)
    cc = np.stack([inp["c"][b], inp["c_ctx"]], axis=-1)
    o["cc"] = np.ascontiguousarray(cc.reshape(16, 128, 2).transpose(1, 0, 2), dtype=f)
    o["ident"] = np.eye(128, dtype=f)
    o["ada_w"] = inp["ada_w"]
    o["ada_b"] = np.ascontiguousarray(inp["ada_b"].reshape(DEPTH, 96, 128).transpose(0, 2, 1), dtype=f)
    o["norm1_g"] = np.ascontiguousarray(inp["norm1_g"].reshape(DEPTH, 16, 128).transpose(0, 2, 1), dtype=f)
    o["norm2_g"] = np.ascontiguousarray(inp["norm2_g"].reshape(DEPTH, 16, 128).transpose(0, 2, 1), dtype=f)
    o["w_in"] = inp["w_in"]
    o["w_gate"] = inp["w_gate"]
    o["gate_b"] = np.ascontiguousarray(inp["gate_b"].reshape(DEPTH, 4, 16, 128).transpose(0, 3, 1, 2), dtype=f)
    o["w_br"] = inp["w_br"]
    o["w_out"] = inp["w_out"]
    fu = inp["ffn_up"].reshape(DEPTH, D_MODEL, 2, 44, 128).transpose(0, 1, 3, 2, 4)
    o["ffn_up"] = np.ascontiguousarray(fu.reshape(DEPTH, D_MODEL, 2 * D_FF), dtype=f)
    o["ffn_conv_w"] = np.ascontiguousarray(inp["ffn_conv_w"].reshape(DEPTH, 3, 88, 128).transpose(0, 3, 1, 2), dtype=f)
    o["ffn_conv_b"] = np.ascontiguousarray(inp["ffn_conv_b"].reshape(DEPTH, 88, 128).transpose(0, 2, 1), dtype=f)
    o["ffn_down"] = np.ascontiguousarray(inp["ffn_down"].reshape(DEPTH, 44, 128, 16, 128).transpose(0, 3, 2, 1, 4), dtype=f)
    o["rope_cos"], o["rope_sin"], o["rope_perm"] = rope_tables()
    o["triF"], o["triB"] = tri_consts()
    o["ssd_conv_w"] = inp["ssd_conv_w"]
    o["ssd_conv_b"] = inp["ssd_conv_b"]
    o["ssd_dt_bias"] = np.ascontiguousarray(inp["ssd_dt_bias"].reshape(DEPTH, 16), dtype=f)
    o["ssd_a_log"] = np.ascontiguousarray(inp["ssd_a_log"].reshape(DEPTH, 16), dtype=f)
    o["ssd_d"] = inp["ssd_d"]
    o["ssd_norm_g"] = inp["ssd_norm_g"]
    for k_ in ["rw_mu", "rw_w0", "rw_a0", "rw_k_k", "rw_k_a", "rw_ln_g", "rw_ln_b"]:
        o[k_] = inp[k_]
    o["rw_r_k"] = np.ascontiguousarray(inp["rw_r_k"].reshape(DEPTH, 512), dtype=f)
    o["rw_w_up"] = np.ascontiguousarray(inp["rw_w_up"].reshape(DEPTH, 128, 512), dtype=f)
    o["rw_a_up"] = np.ascontiguousarray(inp["rw_a_up"].reshape(DEPTH, 128, 512), dtype=f)
    o["rw_g_up"] = inp["rw_g_up"]
    o["rw_masks"], o["rw_dqc"], o["rw_mvec"] = rw_consts()
    o["na_bias"], o["na_mask"] = na_tables(inp["na_rpb"])
    o["da_lambda"] = inp["da_lambda"]
    o["da_subln_g"] = np.ascontiguousarray(inp["da_subln_g"].reshape(DEPTH, 128, 1), dtype=f)
    o["final_norm_g"] = np.ascontiguousarray(inp["final_norm_g"].reshape(16, 128).T, dtype=f)
    return o


def make_program(stage_fn, ext_in, ext_out):
    nc = bass.Bass("TRN2", target_bir_lowering=False)
    C = Ctx()
    C.nc = nc
    C.d = {}
    allshapes = dict(INPUT_SHAPES)
    allshapes.update(SCRATCH_SHAPES)
    for name, shp in allshapes.items():
        if name in ext_in:
            kind = "ExternalInput"
        elif name in ext_out:
            kind = "ExternalOutput"
        elif name in SCRATCH_SHAPES:
            kind = "Internal"
        else:
            continue
        dtp = BF16 if name in BF16_SCRATCH else F32
        C.d[name] = nc.dram_tensor(name, list(shp), dtp, kind=kind).ap()
    S = Sched(nc)
    S.annotate = ANNOTATE
    C.S = S
    with S.stack:
        init_common(C)
        stage_fn(C)
        S.barrier()
        S.emit()
    return nc, S


def _tagged(C, name, fn, *a, **k):
    C.S.stage = name
    fn(C, *a, **k)
    C.S.stage = None


def full_stages(C):
    d = C.d
    S = C.S
    for l in range(DEPTH):
        last = (l == DEPTH - 1)
        ctx_out = not last
        _tagged(C, f"L{l}_mod", stage_mod, l)
        xsrc = d["xT"] if l == 0 else d["xres"]
        _tagged(C, f"L{l}_norm1", stage_norm, 1, xsrc, d["hT"], norm_chunks(True))
        _tagged(C, f"L{l}_win", stage_win, l)
        _tagged(C, f"L{l}_rwkv", stage_rwkv, l, ctx_out)
        _tagged(C, f"L{l}_ssd", stage_ssd, l, ctx_out)
        _tagged(C, f"L{l}_da", stage_da, l, ctx_out)
        _tagged(C, f"L{l}_na", stage_na, l, ctx_out)
        _tagged(C, f"L{l}_merge", stage_merge, l, ctx_out)
        _tagged(C, f"L{l}_wout", stage_wout, l, ctx_out)
        _tagged(C, f"L{l}_norm2", stage_norm, 2, d["xres"], d["hT"], norm_chunks(ctx_out))
        _tagged(C, f"L{l}_ffnup", stage_ffn_up, l, ctx_out)
        _tagged(C, f"L{l}_ffndn", stage_ffn_down, l, ctx_out)
    with ExitStack() as st:
        fg = S.sbuf("fin_g", [128, 16], stack=st)
        S.dma("sp", fg.v, d["final_norm_g"])
        stage_norm(C, 0, d["xres"], d["outT"], [(c0, w, 0) for (c0, w) in chunks(NLAT, 256)], final_g=fg)


_PROG = {}


def kernel(**inputs):
    inp = {k: np.asarray(v) for k, v in inputs.items()}
    n = 8
    shared = None
    in_maps = []
    for b in range(n):
        hi = host_inputs(inp, b) if shared is None else None
        if shared is None:
            shared = hi
            in_maps.append(hi)
        else:
            m = dict(shared)
            m["xT"] = np.ascontiguousarray(np.concatenate([inp["x"][b].T, inp["ctx"][b].T], axis=1), dtype=np.float32)
            cc = np.stack([inp["c"][b], inp["c_ctx"]], axis=-1)
            m["cc"] = np.ascontiguousarray(cc.reshape(16, 128, 2).transpose(1, 0, 2), dtype=np.float32)
            in_maps.append(m)
    if "nc" not in _PROG:
        nc, S = make_program(full_stages, set(INPUT_SHAPES.keys()), {"outT"})
        _PROG["nc"] = nc
    nc = _PROG["nc"]
    res = run_bass_kernel_spmd(nc, in_maps, core_ids=list(range(n)))
    out = np.stack([np.ascontiguousarray(res.results[b]["outT"].T) for b in range(n)], axis=0)
    return out.astype(np.float32)
```
